# Optimizing a Trainium2 kernel written in Bass

```python
import math
import jax
import jax.numpy as jnp
from jax import lax
import numpy as np

D_MODEL = 1024
BATCH = 2
SEQ = 16384
DEPTH = 4

GRID_W = 64
CTX_LEN = 256
N_MOD = 9

S5_WIDTH = 256
S5_GROUP = 16
S5_GROUPS = S5_WIDTH // S5_GROUP
S5_STATE = 64
S5_DT_MIN = 1e-3
S5_DT_MAX = 1e-1

N_Q_HEADS = 8
N_KV_HEADS = 2
HEAD_DIM = 64
Q_PER_KV = N_Q_HEADS // N_KV_HEADS
ATT_Q_W = N_Q_HEADS * HEAD_DIM
ATT_KV_W = N_KV_HEADS * HEAD_DIM
Q_BLOCK = 128
ROPE_THETA = 10000.0

RWKV_HEADS = 4
RWKV_HEAD = 64
RWKV_WIDTH = RWKV_HEADS * RWKV_HEAD
DECAY_LORA = 64
ICLR_LORA = 64
GATE_LORA = 128
RWKV_COLS = 3 * RWKV_WIDTH + DECAY_LORA + ICLR_LORA + GATE_LORA
GN_EPS = 64e-5

D_FF = 2816
N_BRANCH = 3
NORM_EPS = 1e-6

CTX_STATE_COLS = S5_WIDTH + 2 * ATT_KV_W + RWKV_COLS
MIX_COLS = CTX_STATE_COLS + ATT_Q_W
IN_COLS = MIX_COLS + N_BRANCH * D_MODEL

F32 = jnp.float32

kernel_name = "hybrid_s5_gqa_rwkv7_dit_block"


def split_cols(p, sizes):
    offs = np.cumsum(sizes)[:-1].tolist()
    return jnp.split(p, offs, axis=-1)


def rms_norm(x, w):
    xf = x.astype(F32)
    y = xf * lax.rsqrt(jnp.mean(xf * xf, axis=-1, keepdims=True) + NORM_EPS)
    return (y * w.astype(F32)).astype(x.dtype)


def ada_params(cond, w_ada, b_ada):
    m = jax.nn.silu(cond) @ w_ada + b_ada
    return [t[:, None, :] for t in jnp.split(m, N_MOD, axis=-1)]


def norm_modulate(x, w, shift, scale):
    return rms_norm(x, w) * (1.0 + scale) + shift


def ffn_half(x, norm_w, shift, scale, gate, w1, w3, w2):
    h = norm_modulate(x, norm_w, shift, scale)
    return x + 0.5 * gate * ((jax.nn.silu(h @ w1) * (h @ w3)) @ w2)


def centred_shift_lerp(p, mu):
    zero = jnp.zeros_like(p[:, :1])
    prev = jnp.concatenate([zero, p[:, :-1]], axis=1)
    nxt = jnp.concatenate([p[:, 1:], zero], axis=1)
    return p + (0.5 * (prev + nxt) - p) * mu


def s5_discretise(lam_re, lam_im, log_dt, b_re, b_im):
    lam_re = lam_re.astype(F32)
    lam_im = lam_im.astype(F32)
    b_re = b_re.astype(F32)
    b_im = b_im.astype(F32)
    dt = jnp.exp(log_dt.astype(F32))[..., None]
    mag = jnp.exp(lam_re * dt)
    ang = lam_im * dt
    lb_re = mag * jnp.cos(ang)
    lb_im = mag * jnp.sin(ang)
    den = lam_re * lam_re + lam_im * lam_im
    num_re = lb_re - 1.0
    f_re = (num_re * lam_re + lb_im * lam_im) / den
    f_im = (lb_im * lam_re - num_re * lam_im) / den
    bb_re = f_re[..., None] * b_re - f_im[..., None] * b_im
    bb_im = f_re[..., None] * b_im + f_im[..., None] * b_re
    return lb_re, lb_im, bb_re, bb_im


def _linear_recurrence_combine(e1, e2):
    a1r, a1i, b1r, b1i = e1
    a2r, a2i, b2r, b2i = e2
    return (a1r * a2r - a1i * a2i,
            a1r * a2i + a1i * a2r,
            a2r * b1r - a2i * b1i + b2r,
            a2r * b1i + a2i * b1r + b2i)


def s5_states(u, lb_re, lb_im, bb_re, bb_im, h0_re, h0_im, reverse):
    bu_re = jnp.einsum('blgh,gph->blgp', u, bb_re)
    bu_im = jnp.einsum('blgh,gph->blgp', u, bb_im)
    if h0_re is not None:
        idx = -1 if reverse else 0
        bu_re = bu_re.at[:, idx].add(lb_re * h0_re - lb_im * h0_im)
        bu_im = bu_im.at[:, idx].add(lb_re * h0_im + lb_im * h0_re)
    a_re = jnp.broadcast_to(lb_re, bu_re.shape)
    a_im = jnp.broadcast_to(lb_im, bu_im.shape)
    _, _, x_re, x_im = lax.associative_scan(_linear_recurrence_combine, (a_re, a_im, bu_re, bu_im),
                                            reverse=reverse, axis=1)
    return x_re, x_im


def s5_readout(x_re, x_im, c_re, c_im):
    return jnp.einsum('blgp,ghp->blgh', x_re, c_re) - jnp.einsum('blgp,ghp->blgh', x_im, c_im)


def s5_mixer(u_c, u_l, need_ctx_out, lam_re, lam_im, log_dt, b_re, b_im, c_re, c_im, d, w_glu, b_glu):
    dtype = u_l.dtype
    B, L, _ = u_l.shape
    n_ctx = u_c.shape[1]
    uc = u_c.astype(F32).reshape(B, n_ctx, S5_GROUPS, S5_GROUP)
    ul = u_l.astype(F32).reshape(B, L, S5_GROUPS, S5_GROUP)
    lb_re, lb_im, bb_re, bb_im = s5_discretise(lam_re, lam_im, log_dt, b_re, b_im)
    c_re = c_re.astype(F32)
    c_im = c_im.astype(F32)
    y_l, y_c = [], []
    for dr in range(2):
        rev = dr == 1
        end = 0 if rev else -1
        hc_re, hc_im = s5_states(uc, lb_re[dr], lb_im[dr], bb_re[dr], bb_im[dr], None, None, rev)
        hl_re, hl_im = s5_states(ul, lb_re[dr], lb_im[dr], bb_re[dr], bb_im[dr],
                                 hc_re[:, end], hc_im[:, end], rev)
        y_l.append(s5_readout(hl_re, hl_im, c_re[dr], c_im[dr]))
        if need_ctx_out:
            y_c.append(s5_readout(hc_re, hc_im, c_re[dr], c_im[dr]))
    d_g = d.astype(F32).reshape(S5_GROUPS, S5_GROUP)
    w_glu = w_glu.astype(F32)
    b_glu = b_glu.astype(F32)

    def finish(ys, u):
        y = ys[0] + ys[1] + d_g * u
        y = jax.nn.gelu(y.reshape(B, u.shape[1], S5_WIDTH))
        y = y * jax.nn.sigmoid(y @ w_glu + b_glu)
        return y.astype(dtype)

    out_l = finish(y_l, ul)
    out_c = finish(y_c, uc) if need_ctx_out else None
    return out_l, out_c


def axial_rope(n_tokens):
    rows = n_tokens // GRID_W
    row = jnp.repeat(jnp.arange(rows, dtype=F32), GRID_W)
    col = jnp.tile(jnp.arange(GRID_W, dtype=F32), rows)
    half = HEAD_DIM // 2
    inv_freq = ROPE_THETA ** (-jnp.arange(0, half, 2, dtype=F32) / half)
    ang = jnp.concatenate([row[:, None] * inv_freq, col[:, None] * inv_freq], axis=-1)
    ang = jnp.concatenate([ang, ang], axis=-1)
    return jnp.cos(ang), jnp.sin(ang)


def apply_rope(x, cos, sin):
    half = HEAD_DIM // 2
    rot = jnp.concatenate([-x[..., half:], x[..., :half]], axis=-1)
    return (x * cos[None, :, None] + rot * sin[None, :, None]).astype(x.dtype)


def attend(q, k, v):
    s = jnp.einsum('bqhgd,bkhd->bhgqk', q, k).astype(F32) * (HEAD_DIM ** -0.5)
    p = jax.nn.softmax(s, axis=-1).astype(v.dtype)
    return jnp.einsum('bhgqk,bkhd->bqhgd', p, v)


def gqa_mixer(q_c, k_c, v_c, q_l, k_l, v_l, q_norm_w, k_norm_w, need_ctx_out):
    B, L, _ = q_l.shape
    n_ctx = k_c.shape[1]
    cos, sin = axial_rope(L)
    kc = rms_norm(k_c.reshape(B, n_ctx, N_KV_HEADS, HEAD_DIM), k_norm_w)
    vc = v_c.reshape(B, n_ctx, N_KV_HEADS, HEAD_DIM)
    kl = apply_rope(rms_norm(k_l.reshape(B, L, N_KV_HEADS, HEAD_DIM), k_norm_w), cos, sin)
    vl = v_l.reshape(B, L, N_KV_HEADS, HEAD_DIM)
    ql = apply_rope(rms_norm(q_l.reshape(B, L, N_Q_HEADS, HEAD_DIM), q_norm_w), cos, sin)
    k_all = jnp.concatenate([kc, kl], axis=1)
    v_all = jnp.concatenate([vc, vl], axis=1)
    q_blocks = jnp.moveaxis(ql.reshape(B, L // Q_BLOCK, Q_BLOCK, N_KV_HEADS, Q_PER_KV, HEAD_DIM), 1, 0)
    o = lax.map(lambda qb: attend(qb, k_all, v_all), q_blocks)
    out_l = jnp.moveaxis(o, 0, 1).reshape(B, L, ATT_Q_W)
    if not need_ctx_out:
        return out_l, None
    qc = rms_norm(q_c.reshape(B, n_ctx, N_Q_HEADS, HEAD_DIM), q_norm_w)
    qc = qc.reshape(B, n_ctx, N_KV_HEADS, Q_PER_KV, HEAD_DIM)
    out_c = attend(qc, kc, vc).reshape(B, n_ctx, ATT_Q_W)
    return out_l, out_c


def rwkv7_features(f, mu, w0, w_up, a0, a_up, k_k, k_a):
    f = centred_shift_lerp(f.astype(F32), mu)
    r, k, v, w_lo, a_lo, g_lo = split_cols(
        f, (RWKV_WIDTH, RWKV_WIDTH, RWKV_WIDTH, DECAY_LORA, ICLR_LORA, GATE_LORA))
    B, L, _ = k.shape
    w_pre = w0[:, None, None] + jnp.einsum('blr,drc->dblc', jnp.tanh(w_lo), w_up)
    decay = jnp.exp(-jnp.exp(-jax.nn.softplus(-w_pre) - 0.5))
    a = jax.nn.sigmoid(a0[:, None, None] + jnp.einsum('blr,drc->dblc', a_lo, a_up))
    kk = (k * k_k).reshape(B, L, RWKV_HEADS, RWKV_HEAD)
    kk = kk / jnp.maximum(jnp.sqrt(jnp.sum(kk * kk, axis=-1, keepdims=True)), 1e-12)
    kk = kk.reshape(B, L, RWKV_WIDTH)
    k_dir = k * (1.0 + (a - 1.0) * k_a)
    return r, v, g_lo, kk, decay, a, k_dir


def to_scan_layout(t):
    fwd, bwd = (t, t) if t.ndim == 3 else (t[0], t[1])
    t2 = jnp.stack([fwd, jnp.flip(bwd, axis=1)])
    n_dir, B, L, _ = t2.shape
    return jnp.moveaxis(t2.reshape(n_dir, B, L, RWKV_HEADS, RWKV_HEAD), 2, 0)


def rwkv7_scan(s0, decay, neg_kk, kk_a, v, k, r):
    def step(s, inp):
        w_t, nkk_t, b_t, v_t, k_t = inp[:5]
        sa = jnp.einsum('dbhvk,dbhk->dbhv', s, nkk_t)
        s = s * w_t[..., None, :] + sa[..., :, None] * b_t[..., None, :] + v_t[..., :, None] * k_t[..., None, :]
        y = jnp.einsum('dbhvk,dbhk->dbhv', s, inp[5]) if len(inp) == 6 else None
        return s, y
    xs = (decay, neg_kk, kk_a, v, k) + (() if r is None else (r,))
    return lax.scan(step, s0, xs)


def rwkv7_mixer(f_c, f_l, need_ctx_out, mu, w0, w_up, a0, a_up, g_up, k_k, k_a, r_k, gn_w, gn_b):
    dtype = f_l.dtype
    mu, w0, w_up, a0, a_up, g_up = (t.astype(F32) for t in (mu, w0, w_up, a0, a_up, g_up))
    k_k, k_a, r_k, gn_w, gn_b = (t.astype(F32) for t in (k_k, k_a, r_k, gn_w, gn_b))

    def run(f, s0, with_out):
        r, v, g_lo, kk, decay, a, k_dir = rwkv7_features(f, mu, w0, w_up, a0, a_up, k_k, k_a)
        s, ys = rwkv7_scan(s0, to_scan_layout(decay), to_scan_layout(-kk), to_scan_layout(kk * a),
                           to_scan_layout(v), to_scan_layout(k_dir),
                           to_scan_layout(r) if with_out else None)
        return s, ys, r, v, g_lo, k_dir

    def readout(ys, r, v, g_lo, k_dir):
        B, L, _ = r.shape
        ys = jnp.moveaxis(ys, 0, 2)
        y = ys[0] + jnp.flip(ys[1], axis=1)
        mean = jnp.mean(y, axis=-1, keepdims=True)
        var = jnp.mean(jnp.square(y - mean), axis=-1, keepdims=True)
        y = ((y - mean) * lax.rsqrt(var + GN_EPS)).reshape(B, L, RWKV_WIDTH) * gn_w + gn_b
        rh = r.reshape(B, L, RWKV_HEADS, RWKV_HEAD)
        kh = k_dir.reshape(2, B, L, RWKV_HEADS, RWKV_HEAD)
        bonus = jnp.sum(rh * kh * r_k, axis=(0, 4))[..., None] * v.reshape(B, L, RWKV_HEADS, RWKV_HEAD)
        g = jax.nn.sigmoid(g_lo) @ g_up
        return ((y + bonus.reshape(B, L, RWKV_WIDTH)) * g).astype(dtype)

    B = f_l.shape[0]
    s0 = jnp.zeros((2, B, RWKV_HEADS, RWKV_HEAD, RWKV_HEAD), F32)
    s_c, ys_c, r_c, v_c, g_c, k_c = run(f_c, s0, need_ctx_out)
    _, ys_l, r_l, v_l, g_l, k_l = run(f_l, s_c, True)
    out_l = readout(ys_l, r_l, v_l, g_l, k_l)
    out_c = readout(ys_c, r_c, v_c, g_c, k_c) if need_ctx_out else None
    return out_l, out_c


def hybrid_mixer(h, hc, need_ctx_out, w_in,
                 s5_lam_re, s5_lam_im, s5_log_dt, s5_b_re, s5_b_im, s5_c_re, s5_c_im, s5_d, s5_w_glu, s5_b_glu,
                 q_norm_w, k_norm_w,
                 rwkv_mu, rwkv_w0, rwkv_w_up, rwkv_a0, rwkv_a_up, rwkv_g_up, rwkv_k_k, rwkv_k_a, rwkv_r_k,
                 rwkv_gn_w, rwkv_gn_b,
                 w_br_s5, w_br_att, w_br_rwkv, w_out):
    proj = h @ w_in
    proj_c = hc @ (w_in if need_ctx_out else w_in[:, :CTX_STATE_COLS])
    sizes = (S5_WIDTH, ATT_KV_W, ATT_KV_W, RWKV_COLS)
    u_l, k_l, v_l, f_l = split_cols(proj[..., :CTX_STATE_COLS], sizes)
    u_c, k_c, v_c, f_c = split_cols(proj_c[..., :CTX_STATE_COLS], sizes)
    q_l = proj[..., CTX_STATE_COLS:MIX_COLS]
    q_c = proj_c[..., CTX_STATE_COLS:MIX_COLS] if need_ctx_out else None

    y_s5, y_s5_c = s5_mixer(u_c, u_l, need_ctx_out, s5_lam_re, s5_lam_im, s5_log_dt, s5_b_re, s5_b_im,
                            s5_c_re, s5_c_im, s5_d, s5_w_glu, s5_b_glu)
    y_att, y_att_c = gqa_mixer(q_c, k_c, v_c, q_l, k_l, v_l, q_norm_w, k_norm_w, need_ctx_out)
    y_rw, y_rw_c = rwkv7_mixer(f_c, f_l, need_ctx_out, rwkv_mu, rwkv_w0, rwkv_w_up, rwkv_a0, rwkv_a_up,
                               rwkv_g_up, rwkv_k_k, rwkv_k_a, rwkv_r_k, rwkv_gn_w, rwkv_gn_b)

    def merge(gate_pre, ys5, yatt, yrw):
        g_s5, g_att, g_rw = jnp.split(jax.nn.sigmoid(gate_pre), N_BRANCH, axis=-1)
        mixed = g_s5 * (ys5 @ w_br_s5) + g_att * (yatt @ w_br_att) + g_rw * (yrw @ w_br_rwkv)
        return mixed @ w_out

    out_l = merge(proj[..., MIX_COLS:], y_s5, y_att, y_rw)
    out_c = merge(proj_c[..., MIX_COLS:], y_s5_c, y_att_c, y_rw_c) if need_ctx_out else None
    return out_l, out_c


def setup_inputs(seed: int = 0) -> dict:
    key = jax.random.key(seed)
    keys = iter(jax.random.split(key, 40))

    def nrm(shape, scale):
        return scale * jax.random.normal(next(keys), shape, F32)

    def uni(shape, lo, hi):
        return jax.random.uniform(next(keys), shape, F32, lo, hi)

    s5_gp = (DEPTH, 2, S5_GROUPS, S5_STATE)
    return {
        "x": nrm((BATCH, SEQ, D_MODEL), 1.0),
        "c": nrm((BATCH, D_MODEL), 1.0),
        "ctx": nrm((BATCH, CTX_LEN, D_MODEL), 1.0),
        "c_ctx": nrm((D_MODEL,), 1.0),
        "w_ada": nrm((DEPTH, D_MODEL, N_MOD * D_MODEL), 0.5 * D_MODEL ** -0.5),
        "b_ada": nrm((DEPTH, N_MOD * D_MODEL), 0.02),
        "norm_w": 1.0 + nrm((DEPTH, 3, D_MODEL), 0.02),
        "ffn_w1": nrm((DEPTH, 2, D_MODEL, D_FF), D_MODEL ** -0.5),
        "ffn_w3": nrm((DEPTH, 2, D_MODEL, D_FF), D_MODEL ** -0.5),
        "ffn_w2": nrm((DEPTH, 2, D_FF, D_MODEL), D_FF ** -0.5),
        "w_in": nrm((DEPTH, D_MODEL, IN_COLS), D_MODEL ** -0.5),
        "s5_lam_re": -0.5 + nrm(s5_gp, 0.01),
        "s5_lam_im": math.pi * jnp.arange(S5_STATE, dtype=F32) + nrm(s5_gp, 0.01),
        "s5_log_dt": uni((DEPTH, 2, S5_GROUPS), math.log(S5_DT_MIN), math.log(S5_DT_MAX)),
        "s5_b_re": nrm((DEPTH, 2, S5_GROUPS, S5_STATE, S5_GROUP), (2 * S5_GROUP) ** -0.5),
        "s5_b_im": nrm((DEPTH, 2, S5_GROUPS, S5_STATE, S5_GROUP), (2 * S5_GROUP) ** -0.5),
        "s5_c_re": nrm((DEPTH, 2, S5_GROUPS, S5_GROUP, S5_STATE), S5_STATE ** -0.5),
        "s5_c_im": nrm((DEPTH, 2, S5_GROUPS, S5_GROUP, S5_STATE), S5_STATE ** -0.5),
        "s5_d": nrm((DEPTH, S5_WIDTH), 1.0),
        "s5_w_glu": nrm((DEPTH, S5_WIDTH, S5_WIDTH), S5_WIDTH ** -0.5),
        "s5_b_glu": nrm((DEPTH, S5_WIDTH), 0.02),
        "q_norm_w": 1.0 + nrm((DEPTH, HEAD_DIM), 0.02),
        "k_norm_w": 1.0 + nrm((DEPTH, HEAD_DIM), 0.02),
        "rwkv_mu": uni((DEPTH, RWKV_COLS), 0.0, 1.0),
        "rwkv_w0": uni((DEPTH, 2, RWKV_WIDTH), -6.0, 1.0),
        "rwkv_w_up": nrm((DEPTH, 2, DECAY_LORA, RWKV_WIDTH), 0.1),
        "rwkv_a0": nrm((DEPTH, 2, RWKV_WIDTH), 0.1),
        "rwkv_a_up": nrm((DEPTH, 2, ICLR_LORA, RWKV_WIDTH), 0.1),
        "rwkv_g_up": nrm((DEPTH, GATE_LORA, RWKV_WIDTH), 0.1),
        "rwkv_k_k": 0.85 + nrm((DEPTH, RWKV_WIDTH), 0.02),
        "rwkv_k_a": 1.0 + nrm((DEPTH, RWKV_WIDTH), 0.02),
        "rwkv_r_k": nrm((DEPTH, RWKV_HEADS, RWKV_HEAD), 0.1),
        "rwkv_gn_w": 1.0 + nrm((DEPTH, RWKV_WIDTH), 0.02),
        "rwkv_gn_b": nrm((DEPTH, RWKV_WIDTH), 0.02),
        "w_br_s5": nrm((DEPTH, S5_WIDTH, D_MODEL), S5_WIDTH ** -0.5),
        "w_br_att": nrm((DEPTH, ATT_Q_W, D_MODEL), ATT_Q_W ** -0.5),
        "w_br_rwkv": nrm((DEPTH, RWKV_WIDTH, D_MODEL), RWKV_WIDTH ** -0.5),
        "w_out": nrm((DEPTH, D_MODEL, D_MODEL), D_MODEL ** -0.5),
    }


def reference(x, c, ctx, c_ctx, w_ada, b_ada, norm_w, ffn_w1, ffn_w3, ffn_w2, w_in,
              s5_lam_re, s5_lam_im, s5_log_dt, s5_b_re, s5_b_im, s5_c_re, s5_c_im, s5_d, s5_w_glu, s5_b_glu,
              q_norm_w, k_norm_w,
              rwkv_mu, rwkv_w0, rwkv_w_up, rwkv_a0, rwkv_a_up, rwkv_g_up, rwkv_k_k, rwkv_k_a, rwkv_r_k,
              rwkv_gn_w, rwkv_gn_b,
              w_br_s5, w_br_att, w_br_rwkv, w_out):
    for layer in range(DEPTH):
        last = layer == DEPTH - 1
        m = ada_params(c, w_ada[layer], b_ada[layer])
        mc = ada_params(c_ctx[None, :], w_ada[layer], b_ada[layer])
        ffn_a = (ffn_w1[layer, 0], ffn_w3[layer, 0], ffn_w2[layer, 0])
        ffn_b = (ffn_w1[layer, 1], ffn_w3[layer, 1], ffn_w2[layer, 1])

        x = ffn_half(x, norm_w[layer, 0], m[0], m[1], m[2], *ffn_a)
        ctx = ffn_half(ctx, norm_w[layer, 0], mc[0], mc[1], mc[2], *ffn_a)

        h = norm_modulate(x, norm_w[layer, 1], m[3], m[4])
        hc = norm_modulate(ctx, norm_w[layer, 1], mc[3], mc[4])
        mix, mix_c = hybrid_mixer(
            h, hc, not last, w_in[layer],
            s5_lam_re[layer], s5_lam_im[layer], s5_log_dt[layer], s5_b_re[layer], s5_b_im[layer],
            s5_c_re[layer], s5_c_im[layer], s5_d[layer], s5_w_glu[layer], s5_b_glu[layer],
            q_norm_w[layer], k_norm_w[layer],
            rwkv_mu[layer], rwkv_w0[layer], rwkv_w_up[layer], rwkv_a0[layer], rwkv_a_up[layer],
            rwkv_g_up[layer], rwkv_k_k[layer], rwkv_k_a[layer], rwkv_r_k[layer],
            rwkv_gn_w[layer], rwkv_gn_b[layer],
            w_br_s5[layer], w_br_att[layer], w_br_rwkv[layer], w_out[layer])
        x = x + m[5] * mix

        x = ffn_half(x, norm_w[layer, 2], m[6], m[7], m[8], *ffn_b)
        if not last:
            ctx = ctx + mc[5] * mix_c
            ctx = ffn_half(ctx, norm_w[layer, 2], mc[6], mc[7], mc[8], *ffn_b)
    return x
```

```python
import contextlib
import numpy as np
import concourse.bass as bass
import concourse.mybir as mybir
from concourse.bass_utils import run_bass_kernel_spmd

F32 = mybir.dt.float32
AF = mybir.ActivationFunctionType
ALU = mybir.AluOpType
AX = mybir.AxisListType

EPOCH = 10**9
ENGS = ("pe", "act", "dve", "pool", "sp")


class View:
    __slots__ = ("buf", "ap")

    def __init__(self, buf, ap):
        self.buf = buf
        self.ap = ap

    def __getitem__(self, k):
        return View(self.buf, self.ap[k])

    def re(self, pat_, **kw):
        return View(self.buf, self.ap.rearrange(pat_, **kw))

    def bc(self, shape):
        return View(self.buf, self.ap.to_broadcast(list(shape)))


class Buf:
    __slots__ = ("name", "t", "last_w", "readers", "dsem", "is_dram", "is_ap")

    def __init__(self, name, t, is_dram=False, is_ap=False):
        self.name = name
        self.t = t
        self.is_ap = is_ap or is_dram
        self.last_w = None
        self.readers = []
        self.dsem = None
        self.is_dram = is_dram

    def __getitem__(self, k):
        return View(self, self.t[k])

    def re(self, pat_, **kw):
        return View(self, self.t.rearrange(pat_, **kw) if self.is_ap else self.t[:].rearrange(pat_, **kw))


class Op:
    __slots__ = ("eng", "fn", "waits", "sem", "val", "inc", "idx", "dref")


def _ap(x):
    return x.ap if isinstance(x, View) else x


def _bufs(*xs):
    return [x.buf for x in xs if isinstance(x, View)]


class Prog:
    def __init__(self, nc):
        self.nc = nc
        self.ops = []
        self.stack = contextlib.ExitStack()
        self.eng_count = {e: 0 for e in ENGS}
        self.eng_sems = {e: [] for e in ENGS}
        self.waited = {}
        self.nsem = 0
        self.dma_rr = 0
        self.scopes = []
        self.pending = {e: {} for e in ENGS}
        self.all_dsems = []
        self.free_dsems = []
        self.scope_dsems = [[]]

    def _stk(self):
        return self.scopes[-1] if self.scopes else self.stack

    def sb(self, name, shape, dt=F32):
        self.nsem += 1
        t = self._stk().enter_context(self.nc.sbuf_tensor(f"sb_{name}_{self.nsem}", list(shape), dt))
        return Buf(name, t)

    def ps(self, name, shape, dt=F32):
        self.nsem += 1
        t = self._stk().enter_context(self.nc.psum_tensor(f"ps_{name}_{self.nsem}", list(shape), dt))
        return Buf(name, t)

    @contextlib.contextmanager
    def scope(self):
        st = contextlib.ExitStack()
        self.scopes.append(st)
        self.scope_dsems.append([])
        try:
            yield
        finally:
            self.scopes.pop()
            self.barrier()
            st.close()
            self.free_dsems.extend(self.scope_dsems.pop())

    def barrier(self):
        tot = {}
        for e in ENGS:
            c = self.eng_count[e]
            if c:
                sem = self.eng_sems[e][(c - 1) // EPOCH]
                tot[sem.name] = (sem, ((c - 1) % EPOCH) + 1)
        for d in self.all_dsems:
            if d[1]:
                tot[d[0].name] = (d[0], d[1])
        for e in ENGS:
            self.pending[e].update(tot)

    def dram(self, name, shape, dt=F32, kind="ExternalInput"):
        t = self.nc.dram_tensor(name, list(shape), dt, kind=kind).ap()
        return Buf(name, t, is_dram=True)

    def sub(self, view, name="sub"):
        self.nsem += 1
        return Buf(f"{name}_{self.nsem}", view.ap, is_ap=True)

    def _get_dsem(self, sb):
        if self.free_dsems:
            d = self.free_dsems.pop()
        else:
            d = [self._newsem(f"d_{self.nsem}"), 0]
            self.all_dsems.append(d)
        if sb.is_dram or not self.scopes:
            pass
        else:
            self.scope_dsems[-1].append(d)
        return d

    def _newsem(self, name):
        self.nsem += 1
        return self.stack.enter_context(self.nc.semaphore(name))

    def _deps(self, reads, writes):
        deps = set()
        for r in reads:
            if r.last_w is not None:
                deps.add(r.last_w)
        for w in writes:
            if w.last_w is not None:
                deps.add(w.last_w)
            deps.update(w.readers)
        return deps

    def _commit(self, idx, reads, writes):
        for r in reads:
            if r not in writes:
                r.readers.append(idx)
        for w in writes:
            w.last_w = idx
            w.readers = []

    def op(self, eng, fn, reads=(), writes=(), accum=False):
        o = Op()
        o.idx = len(self.ops)
        o.eng = eng
        o.fn = fn
        deps = self._deps(reads, writes)
        o.waits = self._mk_waits(eng, deps, accum)
        c = self.eng_count[eng]
        ep = c // EPOCH
        sems = self.eng_sems[eng]
        while len(sems) <= ep:
            sems.append(self._newsem(f"s_{eng}_{len(sems)}"))
        o.sem = sems[ep]
        o.val = (c % EPOCH) + 1
        o.inc = 1
        o.dref = None
        self.eng_count[eng] = c + 1
        self.ops.append(o)
        self._commit(o.idx, reads, writes)
        return o

    def dma(self, out, in_, eng=None, **kw):
        if eng is None:
            eng = "sp"
        o = Op()
        o.idx = len(self.ops)
        o.eng = eng
        oa, ia = out.ap, in_.ap
        o.fn = lambda e: e.dma_start(out=oa, in_=ia, **kw)
        reads = [in_.buf]
        writes = [out.buf]
        deps = self._deps(reads, writes)
        o.waits = self._mk_waits(eng, deps, False)
        sb = out.buf if not out.buf.is_dram else in_.buf
        if sb.is_dram:
            if getattr(self, "dd_sem", None) is None:
                self.dd_sem = [self._newsem("dd_sem"), 0]
                self.all_dsems.append(self.dd_sem)
            sb.dsem = self.dd_sem
        if sb.dsem is None:
            sb.dsem = self._get_dsem(sb)
        sb.dsem[1] += 16
        o.sem = sb.dsem[0]
        o.val = sb.dsem[1]
        o.dref = sb.dsem
        o.inc = 16
        self.ops.append(o)
        self._commit(o.idx, reads, writes)
        return o

    def allgather(self, out, in_, groups=None):
        groups = groups or [list(range(8))]
        oa, ia = out.ap, in_.ap
        if getattr(self, "cc_sem", None) is None:
            self.cc_sem = self._newsem("cc_sem")
        ccs = self.cc_sem

        def fn(e):
            e.collective_compute("AllGather", ALU.bypass, replica_groups=groups,
                                 ins=[ia.opt()], outs=[oa.opt()]).then_inc(ccs)
            e.wait_ge(ccs, 1)
            e.sem_clear(ccs)
            return e.nop()

        return self.op("pool", fn, reads=[in_.buf], writes=[out.buf])

    def _mk_waits(self, eng, deps, accum):
        waits = {}
        for d in deps:
            p = self.ops[d]
            if accum and p.eng == "pe" and eng == "pe" and p.dref is None:
                continue
            sem, val = p.sem, p.val
            if p.dref is not None:
                val = p.dref[1]
            key = sem.name
            if waits.get(key, (None, 0))[1] < val:
                waits[key] = (sem, val)
        if self.pending[eng]:
            for key, (sem, val) in self.pending[eng].items():
                if waits.get(key, (None, 0))[1] < val:
                    waits[key] = (sem, val)
            self.pending[eng] = {}
        out = []
        for key, (sem, val) in waits.items():
            k = (eng, key)
            if self.waited.get(k, 0) >= val:
                continue
            self.waited[k] = val
            out.append((sem, val))
        return out

    def mm(self, out, lhsT, rhs, start=True, stop=True, nowaw=False):
        oa, la, ra = out.ap, lhsT.ap, rhs.ap
        return self.op("pe", lambda e: e.matmul(oa, la, ra, start=start, stop=stop),
                       reads=_bufs(lhsT, rhs), writes=[out.buf], accum=(not start) or nowaw)

    def act(self, out, in_, func, bias=None, scale=None, accum_out=None, eng="act"):
        kw = {}
        if bias is not None:
            kw["bias"] = _ap(bias)
        if scale is not None:
            kw["scale"] = _ap(scale)
        if accum_out is not None:
            kw["accum_out"] = _ap(accum_out)
        oa, ia = out.ap, in_.ap
        return self.op("act", lambda e: e.activation(out=oa, in_=ia, func=func, **kw),
                       reads=_bufs(in_, bias, scale), writes=_bufs(out, accum_out))

    def tt(self, out, in0, in1, op, eng="dve"):
        oa, a, b = out.ap, in0.ap, in1.ap
        return self.op(eng, lambda e: e.tensor_tensor(out=oa, in0=a, in1=b, op=op),
                       reads=_bufs(in0, in1), writes=[out.buf])

    def ts(self, out, in0, s1, op0, s2=None, op1=None, eng="dve"):
        oa, a = out.ap, in0.ap
        s1a, s2a = _ap(s1), _ap(s2)
        if op1 is None:
            f = lambda e: e.tensor_scalar(out=oa, in0=a, scalar1=s1a, scalar2=None, op0=op0)
        else:
            f = lambda e: e.tensor_scalar(out=oa, in0=a, scalar1=s1a, scalar2=s2a, op0=op0, op1=op1)
        return self.op(eng, f, reads=_bufs(in0, s1, s2), writes=[out.buf])

    def stt(self, out, in0, scalar, in1, op0, op1, eng="dve"):
        oa, a, b, s = out.ap, in0.ap, in1.ap, _ap(scalar)
        return self.op(eng, lambda e: e.scalar_tensor_tensor(out=oa, in0=a, scalar=s, in1=b, op0=op0, op1=op1),
                       reads=_bufs(in0, in1, scalar), writes=[out.buf])

    def copy(self, out, in_, eng="dve"):
        oa, a = out.ap, in_.ap
        if eng == "act":
            return self.op("act", lambda e: e.copy(out=oa, in_=a), reads=[in_.buf], writes=[out.buf])
        return self.op(eng, lambda e: e.tensor_copy(out=oa, in_=a), reads=[in_.buf], writes=[out.buf])

    def recip(self, out, in_):
        oa, a = out.ap, in_.ap
        return self.op("dve", lambda e: e.reciprocal(out=oa, in_=a), reads=[in_.buf], writes=[out.buf])

    def reduce(self, out, in_, op=None, axis=None, eng="dve"):
        oa, a = out.ap, in_.ap
        op = ALU.add if op is None else op
        axis = AX.X if axis is None else axis
        return self.op(eng, lambda e: e.tensor_reduce(out=oa, in_=a, axis=axis, op=op), reads=[in_.buf], writes=[out.buf])

    def memset(self, out, val, eng="dve"):
        oa = out.ap
        return self.op(eng, lambda e: e.memset(oa, val), reads=[], writes=[out.buf])

    def scan(self, out, d0, d1, initial, op0, op1, eng="dve"):
        oa, a, b, i = out.ap, d0.ap, d1.ap, _ap(initial)
        return self.op(eng, lambda e: e.tensor_tensor_scan(out=oa, data0=a, data1=b, initial=i, op0=op0, op1=op1),
                       reads=_bufs(d0, d1, initial), writes=[out.buf])

    def emit(self, final_waits=()):
        nc = self.nc
        per = {e: [] for e in ENGS}
        for o in self.ops:
            per[o.eng].append(o)
        self.barrier()
        fin = list(self.pending["sp"].values())

        def run(e, lst, last=False):
            for o in lst:
                for sem, val in o.waits:
                    e.wait_ge(sem, val)
                ins = o.fn(e)
                if o.inc == 1 and o.dref is not None:
                    ins.then_inc(o.sem)
                else:
                    ins.then_inc(o.sem, o.inc)
            if last:
                for sem, val in fin:
                    e.wait_ge(sem, val)

        with nc.Block() as block:
            @block.tensor
            def _(e):
                run(e, per["pe"])

            @block.scalar
            def _(e):
                run(e, per["act"])

            @block.vector
            def _(e):
                run(e, per["dve"])

            @block.gpsimd
            def _(e):
                run(e, per["pool"])

            @block.sync
            def _(e):
                run(e, per["sp"], last=True)
        self.stack.close()


def run_spmd(nc, in_maps):
    res = run_bass_kernel_spmd(nc, in_maps, core_ids=list(range(len(in_maps))))
    return res.results


D = 1024
DFF = 2816
NK = 8
NF = 22
EPS = 1e-6
INCOLS = 5120
C0 = 0.6065306597126334
GN_EPS = 64e-5


class Cfg:
    def __init__(self, ntl=4096, ncx=64):
        self.ntl = ntl
        self.ncx = ncx
        self.nt = ntl + ncx
        self.L = 4 * ntl
        self.CTX = 4 * ncx
        self.TSEQ = self.L + self.CTX
        self.TN = min(512, ntl)

    def tiles(self):
        out = []
        c = 0
        while c < self.ntl:
            out.append((c, self.TN, 0))
            c += self.TN
        out.append((self.ntl, self.ncx, 1))
        return out


class Pieced:
    def __init__(self, nc, name, R, Rp, bounds):
        self.Rp = Rp
        self.bounds = bounds
        self.loc_t = {}
        self.gat_t = {}
        for rb in range(R // Rp):
            for ci, (a, b) in enumerate(bounds):
                self.loc_t[(rb, ci)] = Buf(f"{name}_l{rb}_{ci}", nc.dram_tensor(f"{name}_l{rb}_{ci}", [Rp, b - a], F32).ap(), is_dram=True)
                self.gat_t[(rb, ci)] = Buf(f"{name}_g{rb}_{ci}", nc.dram_tensor(f"{name}_g{rb}_{ci}", [4 * Rp, b - a], F32).ap(), is_dram=True)

    def _ci(self, c0, n):
        for ci, (a, b) in enumerate(self.bounds):
            if a <= c0 and c0 + n <= b:
                return ci, a
        raise ValueError((c0, n, self.bounds))

    def loc(self, r0, r1, c0, n):
        rb = r0 // self.Rp
        assert (r1 - 1) // self.Rp == rb
        ci, a = self._ci(c0, n)
        return self.loc_t[(rb, ci)][r0 - rb * self.Rp:r1 - rb * self.Rp, c0 - a:c0 - a + n]

    def gat(self, r0, r1, c0, n):
        rb = r0 // self.Rp
        assert (r1 - 1) // self.Rp == rb
        ci, a = self._ci(c0, n)
        v = self.gat_t[(rb, ci)].re("(r p) n -> p r n", r=4)
        return v[r0 - rb * self.Rp:r1 - rb * self.Rp, :, c0 - a:c0 - a + n]

    def gather(self, P):
        for k in self.loc_t:
            P.allgather(self.gat_t[k][:], self.loc_t[k][:], GRP4)


class PiecedT:
    def __init__(self, nc, name, F_, bounds):
        self.bounds = bounds
        self.loc_t = []
        self.gat_t = []
        for ci, (a, b) in enumerate(bounds):
            self.loc_t.append(Buf(f"{name}_l{ci}", nc.dram_tensor(f"{name}_l{ci}", [b - a, F_], F32).ap(), is_dram=True))
            self.gat_t.append(Buf(f"{name}_g{ci}", nc.dram_tensor(f"{name}_g{ci}", [4 * (b - a), F_], F32).ap(), is_dram=True))

    def loc(self, c0, n):
        for ci, (a, b) in enumerate(self.bounds):
            if a <= c0 and c0 + n <= b:
                return self.loc_t[ci][c0 - a:c0 - a + n, :]
        raise ValueError((c0, n))

    def gather(self, P):
        for l_, g_ in zip(self.loc_t, self.gat_t):
            P.allgather(g_[:], l_[:], GRP4)


def col_bounds(cfg, pc, total_lat, ctxw):
    pc = min(pc, total_lat)
    out = [(a, a + pc) for a in range(0, total_lat, pc)]
    out.append((total_lat, total_lat + ctxw))
    return out


class Consts:
    def __init__(self, P, dr):
        self.ident = P.sb("ident", [128, 128])
        self.rotm = P.sb("rotm", [128, 128])
        self.blk64 = P.sb("blk64", [128, 128])
        self.ones = P.sb("ones", [128, 128])
        P.dma(self.ident[:], dr["c_ident"][:])
        P.dma(self.rotm[:], dr["c_rotm"][:])
        P.dma(self.blk64[:], dr["c_blk64"][:])
        P.memset(self.ones[:], 1.0)


class TokCtx:
    def __init__(self, P, cfg, K):
        self.P = P
        self.K = K
        TN = cfg.TN
        self.TN = TN
        self.pa = [P.ps(f"pa{i}", [128, 512]) for i in range(2)]
        self.pb = [P.ps(f"pb{i}", [128, 512]) for i in range(2)]
        self.po = [P.ps(f"po{i}", [128, 512]) for i in range(4)]
        self.wA = [P.sb(f"wA{i}", [128, NK, 256]) for i in range(2)]
        self.wB = [P.sb(f"wB{i}", [128, NK, 256]) for i in range(2)]
        self.w2 = [P.sb(f"w2_{i}", [128, 512]) for i in range(3)]
        self.sq = [P.sb(f"sq{i}", [128, TN]) for i in range(2)]
        self.rstd = P.sb("rstd", [128, TN])
        self.tmp = [P.sb(f"tmp{i}", [128, TN]) for i in range(2)]
        self.g = P.sb("g", [128, NF, TN])
        self.cnt = 0
        self.A = P.sb("Asc", [128, 2, 3, 8])
        self.G = P.sb("Gsc", [128, 2, 2, 8])

    def set_mods(self, mod, nw):
        P = self.P
        self.mod = mod
        for s in range(2):
            for i in range(3):
                P.stt(self.A[:, s, i, :], mod[:, s, 3 * i + 1, :], 1.0, nw[:, i, :], ALU.add, ALU.mult)
            for i, j in ((0, 2), (1, 8)):
                P.ts(self.G[:, s, i, :], mod[:, s, j, :], 0.5, ALU.mult)

    def normmod(self, x, h, n, s, i):
        P = self.P
        ss = self.pa[0]
        for k in range(NK):
            sq = self.sq[k % 2]
            P.act(sq[:, :n], x[:, k, :n], AF.Square)
            P.mm(ss[:, :n], self.K.ones[:], sq[:, :n], start=(k == 0), stop=(k == NK - 1))
        P.ts(self.rstd[:, :n], ss[:, :n], 1.0 / D, ALU.mult, EPS, ALU.add)
        P.act(self.rstd[:, :n], self.rstd[:, :n], AF.Sqrt)
        P.recip(self.rstd[:, :n], self.rstd[:, :n])
        for k in range(NK):
            t = self.tmp[k % 2]
            P.tt(t[:, :n], x[:, k, :n], self.rstd[:, :n], ALU.mult)
            P.act(h[:, k, :n], t[:, :n], AF.Identity, bias=self.mod[:, s, 3 * i, k:k + 1],
                  scale=self.A[:, s, i, k:k + 1])

    def ffn(self, x, h, n, s, gi, w1_d, w3_d, w2_d):
        P = self.P
        for blk in range(NF // 2):
            wa = self.wA[blk % 2]
            wb = self.wB[blk % 2]
            P.dma(wa[:], w1_d[:, blk * 256:(blk + 1) * 256].re("(k p) f -> p k f", p=128))
            P.dma(wb[:], w3_d[:, blk * 256:(blk + 1) * 256].re("(k p) f -> p k f", p=128))
            for c in range(2):
                j = blk * 2 + c
                pa = self.pa[j % 2]
                pb = self.pb[j % 2]
                for k in range(NK):
                    P.mm(pa[:, :n], wa[:, k, c * 128:(c + 1) * 128], h[:, k, :n], start=(k == 0), stop=(k == NK - 1))
                for k in range(NK):
                    P.mm(pb[:, :n], wb[:, k, c * 128:(c + 1) * 128], h[:, k, :n], start=(k == 0), stop=(k == NK - 1))
                t = self.tmp[j % 2]
                P.act(t[:, :n], pa[:, :n], AF.Silu)
                P.tt(self.g[:, j, :n], t[:, :n], pb[:, :n], ALU.mult)
        for mh in range(2):
            for j in range(NF):
                w2 = self.w2[self.cnt % 3]
                self.cnt += 1
                P.dma(w2[:], w2_d[j * 128:(j + 1) * 128, mh * 512:(mh + 1) * 512])
                for m in range(4):
                    P.mm(self.po[m][:, :n], w2[:, m * 128:(m + 1) * 128], self.g[:, j, :n],
                         start=(j == 0), stop=(j == NF - 1))
            for m in range(4):
                k = mh * 4 + m
                P.stt(x[:, k, :n], self.po[m][:, :n], self.G[:, s, gi, k:k + 1], x[:, k, :n], ALU.mult, ALU.add)


def stage_a(P, cfg, K, mod, nw, dr, lw, bufs):
    with P.scope():
        T = TokCtx(P, cfg, K)
        T.set_mods(mod, nw)
        TN = cfg.TN
        x = P.sb("x", [128, NK, TN])
        h = P.sb("h", [128, NK, TN])
        stage = [P.sb(f"st{i}", [128, TN]) for i in range(4)]
        cs = [P.sb(f"ropec{i}", [128, TN]) for i in range(2)]
        sn = [P.sb(f"ropes{i}", [128, TN]) for i in range(2)]
        qkw = P.sb("qkw", [128, 2])
        P.dma(qkw[:], lw["qkw"][:])
        xa = bufs["xa"]
        xv = xa.re("(k p) n -> p k n", p=128)
        win = lw["w_in"]
        sti = 0
        for ti, (c0, n, s) in enumerate(cfg.tiles()):
            P.dma(x[:, :, :n], xv[:, :, c0:c0 + n])
            if s == 0:
                rc = cs[ti % 2]
                rs = sn[ti % 2]
                for hh in range(2):
                    P.dma(rc[hh * 64:(hh + 1) * 64, :n], dr["ropec"][:, c0:c0 + n])
                    P.dma(rs[hh * 64:(hh + 1) * 64, :n], dr["ropes"][:, c0:c0 + n])
            T.normmod(x, h, n, s, 0)
            T.ffn(x, h, n, s, 0, lw["w1a"], lw["w3a"], lw["w2a"])
            P.dma(xv[:, :, c0:c0 + n], x[:, :, :n], eng="pool")
            T.normmod(x, h, n, s, 1)
            for blk in range(INCOLS // 256):
                wa = T.wA[blk % 2]
                P.dma(wa[:], win[:, blk * 256:(blk + 1) * 256].re("(k p) f -> p k f", p=128))
                for c in range(2):
                    j = blk * 2 + c
                    if j == 3:
                        pv = T.po[2]
                        nsub = (n + 127) // 128
                        for sub in range(nsub):
                            w = min(128, n - sub * 128)
                            for k in range(NK):
                                P.mm(pv[:w, sub * 128:(sub + 1) * 128], h[:, k, sub * 128:sub * 128 + w],
                                     wa[:, k, 128:256], start=(k == 0), stop=(k == NK - 1))
                        st = stage[sti % 4]; sti += 1
                        for sub in range(nsub):
                            w = min(128, n - sub * 128)
                            P.copy(st[:w, sub * 128:(sub + 1) * 128], pv[:w, sub * 128:(sub + 1) * 128], eng="act")
                            P.dma(bufs["gv"].loc(c0 + sub * 128, w), st[:w, sub * 128:(sub + 1) * 128], eng="pool")
                        continue
                    pa = (T.pa + T.pb)[j % 4]
                    for k in range(NK):
                        P.mm(pa[:, :n], wa[:, k, c * 128:(c + 1) * 128], h[:, k, :n], start=(k == 0), stop=(k == NK - 1))
                    st = stage[sti % 4]; sti += 1
                    if j == 2 or 12 <= j < 16:
                        wcol = qkw[:, 1:2] if j == 2 else qkw[:, 0:1]
                        sq = T.sq[0]
                        P.act(sq[:, :n], pa[:, :n], AF.Square)
                        P.mm(T.po[0][:, :n], K.blk64[:], sq[:, :n])
                        P.ts(T.rstd[:, :n], T.po[0][:, :n], 1.0 / 64, ALU.mult, EPS, ALU.add)
                        P.act(T.rstd[:, :n], T.rstd[:, :n], AF.Sqrt)
                        P.recip(T.rstd[:, :n], T.rstd[:, :n])
                        if s == 1:
                            P.stt(st[:, :n], pa[:, :n], wcol, T.rstd[:, :n], ALU.mult, ALU.mult)
                        else:
                            xn = T.sq[1]
                            P.stt(xn[:, :n], pa[:, :n], wcol, T.rstd[:, :n], ALU.mult, ALU.mult)
                            P.mm(T.po[1][:, :n], K.rotm[:], xn[:, :n])
                            P.tt(T.tmp[0][:, :n], xn[:, :n], rc[:, :n], ALU.mult)
                            P.tt(T.tmp[1][:, :n], T.po[1][:, :n], rs[:, :n], ALU.mult)
                            P.tt(st[:, :n], T.tmp[0][:, :n], T.tmp[1][:, :n], ALU.add, eng="pool")
                        if j == 2:
                            dst = bufs["gk"].loc(0, 128, c0, n)
                        else:
                            dst = bufs["qb"][(j - 12) * 128:(j - 11) * 128, c0:c0 + n]
                    elif j >= 16:
                        P.act(st[:, :n], pa[:, :n], AF.Sigmoid)
                        dst = bufs["sgate"][(j - 16) * 128:(j - 15) * 128, c0:c0 + n]
                    else:
                        if j % 2 == 0:
                            P.copy(st[:, :n], pa[:, :n], eng="act")
                        else:
                            P.copy(st[:, :n], pa[:, :n], eng="dve")
                        if j < 2:
                            dst = bufs["gu"].loc(j * 128, (j + 1) * 128, c0, n)
                        else:
                            dst = bufs["gf"].loc((j - 4) * 128, (j - 3) * 128, c0, n)
                    P.dma(dst, st[:, :n], eng="pool")


def stage_gqa(P, cfg, K, bufs):
    with P.scope():
        NT, TS = cfg.nt, cfg.TSEQ
        nkt = TS // 128
        nct = cfg.CTX // 128
        gk, gv = bufs["gk"], bufs["gv"]
        kt = P.sb("kt", [128, TS])
        vt = P.sb("vt", [128, nkt, 2, 65])
        P.memset(vt[:, :, :, 64:65], 1.0)
        off = 0
        for ci, (a, b) in enumerate(gk.bounds):
            w = b - a
            P.dma(kt[:, off:off + 4 * w].re("p (r n) -> p r n", r=4), gk.gat(0, 128, a, w))
            off += 4 * w
        assert gv.bounds == gk.bounds
        off = 0
        for ci, (a, b) in enumerate(gv.bounds):
            w4 = 4 * (b - a)
            assert off % 128 == 0 and w4 % 128 == 0
            vview = gv.gat_t[ci].re("(t p) (h d) -> p t h d", p=128, h=2)
            nt_ = w4 // 128
            for t0 in range(0, nt_, 13):
                t1 = min(nt_, t0 + 13)
                for hh in range(2):
                    P.dma(vt[:, off // 128 + t0:off // 128 + t1, hh, 0:64], vview[:, t0:t1, hh, :])
            off += w4
        ktc = kt[:, cfg.L:TS]
        vtc = vt[:, cfg.L // 128:nkt, :, :]
        qt = [P.sb(f"qt{i}", [128, 4, cfg.TN]) for i in range(2)]
        pt = [P.sb(f"pt{i}", [128, cfg.TN]) for i in range(3)]
        ytm = P.sb("ytm", [128, 4, 512])
        yfm = [P.sb(f"yfm{i}", [128, cfg.TN]) for i in range(2)]
        rec = P.sb("rec", [128, 4])
        ps = [P.ps(f"ps{i}", [128, 512]) for i in range(3)]
        po = [P.ps(f"po{i}", [128, 128]) for i in range(4)]
        p2 = [P.ps(f"p2{i}", [128, 512]) for i in range(1)]
        qb = bufs["qb"]
        qv = qb.re("(hk g d) n -> hk d g n", hk=2, g=4)
        it = 0
        for ti, (c0, n, s) in enumerate(cfg.tiles()):
            q = qt[ti % 2]
            for hk in range(2):
                P.dma(q[hk * 64:(hk + 1) * 64, :, :n], qv[hk, :, :, c0:c0 + n])
            nsub = (n + 127) // 128
            keys, vals, ntile = (kt, vt, nkt) if s == 0 else (ktc, vtc, nct)
            for hk in range(2):
                for g in range(4):
                    hd = hk * 4 + g
                    for t in range(ntile):
                        sc = ps[it % 3]
                        pr = pt[it % 3]
                        it += 1
                        P.mm(sc[:, :n], keys[hk * 64:(hk + 1) * 64, t * 128:(t + 1) * 128],
                             q[hk * 64:(hk + 1) * 64, g, :n])
                        P.act(pr[:, :n], sc[:, :n], AF.Exp, scale=0.125)
                        for sub in range(nsub):
                            w = min(128, n - sub * 128)
                            P.mm(po[sub][:w, 0:65], pr[:, sub * 128:sub * 128 + w], vals[:, t, hk, :],
                                 start=(t == 0), stop=(t == ntile - 1))
                    for sub in range(nsub):
                        w = min(128, n - sub * 128)
                        P.recip(rec[:w, sub:sub + 1], po[sub][:w, 64:65])
                        P.ts(ytm[:w, sub, hd * 64:(hd + 1) * 64], po[sub][:w, 0:64], rec[:w, sub:sub + 1], ALU.mult)
            for fc in range(4):
                pp = p2[0]
                yf = yfm[fc % 2]
                for sub in range(nsub):
                    w = min(128, n - sub * 128)
                    P.mm(pp[:, sub * 128:sub * 128 + w], ytm[:w, sub, fc * 128:(fc + 1) * 128], K.ident[:w, :w])
                P.copy(yf[:, :n], pp[:, :n], eng="act")
                P.dma(bufs["yatt"][fc * 128:(fc + 1) * 128, c0:c0 + n], yf[:, :n], eng="pool")


def stage_b(P, cfg, K, mod, nw, dr, lw, bufs, out_dst):
    with P.scope():
        T = TokCtx(P, cfg, K)
        T.set_mods(mod, nw)
        TN = cfg.TN
        x = P.sb("x", [128, NK, TN])
        h = P.sb("h", [128, NK, TN])
        sel = P.sb("sel", [128, 4, 128])
        P.dma(sel[:], dr["selI"][:])
        wglu = P.sb("wglu", [128, 2, 256])
        bglu = P.sb("bglu", [128, 2])
        wbs = P.sb("wbs", [128, 2, D])
        wba = P.sb("wba", [128, 4, D])
        wbr = P.sb("wbr", [128, 2, D])
        P.dma(wglu[:], lw["w_glu"].re("(k p) f -> p k f", p=128))
        P.dma(bglu[:], lw["b_glu"][:])
        P.dma(wbs[:], lw["w_br_s5"].re("(k p) f -> p k f", p=128))
        P.dma(wba[:], lw["w_br_att"].re("(k p) f -> p k f", p=128))
        P.dma(wbr[:], lw["w_br_rw"].re("(k p) f -> p k f", p=128))
        cand = [P.sb(f"cand{i}", [128, 4, TN]) for i in range(1)]
        ys = P.sb("ys", [128, 2, TN])
        yr = h[:, 6:8, :]
        ya = h[:, 2:6, :]
        ge = h[:, 0:2, :]
        sg = [P.sb(f"sg{i}", [128, 3, TN]) for i in range(1)]
        xv = bufs["xa"].re("(k p) n -> p k n", p=128)
        ov = out_dst.re("(k p) n -> p k n", p=128)
        gs5 = bufs["s5o"]
        grw = bufs["rwo"]
        sgv = bufs["sgate"].re("(b k p) n -> p b k n", b=3, p=128)
        mixed = T.g
        ci = 0
        for ti, (c0, n, s) in enumerate(cfg.tiles()):
            P.dma(x[:, :, :n], xv[:, :, c0:c0 + n])
            P.dma(ya[:, :, :n], bufs["yatt"].re("(k p) n -> p k n", p=128)[:, :, c0:c0 + n])
            for src, dstt in ((gs5, ys), (grw, yr)):
                for kk in range(2):
                    cd = cand[0]
                    for r_ in range(4):
                        if s == 0:
                            gt_ = src.gat_t[(0, r_)][kk * 128:(kk + 1) * 128, c0:c0 + n]
                        else:
                            gt_ = src.gat_t[(0, 4)][kk * 128:(kk + 1) * 128, r_ * cfg.ncx:(r_ + 1) * cfg.ncx]
                        P.dma(cd[:, r_, :n], gt_)
                    pp = T.pa[kk]
                    for r in range(4):
                        P.mm(pp[:, :n], sel[:, r, :], cd[:, r, :n], start=(r == 0), stop=(r == 3))
                    P.copy(dstt[:, kk, :n], pp[:, :n], eng="act")
            for kk in range(2):
                y = ys[:, kk, :n]
                t0 = T.tmp[0][:, :n]
                t1 = T.tmp[1][:, :n]
                P.tt(t0, y, y, ALU.mult)
                P.ts(t0, t0, 0.044715, ALU.mult, 1.0, ALU.add)
                P.tt(t0, t0, y, ALU.mult)
                P.act(t1, t0, AF.Sigmoid, scale=1.5957691216057308)
                P.tt(ge[:, kk, :n], y, t1, ALU.mult)
            for mm_ in range(2):
                pp = T.pb[mm_]
                for kk in range(2):
                    P.mm(pp[:, :n], wglu[:, kk, mm_ * 128:(mm_ + 1) * 128], ge[:, kk, :n], start=(kk == 0), stop=(kk == 1))
                t1 = T.tmp[mm_][:, :n]
                P.act(t1, pp[:, :n], AF.Sigmoid, bias=bglu[:, mm_:mm_ + 1])
                P.tt(ys[:, mm_, :n], ge[:, mm_, :n], t1, ALU.mult)
            for m in range(NK):
                sgt = sg[0]
                P.dma(sgt[:, :, :n], sgv[:, :, m, c0:c0 + n])
                p0, p1, p2 = T.po[0], T.po[1], T.po[2]
                for kk in range(2):
                    P.mm(p0[:, :n], wbs[:, kk, m * 128:(m + 1) * 128], ys[:, kk, :n], start=(kk == 0), stop=(kk == 1))
                for kk in range(4):
                    P.mm(p1[:, :n], wba[:, kk, m * 128:(m + 1) * 128], ya[:, kk, :n], start=(kk == 0), stop=(kk == 3))
                for kk in range(2):
                    P.mm(p2[:, :n], wbr[:, kk, m * 128:(m + 1) * 128], yr[:, kk, :n], start=(kk == 0), stop=(kk == 1))
                t0 = T.tmp[0][:, :n]
                t1 = T.tmp[1][:, :n]
                P.tt(t0, p0[:, :n], sgt[:, 0, :n], ALU.mult)
                P.tt(t1, p1[:, :n], sgt[:, 1, :n], ALU.mult)
                P.tt(t0, t0, t1, ALU.add, eng="pool")
                P.tt(t1, p2[:, :n], sgt[:, 2, :n], ALU.mult)
                P.tt(mixed[:, m, :n], t0, t1, ALU.add, eng="pool")
            wout = lw["w_out"]
            for blk in range(4):
                wa = T.wA[blk % 2]
                P.dma(wa[:], wout[:, blk * 256:(blk + 1) * 256].re("(k p) f -> p k f", p=128))
                for c in range(2):
                    m = blk * 2 + c
                    pp = T.pa[m % 2]
                    for kk in range(NK):
                        P.mm(pp[:, :n], wa[:, kk, c * 128:(c + 1) * 128], mixed[:, kk, :n], start=(kk == 0), stop=(kk == NK - 1))
                    P.stt(x[:, m, :n], pp[:, :n], T.mod[:, s, 5, m:m + 1], x[:, m, :n], ALU.mult, ALU.add)
            T.normmod(x, h, n, s, 2)
            T.ffn(x, h, n, s, 1, lw["w1b"], lw["w3b"], lw["w2b"])
            P.dma(ov[:, :, c0:c0 + n], x[:, :, :n], eng="pool")


I32 = mybir.dt.int32
TWO_PI = 6.283185307179586


def sincos(P, ang, sin_out, cos_out, tmp, tmp2, tmpi):
    for out, off in ((sin_out, 32.5), (cos_out, 32.75)):
        P.ts(tmp, ang, 1.0 / TWO_PI, ALU.mult, off, ALU.add)
        P.copy(tmpi, tmp)
        P.copy(tmp2, tmpi)
        P.tt(tmp, tmp, tmp2, ALU.subtract)
        P.ts(tmp2, tmp, 0.0, ALU.is_lt)
        P.tt(tmp, tmp, tmp2, ALU.add)
        P.act(out, tmp, AF.Sin, scale=TWO_PI * (1 - 1e-6), bias=-3.141592653589793 * (1 - 1e-6))


def stage_s5(P, cfg, K, dr, lw, bufs, layer):
    NS = cfg.TSEQ // 8
    NCS = cfg.CTX // 8
    NLS = cfg.ntl // 8
    gu = bufs["gu"]
    with P.scope():
        lam = P.sb("lam", [128, 3, 4])
        bL = P.sb("bL", [128, 2, 4, 256])
        cR = P.sb("cR", [128, 2, 4, 64])
        cIn = P.sb("cIn", [128, 4, 64])
        Dm = P.sb("Dm", [128, 2, 64])
        jt = P.sb("jt", [128, 4, 9])
        P.dma(lam[:], lw["s5_lam"][:])
        P.dma(bL[:], lw["s5_bL"][:])
        P.dma(cR[:], lw["s5_cR"][:])
        P.dma(Dm[:], lw["s5_D"][:])
        P.dma(jt[:], dr["c_jt"][:])
        P.ts(cIn[:], cR[:, 1, :, :], -1.0, ALU.mult)
        dt = P.sb("dt", [128, 4]); aa = P.sb("aa", [128, 4]); th = P.sb("th", [128, 4])
        P.act(dt[:], lam[:, 2, :], AF.Exp)
        P.tt(aa[:], lam[:, 0, :], dt[:], ALU.mult)
        P.tt(th[:], lam[:, 1, :], dt[:], ALU.mult)
        ea = P.sb("ea", [128, 4, 9]); an = P.sb("an", [128, 4, 9])
        sn = P.sb("sn", [128, 4, 9]); cn = P.sb("cn", [128, 4, 9])
        t1 = P.sb("t1", [128, 4, 9]); t2 = P.sb("t2", [128, 4, 9]); ti = P.sb("ti", [128, 4, 9], I32)
        for c in range(4):
            P.ts(ea[:, c, :], jt[:, c, :], aa[:, c:c + 1], ALU.mult)
            P.ts(an[:, c, :], jt[:, c, :], th[:, c:c + 1], ALU.mult)
        P.act(ea[:], ea[:], AF.Exp)
        sincos(P, an[:], sn[:], cn[:], t1[:], t2[:], ti[:])
        pr = P.sb("pr", [128, 4, 9]); pi = P.sb("pi", [128, 4, 9])
        P.tt(pr[:], ea[:], cn[:], ALU.mult)
        P.tt(pi[:], ea[:], sn[:], ALU.mult)
        den = P.sb("den", [128, 4]); nr = P.sb("nr", [128, 4]); fr = P.sb("fr", [128, 4]); fi = P.sb("fi", [128, 4])
        u1 = P.sb("u1", [128, 4]); u2 = P.sb("u2", [128, 4])
        P.tt(den[:], lam[:, 0, :], lam[:, 0, :], ALU.mult)
        P.tt(u1[:], lam[:, 1, :], lam[:, 1, :], ALU.mult)
        P.tt(den[:], den[:], u1[:], ALU.add)
        P.recip(den[:], den[:])
        P.ts(nr[:], pr[:, :, 1], -1.0, ALU.add)
        P.tt(u1[:], nr[:], lam[:, 0, :], ALU.mult)
        P.tt(u2[:], pi[:, :, 1], lam[:, 1, :], ALU.mult)
        P.tt(fr[:], u1[:], u2[:], ALU.add)
        P.tt(fr[:], fr[:], den[:], ALU.mult)
        P.tt(u1[:], pi[:, :, 1], lam[:, 0, :], ALU.mult)
        P.tt(u2[:], nr[:], lam[:, 1, :], ALU.mult)
        P.tt(fi[:], u1[:], u2[:], ALU.subtract)
        P.tt(fi[:], fi[:], den[:], ALU.mult)
        zr = P.sb("zr", [128, 4, 9]); zi = P.sb("zi", [128, 4, 9]); zin = P.sb("zin", [128, 4, 9])
        for c in range(4):
            P.ts(t1[:, c, :], pi[:, c, :], fi[:, c:c + 1], ALU.mult)
            P.stt(zr[:, c, :], pr[:, c, :], fr[:, c:c + 1], t1[:, c, :], ALU.mult, ALU.subtract)
            P.ts(t1[:, c, :], pi[:, c, :], fr[:, c:c + 1], ALU.mult)
            P.stt(zi[:, c, :], pr[:, c, :], fi[:, c:c + 1], t1[:, c, :], ALU.mult, ALU.add)
        P.ts(zin[:], zi[:], -1.0, ALU.mult)
        pin = P.sb("pin", [128, 4, 9])
        P.ts(pin[:], pi[:], -1.0, ALU.mult)
        NLEV = 1
        while (1 << NLEV) < NS:
            NLEV += 1
        qr = P.sb("qr", [128, 4, 16]); qi = P.sb("qi", [128, 4, 16]); qin = P.sb("qin", [128, 4, 16])
        P.copy(qr[:, :, 0], pr[:, :, 8]); P.copy(qi[:, :, 0], pi[:, :, 8])
        for k in range(1, NLEV):
            P.tt(u1[:], qr[:, :, k - 1], qr[:, :, k - 1], ALU.mult)
            P.tt(u2[:], qi[:, :, k - 1], qi[:, :, k - 1], ALU.mult)
            P.tt(qr[:, :, k], u1[:], u2[:], ALU.subtract)
            P.tt(u1[:], qr[:, :, k - 1], qi[:, :, k - 1], ALU.mult)
            P.ts(qi[:, :, k], u1[:], 2.0, ALU.mult)
        P.ts(qin[:], qi[:], -1.0, ALU.mult)

        X = [[[P.sb(f"X{d}{p}{r}", [128, NS + 2]) for r in range(2)] for p in range(2)] for d in range(2)]
        Kmat = [[P.sb(f"Km{d}{j}", [128, 2, 64]) for j in range(8)] for d in range(2)]
        for d in range(2):
            for p in range(2):
                for r in range(2):
                    P.memset(X[d][p][r][:], 0.0)
        pw = [P.ps(f"pw{i}", [128, 512]) for i in range(4)]
        pk = [P.ps(f"pk{i}", [128, 2, 64]) for i in range(2)]

        def blocks():
            return [(True, 0, cfg.CTX)] + [(False, r, cfg.ntl) for r in range(4)]

        def pos_f(is_ctx, r):
            return 0 if is_ctx else NCS + r * NLS

        def pos_b(is_ctx, r):
            return 4 * NLS if is_ctx else r * NLS

        def load_u(ut, is_ctx, r):
            if is_ctx:
                for k_ in range(2):
                    P.dma(ut[:, k_, :cfg.CTX].re("p (r n) -> p r n", r=4), gu.gat(k_ * 128, (k_ + 1) * 128, cfg.ntl, cfg.ncx))
            else:
                for (a, b) in gu.bounds[:-1]:
                    for k_ in range(2):
                        P.dma(ut[:, k_, a:b], gu.gat(k_ * 128, (k_ + 1) * 128, a, b - a)[:, r, :])

        with P.scope():
            BcT = [[[P.sb(f"Bc{c}{r}{j}", [128, 2, 128]) for j in range(8)] for r in range(2)] for c in range(4)]
            Wre = [P.sb(f"Wre{i}", [128, 256]) for i in range(2)]
            Wim = [P.sb(f"Wim{i}", [128, 256]) for i in range(2)]
            tw = P.sb("tw", [128, 256])
            for d in range(2):
                for j in range(8):
                    for p in range(2):
                        c = d * 2 + p
                        wr, wi = Wre[p], Wim[p]
                        P.ts(tw[:], bL[:, 1, c, :], zi[:, c, j:j + 1], ALU.mult)
                        P.stt(wr[:], bL[:, 0, c, :], zr[:, c, j:j + 1], tw[:], ALU.mult, ALU.subtract)
                        P.ts(tw[:], bL[:, 1, c, :], zr[:, c, j:j + 1], ALU.mult)
                        P.stt(wi[:], bL[:, 0, c, :], zi[:, c, j:j + 1], tw[:], ALU.mult, ALU.add)
                        for r, w in ((0, wr), (1, wi)):
                            pp = pw[(2 * p + r) % 4]
                            for kt in range(2):
                                P.mm(pp[:, kt * 128:(kt + 1) * 128], w[:, kt * 128:(kt + 1) * 128], K.ident[:])
                            P.copy(BcT[c][r][j][:], pp[:, 0:256].re("p (k m) -> p k m", k=2), eng="act")
                    pq = pk[j % 2]
                    for kt in range(2):
                        lst = []
                        for p in range(2):
                            c = d * 2 + p
                            lst.append((Wre[p][:, kt * 128:(kt + 1) * 128], cR[:, 0, c, :]))
                            lst.append((Wim[p][:, kt * 128:(kt + 1) * 128], cIn[:, c, :]))
                        for i_, (l_, r_) in enumerate(lst):
                            P.mm(pq[:, kt, :], l_, r_, start=(i_ == 0), stop=(i_ == 3))
                    if j == 0 and d == 0:
                        P.tt(Kmat[d][j][:], pq[:], Dm[:], ALU.add)
                    else:
                        P.copy(Kmat[d][j][:], pq[:])
            ut = P.sb("ut", [128, 2, max(cfg.ntl, cfg.CTX)])
            cnt = 0
            for (is_ctx, r, ntok) in blocks():
                load_u(ut, is_ctx, r)
                nsup = ntok // 8
                uv = ut[:, :, :ntok].re("p k (c s) -> p k s c", s=8)
                for d in range(2):
                    p0 = (pos_f if d == 0 else pos_b)(is_ctx, r) + (1 if d == 0 else 0)
                    for p in range(2):
                        c = d * 2 + p
                        for ri in range(2):
                            pp = pw[cnt % 4]; cnt += 1
                            n_ = 0
                            for s_ in range(8):
                                j = 7 - s_ if d == 0 else s_
                                for kt in range(2):
                                    P.mm(pp[:, :nsup], BcT[c][ri][j][:, kt, :], uv[:, kt, s_, :],
                                         start=(n_ == 0), stop=(n_ == 15))
                                    n_ += 1
                            if cnt % 2:
                                P.copy(X[d][p][ri][:, p0:p0 + nsup], pp[:, :nsup], eng="act")
                            else:
                                P.copy(X[d][p][ri][:, p0:p0 + nsup], pp[:, :nsup])
        with P.scope():
            Y = [P.sb(f"Y{r}", [128, NS + 2]) for r in range(2)]
            for d in range(2):
                for p in range(2):
                    c = d * 2 + p
                    P.memset(Y[0][:], 0.0); P.memset(Y[1][:], 0.0)
                    src = X[d][p]
                    dst = Y
                    for k in range(NLEV):
                        dd = 1 << k
                        a_, b_, bn_ = qr[:, c, k:k + 1], qi[:, c, k:k + 1], qin[:, c, k:k + 1]
                        if d == 0:
                            lo, hi = 1, NS + 1
                            o_t, o_s, o_k = slice(lo + dd, hi), slice(lo, hi - dd), slice(lo, lo + dd)
                        else:
                            lo, hi = 0, NS
                            o_t, o_s, o_k = slice(lo, hi - dd), slice(lo + dd, hi), slice(hi - dd, hi)
                        sr, si = src[0], src[1]
                        dr_, di = dst[0], dst[1]
                        P.stt(dr_[:, o_t], sr[:, o_s], a_, sr[:, o_t], ALU.mult, ALU.add)
                        P.stt(dr_[:, o_t], si[:, o_s], bn_, dr_[:, o_t], ALU.mult, ALU.add)
                        P.stt(di[:, o_t], si[:, o_s], a_, si[:, o_t], ALU.mult, ALU.add)
                        P.stt(di[:, o_t], sr[:, o_s], b_, di[:, o_t], ALU.mult, ALU.add)
                        P.copy(dr_[:, o_k], sr[:, o_k], eng="act")
                        P.copy(di[:, o_k], si[:, o_k], eng="act")
                        src, dst = dst, src
                    if NLEV % 2 == 1:
                        P.copy(X[d][p][0][:], Y[0][:], eng="act")
                        P.copy(X[d][p][1][:], Y[1][:], eng="act")
        with P.scope():
            Cc = [[[P.sb(f"Cc{c}{r}{j}", [128, 64]) for j in range(9)] for r in range(2)] for c in range(4)]
            tc_ = P.sb("tc", [128, 64])
            for c in range(4):
                for j in range(1, 9):
                    P.ts(tc_[:], cR[:, 1, c, :], pi[:, c, j:j + 1], ALU.mult)
                    P.stt(Cc[c][0][j][:], cR[:, 0, c, :], pr[:, c, j:j + 1], tc_[:], ALU.mult, ALU.subtract)
                    P.ts(tc_[:], cR[:, 1, c, :], pr[:, c, j:j + 1], ALU.mult, -1.0, ALU.mult)
                    P.stt(Cc[c][1][j][:], cR[:, 0, c, :], pin[:, c, j:j + 1], tc_[:], ALU.mult, ALU.add)
            ut = P.sb("ut3", [128, 2, max(cfg.ntl, cfg.CTX)])
            yt = [P.sb(f"yt{i}", [64, max(cfg.ntl, cfg.CTX)]) for i in range(2)]
            cnt = 0
            for bi, (is_ctx, r, ntok) in enumerate(blocks()):
                load_u(ut, is_ctx, r)
                nsup = ntok // 8
                uv = ut[:, :, :ntok].re("p k (c s) -> p k s c", s=8)
                y = yt[bi % 2]
                yv = y[:, :ntok].re("p (c s) -> p s c", s=8)
                for tau in range(8):
                    pp = pw[cnt % 4]; cnt += 1
                    ops = []
                    for d in range(2):
                        pos = (pos_f if d == 0 else pos_b)(is_ctx, r)
                        xo = pos if d == 0 else pos + 1
                        jc = tau + 1 if d == 0 else 8 - tau
                        for p in range(2):
                            c = d * 2 + p
                            for ri in range(2):
                                ops.append((Cc[c][ri][jc][:], X[d][p][ri][:, xo:xo + nsup]))
                        taps = range(0, tau + 1) if d == 0 else range(tau, 8)
                        for s_ in taps:
                            j = tau - s_ if d == 0 else s_ - tau
                            km = Kmat[d][j]
                            for kt in range(2):
                                ops.append((km[:, kt, :], uv[:, kt, s_, :]))
                    for i_, (l_, r_) in enumerate(ops):
                        P.mm(pp[0:64, :nsup], l_, r_, start=(i_ == 0), stop=(i_ == len(ops) - 1))
                    if tau % 2:
                        P.copy(yv[:, tau, :], pp[0:64, :nsup], eng="act")
                    else:
                        P.copy(yv[:, tau, :], pp[0:64, :nsup])
                if is_ctx:
                    P.dma(bufs["s5o"].loc(0, 64, cfg.L, cfg.CTX), y[:, :ntok], eng="pool")
                else:
                    P.dma(bufs["s5o"].loc(0, 64, r * cfg.ntl, cfg.ntl), y[:, :ntok], eng="pool")


def stage_rwkv(P, cfg, K, dr, lw, bufs):
    NCH = cfg.TSEQ // 64
    NCC = cfg.CTX // 64
    N = 256 if cfg.ntl >= 256 else cfg.ntl
    NCT = N // 64
    gf = bufs["gf"]
    scr, scrpc, scr2, yscr = bufs["rw_scr"], bufs["rw_pc"], bufs["rw_scr2"], bufs["rw_y"]

    def tiles():
        out = [("ctx", 0, None)]
        for r in range(4):
            for c0 in range(0, cfg.ntl, N):
                out.append(("lat", NCC + (r * cfg.ntl + c0) // 64, (r, c0)))
        return out

    with P.scope():
        sel = P.sb("sel", [128, 2, 128]); P.dma(sel[:], lw["rw_sel"][:])
        cols = P.sb("cols", [128, 10]); P.dma(cols[:], lw["rw_cols"][:])
        lup = P.sb("lup", [128, 128]); P.dma(lup[:], lw["rw_lup"][:])
        gup = P.sb("gup", [128, 64]); P.dma(gup[:], lw["rw_gup"][:])
        gn = P.sb("gn", [64, 2, NCT, 64]); P.dma(gn[:], lw["rw_gn"][:])
        masks = P.sb("masks", [64, 4, 64]); P.dma(masks[:], dr["c_masks"][:])
        reset = P.sb("reset", [128, N]); P.dma(reset[:], dr["c_reset"][:, :N])
        om = P.sb("om", [128, 10]); hm = P.sb("hm", [128, 10])
        P.ts(om[:], cols[:], -1.0, ALU.mult, 1.0, ALU.add)
        P.ts(hm[:], cols[:], 0.5, ALU.mult)
        SL, SU, IL, IU = (masks[:, i, :] for i in range(4))
        mL = (SL, SU); mN = (SU, SL); mNI = (IU, IL)
        id64 = (K.ident[0:64, 0:64], K.ident[64:128, 64:128])

        with P.scope():
            banks = [P.ps(f"bk{i}", [128, 512]) for i in range(8)]
            psel, pwp, pap, pss = banks[0], banks[1], banks[2], banks[3]
            sc = [0]

            def bank():
                sc[0] += 1
                return banks[4 + sc[0] % 4]

            rkv = [P.sb(f"rkv{i}", [128, 6, N + 2]) for i in range(2)]
            lo = [P.sb(f"lo{i}", [128, N + 2]) for i in range(2)]
            glo = [P.sb(f"glo{i}", [128, N + 2]) for i in range(2)]
            W = {nm: P.sb("w_" + nm, [128, N + 2]) for nm in
                 ("ts", "s", "r", "k", "v", "lol", "gll", "tl", "sg", "ag", "kks", "sq", "rn", "kkn", "bv",
                  "kd", "cs", "csd", "tmp", "e1", "e2", "e3", "At", "Bt", "Kt", "Rt", "Bh", "Kh", "rkk", "sgl")}
            tot = P.sb("tot", [128, NCT]); e3t = P.sb("e3t", [128, NCT])
            PAD = {nm: P.sb("pad_" + nm, [128, N]) for nm in ("At", "Bt", "Kt", "v", "Bh", "Kh")}
            for nm in PAD:
                P.memset(PAD[nm][0:64, :], 0.0)
            NSET = 2 * NCT
            CT = {nm: P.sb(f"cu_{nm}", [64, NSET, 64]) for nm in ("L0", "L1", "N0", "N1", "Q", "Atm", "Wsb", "LakT")}
            pack = P.sb("cu_pack", [64, NSET, 8, 64])
            pack2 = P.sb("cu_pack2", [64, NCT, 2, 64])
            MK = {nm: P.sb(f"mk_{nm}", [64, NSET, 64]) for nm in ("L", "N", "NI", "I")}
            for i in range(NSET):
                u = i % 2
                P.copy(MK["L"][:, i, :], mL[u]); P.copy(MK["N"][:, i, :], mN[u]); P.copy(MK["NI"][:, i, :], mNI[u])
                P.copy(MK["I"][:, i, :], K.ident[0:64, 0:64])

            for ti, (kind, ch0, info) in enumerate(tiles()):
                a_rkv, a_lo, a_glo = rkv[ti % 2], lo[ti % 2], glo[ti % 2]
                for t_ in (a_rkv, a_lo, a_glo):
                    if t_ is a_rkv:
                        P.memset(t_[:, :, 0:1], 0.0); P.memset(t_[:, :, N + 1:N + 2], 0.0)
                    else:
                        P.memset(t_[:, 0:1], 0.0); P.memset(t_[:, N + 1:N + 2], 0.0)

                def ld(rank, src0, n_, dst0):
                    kw = {"allow_slow_non_contiguous": True} if n_ == 1 else {}
                    for a_ in range(3):
                        for k_ in range(2):
                            rr0 = a_ * 256 + k_ * 128
                            P.dma(a_rkv[:, 2 * a_ + k_, dst0:dst0 + n_], gf.gat(rr0, rr0 + 128, src0, n_)[:, rank, :], **kw)
                    P.dma(a_lo[:, dst0:dst0 + n_], gf.gat(768, 896, src0, n_)[:, rank, :], **kw)
                    P.dma(a_glo[:, dst0:dst0 + n_], gf.gat(896, 1024, src0, n_)[:, rank, :], **kw)

                if kind == "ctx":
                    for r in range(4):
                        ld(r, cfg.ntl, cfg.ncx, 1 + r * cfg.ncx)
                else:
                    r, c0 = info
                    ld(r, c0, N, 1)
                    if c0 > 0:
                        ld(r, c0 - 1, 1, 0)
                    elif r > 0:
                        ld(r - 1, cfg.ntl - 1, 1, 0)
                    if c0 + N < cfg.ntl:
                        ld(r, c0 + N, 1, N + 1)
                    elif r < 3:
                        ld(r + 1, 0, 1, N + 1)

                def lerp(dst, src, ci):
                    P.tt(W["s"][:, :N], src[:, 0:N], src[:, 2:N + 2], ALU.add, eng="pool")
                    P.ts(W["s"][:, :N], W["s"][:, :N], hm[:, ci:ci + 1], ALU.mult, eng="pool")
                    P.stt(dst[:, :N], src[:, 1:N + 1], om[:, ci:ci + 1], W["s"][:, :N], ALU.mult, ALU.add)

                for ai, nm in enumerate(("r", "k", "v")):
                    for kt in range(2):
                        P.mm(psel[:, :N + 2], sel[:, kt, :], a_rkv[:, 2 * ai + kt, :], start=(kt == 0), stop=(kt == 1))
                    P.copy(W["ts"][:], psel[:, :N + 2], eng="act")
                    lerp(W[nm], W["ts"], ai)
                lerp(W["lol"], a_lo, 8)
                lerp(W["gll"], a_glo, 9)
                r_, k_, v_ = W["r"], W["k"], W["v"]
                P.act(W["tl"][0:64, :N], W["lol"][0:64, :N], AF.Tanh)
                P.mm(pwp[:, :N], lup[0:64, :], W["tl"][0:64, :N])
                P.act(W["sg"][:, :N], pwp[:, :N], AF.Sigmoid, bias=cols[:, 3:4])
                P.mm(pap[:, :N], lup[64:128, :], W["lol"][64:128, :N])
                P.act(W["ag"][:, :N], pap[:, :N], AF.Sigmoid, bias=cols[:, 4:5])
                P.act(W["sgl"][:, :N], W["gll"][:, :N], AF.Sigmoid)
                P.ts(W["kks"][:, :N], k_[:, :N], cols[:, 5:6], ALU.mult)
                P.tt(W["sq"][:, :N], W["kks"][:, :N], W["kks"][:, :N], ALU.mult)
                P.mm(pss[:, :N], K.blk64[:], W["sq"][:, :N])
                P.ts(W["rn"][:, :N], pss[:, :N], 1e-24, ALU.max)
                P.act(W["rn"][:, :N], W["rn"][:, :N], AF.Sqrt)
                P.recip(W["rn"][:, :N], W["rn"][:, :N])
                P.tt(W["kkn"][:, :N], W["kks"][:, :N], W["rn"][:, :N], ALU.mult)
                P.tt(W["bv"][:, :N], W["kkn"][:, :N], W["ag"][:, :N], ALU.mult)
                P.ts(W["kd"][:, :N], W["ag"][:, :N], -1.0, ALU.add, cols[:, 6:7], ALU.mult)
                P.stt(W["kd"][:, :N], W["kd"][:, :N], 1.0, k_[:, :N], ALU.add, ALU.mult)
                sg = W["sg"]
                P.scan(W["cs"][:, :N], reset[:, :N], sg[:, :N], 0.0, ALU.mult, ALU.add)
                P.copy(tot[:], W["cs"][:, :N].re("p (c t) -> p c t", t=64)[:, :, 63])
                P.copy(W["csd"][0:64, :N], W["cs"][0:64, :N])
                P.tt(W["tmp"][64:128, :N], sg[64:128, :N], W["cs"][64:128, :N], ALU.subtract)
                for c in range(NCT):
                    P.ts(W["csd"][64:128, c * 64:(c + 1) * 64], W["tmp"][64:128, c * 64:(c + 1) * 64],
                         tot[64:128, c:c + 1], ALU.add)
                csd = W["csd"]
                P.tt(W["tmp"][:, :N], csd[:, :N], sg[:, :N], ALU.subtract)
                P.act(W["e1"][:, :N], W["tmp"][:, :N], AF.Exp, scale=-C0)
                P.act(W["e2"][:, :N], csd[:, :N], AF.Exp, scale=C0)
                P.act(W["e3"][:, :N], csd[:, :N], AF.Exp, scale=-C0)
                P.act(e3t[:], tot[:], AF.Exp, scale=-C0)
                P.stt(W["At"][:, :N], W["kkn"][:, :N], -1.0, W["e1"][:, :N], ALU.mult, ALU.mult)
                P.tt(W["Bt"][:, :N], W["bv"][:, :N], W["e2"][:, :N], ALU.mult)
                P.tt(W["Kt"][:, :N], W["kd"][:, :N], W["e2"][:, :N], ALU.mult)
                P.tt(W["Rt"][:, :N], r_[:, :N], W["e3"][:, :N], ALU.mult, eng="pool")
                for c in range(NCT):
                    cs_ = slice(c * 64, (c + 1) * 64)
                    P.ts(W["Bh"][:, cs_], W["Bt"][:, cs_], e3t[:, c:c + 1], ALU.mult, eng="pool")
                    P.ts(W["Kh"][:, cs_], W["Kt"][:, cs_], e3t[:, c:c + 1], ALU.mult, eng="pool")
                P.stt(W["rkk"][:, :N], W["kd"][:, :N], cols[:, 7:8], r_[:, :N], ALU.mult, ALU.mult)

                import os
                RWS = int(os.environ.get("RW_SUB", "9"))
                if RWS < 1:
                    continue
                CU = [(c, u) for c in range(NCT) for u in range(2)]

                for pi_, nm in enumerate(PAD):
                    P.copy(PAD[nm][64:128, :N], W[nm][64:128, :N], eng=("act" if pi_ % 2 else "pool"))

                def fm(nm, c, u):
                    return W[nm][u * 64:(u + 1) * 64, c * 64:(c + 1) * 64]

                def fml(nm, c, u):
                    if u == 0:
                        return W[nm][0:64, c * 64:(c + 1) * 64]
                    return PAD[nm][:, c * 64:(c + 1) * 64]

                def fmr(nm, c, u):
                    if u == 0:
                        return W[nm][0:64, c * 64:(c + 1) * 64]
                    return W[nm][:, c * 64:(c + 1) * 64]

                idr = (K.ident[0:64, 0:64], K.ident[:, 64:128])

                def bview(b):
                    return b[0:64, :].re("p (i t) -> p i t", t=64)

                def mm_mask(dst, lname, rname, mk):
                    b = bank()
                    for i, (c, u) in enumerate(CU):
                        P.mm(b[0:64, i * 64:(i + 1) * 64], fml(lname, c, u), fmr(rname, c, u), nowaw=(i > 0))
                    P.tt(dst, bview(b)[:, :NSET, :], MK[mk][:], ALU.mult)

                mm_mask(CT["L0"][:], "At", "Bt", "L")
                mm_mask(CT["N0"][:], "Bt", "At", "N")
                mm_mask(CT["LakT"][:], "Kt", "At", "N")
                mm_mask(pack[:, :, 5, :], "Bt", "Rt", "NI")
                mm_mask(pack[:, :, 6, :], "Kt", "Rt", "NI")
                if RWS < 2:
                    continue
                for nm, dst in (("At", CT["Atm"][:]), ("v", pack[:, :, 4, :]), ("Bh", pack[:, :, 2, :]), ("Kh", pack[:, :, 3, :])):
                    b = bank()
                    for i, (c, u) in enumerate(CU):
                        P.mm(b[0:64, i * 64:(i + 1) * 64], fml(nm, c, u), idr[u], nowaw=(i > 0))
                    P.copy(dst, bview(b)[:, :NSET, :], eng="act")
                if RWS < 3:
                    continue
                P.tt(CT["Q"][:], CT["N0"][:], MK["I"][:], ALU.add, eng="pool")
                cur = 0
                for lvl in range(1, 6):
                    nxt = 1 - cur
                    b = bank()
                    for i in range(NSET):
                        P.mm(b[0:64, i * 64:(i + 1) * 64], CT[f"N{cur}"][:, i, :], CT[f"L{cur}"][:, i, :], nowaw=(i > 0))
                    P.copy(CT[f"L{nxt}"][:], bview(b)[:, :NSET, :], eng="act")
                    if lvl < 5:
                        b = bank()
                        for i in range(NSET):
                            P.mm(b[0:64, i * 64:(i + 1) * 64], CT[f"L{cur}"][:, i, :], CT[f"N{cur}"][:, i, :], nowaw=(i > 0))
                        P.copy(CT[f"N{nxt}"][:], bview(b)[:, :NSET, :], eng="act")
                    b = bank()
                    for i in range(NSET):
                        P.mm(b[0:64, i * 64:(i + 1) * 64], CT[f"L{nxt}"][:, i, :], CT["Q"][:, i, :], nowaw=(i > 0))
                    P.tt(CT["Q"][:], CT["Q"][:], bview(b)[:, :NSET, :], ALU.add)
                    cur = nxt
                if RWS < 4:
                    continue
                b = bank()
                for i in range(NSET):
                    P.mm(b[0:64, i * 64:(i + 1) * 64], CT["LakT"][:, i, :], pack[:, i, 4, :], nowaw=(i > 0))
                P.copy(CT["Wsb"][:], bview(b)[:, :NSET, :], eng="act")
                b = bank()
                for i in range(NSET):
                    P.mm(b[0:64, i * 64:(i + 1) * 64], CT["Q"][:, i, :], CT["Wsb"][:, i, :], nowaw=(i > 0))
                P.copy(pack[:, :, 1, :], bview(b)[:, :NSET, :], eng="act")
                b = bank()
                for i in range(NSET):
                    P.mm(b[0:64, i * 64:(i + 1) * 64], CT["Atm"][:, i, :], CT["Q"][:, i, :], nowaw=(i > 0))
                P.copy(pack[:, :, 0, :], bview(b)[:, :NSET, :])
                if RWS < 5:
                    continue
                b = bank()
                for c in range(NCT):
                    P.mm(b[0:64, c * 64:(c + 1) * 64], W["rkk"][:, c * 64:(c + 1) * 64], K.ones[:, 0:64])
                for c in range(NCT):
                    P.tt(pack2[:, c, 0, :], b[0:64, c * 64:(c + 1) * 64], pack[:, 2 * c, 4, :], ALU.mult)
                b = bank()
                for c in range(NCT):
                    P.mm(b[0:64, c * 64:(c + 1) * 64], W["sgl"][:, c * 64:(c + 1) * 64], gup[:])
                P.copy(pack2[:, :, 1, :], bview(b)[:, :NCT, :], eng="act")
                if RWS < 6:
                    continue
                for i, (c, u) in enumerate(CU):
                    n = ch0 + c
                    P.dma(scr[n, u, :, 0:7, :], pack[:, i, 0:7, :], eng="pool")
                    P.dma(scr[n, u, :, 7, :], fm("Rt", c, u), eng="pool")
                    P.dma(scrpc[n, u, :, :], e3t[u * 64:(u + 1) * 64, c:c + 1], eng="pool", allow_slow_non_contiguous=True)
                for c in range(NCT):
                    P.dma(scr2[ch0 + c, :, :, :], pack2[:, c, :, :], eng="pool")

        import os
        RWP = int(os.environ.get("RW_PHASE", "3"))
        if RWP < 2:
            return
        with P.scope():
            pUb = [P.ps(f"pU{u}", [128, 512])[0:64, 0:64] for u in range(2)]
            pYb = [P.ps(f"pY{u}", [128, 512])[0:64, 0:64] for u in range(2)]
            pSb = [P.ps(f"pS{u}", [128, 512])[0:64, 0:64] for u in range(2)]
            St = [P.sb(f"St{u}", [64, 64]) for u in range(2)]
            for u in range(2):
                P.memset(St[u][:], 0.0)
            NB = 4
            ops = [[P.sb(f"ops{u}_{i}", [64, 8, 64]) for i in range(NB)] for u in range(2)]
            pcs = [[P.sb(f"pc{u}_{i}", [64, 1]) for i in range(NB)] for u in range(2)]
            usb = [[P.sb(f"usb{u}_{i}", [64, 64]) for i in range(2)] for u in range(2)]
            ysb = [[P.sb(f"ysb{u}_{i}", [64, 64]) for i in range(NB)] for u in range(2)]
            order = [list(range(NCH)),
                     list(range(NCC - 1, -1, -1)) + list(range(NCH - 1, NCC - 1, -1))]
            for i in range(NCH):
                for u in range(2):
                    n = order[u][i]
                    o = ops[u][i % NB]; pc = pcs[u][i % NB]
                    P.dma(o[:], scr[n, u, :, :, :])
                    P.dma(pc[:], scrpc[n, u, :, :], allow_slow_non_contiguous=True)
                    U = usb[u][i % 2]
                    pU = pUb[u]
                    P.mm(pU[:], o[:, 0, :], St[u][:])
                    P.tt(U[:], pU[:], o[:, 1, :], ALU.add)
                    pY = pYb[u]
                    P.mm(pY[:], o[:, 7, :], St[u][:], start=True, stop=False)
                    P.mm(pY[:], o[:, 5, :], U[:], start=False, stop=False)
                    P.mm(pY[:], o[:, 6, :], o[:, 4, :], start=False, stop=True)
                    y = ysb[u][i % NB]
                    P.copy(y[:], pY[:], eng="act")
                    P.dma(yscr[n, u, :, :], y[:], eng="pool")
                    pS = pSb[u]
                    P.mm(pS[:], o[:, 2, :], U[:], start=True, stop=False)
                    P.mm(pS[:], o[:, 3, :], o[:, 4, :], start=False, stop=True)
                    P.stt(St[u][:], St[u][:], pc[:, 0:1], pS[:], ALU.mult, ALU.add)

        if RWP < 3:
            return
        with P.scope():
            bk = [P.ps(f"b3k{i}", [128, 512]) for i in range(2)]
            yf = [P.sb(f"yf{i}", [64, NCT, 64]) for i in range(2)]
            yb = [P.sb(f"yb{i}", [64, NCT, 64]) for i in range(2)]
            p2 = [P.sb(f"p2_{i}", [64, 2, NCT, 64]) for i in range(2)]
            y = P.sb("y3", [64, NCT, 64]); sq = P.sb("sq3", [64, NCT, 64])
            s1 = P.sb("s1", [64, NCT]); s2 = P.sb("s2", [64, NCT])
            ofm = [P.sb(f"ofm{i}", [64, N]) for i in range(2)]
            for ti, (kind, ch0, info) in enumerate(tiles()):
                a, b, q = yf[ti % 2], yb[ti % 2], p2[ti % 2]
                P.dma(a[:], yscr[ch0:ch0 + NCT, 0, :, :].re("c t v -> t c v"))
                P.dma(b[:], yscr[ch0:ch0 + NCT, 1, :, :].re("c t v -> t c v"))
                for it_ in range(2):
                    P.dma(q[:, it_, :, :], scr2[ch0:ch0 + NCT, :, it_, :].re("c t v -> t c v"))
                P.tt(y[:], a[:], b[:], ALU.add)
                P.reduce(s1[:], y[:])
                P.ts(s1[:], s1[:], -1.0 / 64, ALU.mult)
                for c in range(NCT):
                    P.ts(y[:, c, :], y[:, c, :], s1[:, c:c + 1], ALU.add)
                P.tt(sq[:], y[:], y[:], ALU.mult, eng="pool")
                P.reduce(s2[:], sq[:])
                P.ts(s2[:], s2[:], 1.0 / 64, ALU.mult, GN_EPS, ALU.add)
                P.act(s2[:], s2[:], AF.Sqrt)
                P.recip(s2[:], s2[:])
                for c in range(NCT):
                    P.ts(y[:, c, :], y[:, c, :], s2[:, c:c + 1], ALU.mult)
                P.tt(y[:], y[:], gn[:, 0, :, :], ALU.mult)
                P.tt(y[:], y[:], gn[:, 1, :, :], ALU.add)
                P.tt(y[:], y[:], q[:, 0, :, :], ALU.add)
                P.tt(y[:], y[:], q[:, 1, :, :], ALU.mult)
                pp = bk[ti % 2]
                for c in range(NCT):
                    P.mm(pp[0:64, c * 64:(c + 1) * 64], y[:, c, :], K.ident[0:64, 0:64])
                of = ofm[ti % 2]
                P.copy(of[:, :N], pp[0:64, :N], eng="act")
                if kind == "ctx":
                    P.dma(bufs["rwo"].loc(0, 64, cfg.L, cfg.CTX), of[:, :cfg.CTX], eng="pool")
                else:
                    r, c0 = info
                    P.dma(bufs["rwo"].loc(0, 64, r * cfg.ntl + c0, N), of[:, :N], eng="pool")


def stage_ada(P, cfg, K, dr, wfull, depth):
    mods = [P.sb(f"mod{l}", [128, 2, 9, 8]) for l in range(depth)]
    with P.scope():
        cond = P.sb("cond", [128, 8, 2]); P.dma(cond[:], dr["cond"][:])
        P.act(cond[:], cond[:], AF.Silu)
        bada = P.sb("bada", [128, depth, 72]); P.dma(bada[:], dr["bada"][:])
        wA = [P.sb(f"wada{i}", [128, NK, 256]) for i in range(2)]
        pp = [P.ps(f"pada{i}", [128, 2]) for i in range(4)]
        cnt = 0
        for l in range(depth):
            for blk in range(36):
                half, b2 = divmod(blk, 18)
                wa = wA[blk % 2]
                P.dma(wa[:], wfull[l]["w_ada"][half][:, b2 * 256:(b2 + 1) * 256].re("(k p) f -> p k f", p=128))
                for c in range(2):
                    cc = blk * 2 + c
                    j, k = divmod(cc, 8)
                    ps = pp[cnt % 4]; cnt += 1
                    for dk in range(NK):
                        P.mm(ps[:], wa[:, dk, c * 128:(c + 1) * 128], cond[:, dk, :], start=(dk == 0), stop=(dk == NK - 1))
                    P.ts(mods[l][:, :, j, k], ps[:], bada[:, l, cc:cc + 1], ALU.add)
    return mods


WSHAPES = {
    "w1a": (D, DFF), "w3a": (D, DFF), "w2a": (DFF, D), "w1b": (D, DFF), "w3b": (D, DFF), "w2b": (DFF, D),
    "w_in": (D, INCOLS), "w_out": (D, D), "w_br_s5": (256, D), "w_br_att": (512, D), "w_br_rw": (256, D),
    "w_glu": (256, 256), "w_ada0": (D, 4608), "w_ada1": (D, 4608),
}
SMALL = {
    "normw": [128, 3, 8], "qkw": [128, 2], "b_glu": [128, 2],
    "s5_lam": [128, 3, 4], "s5_bL": [128, 2, 4, 256], "s5_cR": [128, 2, 4, 64], "s5_D": [128, 2, 64],
    "rw_sel": [128, 2, 128], "rw_cols": [128, 10], "rw_lup": [128, 128], "rw_gup": [128, 64],
}
ALL8 = [list(range(8))]
GRP4 = [[0, 1, 2, 3], [4, 5, 6, 7]]


def build(cfg, depth=4, debug=(), stages=("a", "gqa", "s5", "rwkv", "b")):
    nc = bass.Bass("TRN2", target_bir_lowering=False)
    P = Prog(nc)
    NT, TS = cfg.nt, cfg.TSEQ
    NCH = TS // 64
    NCT = (256 if cfg.ntl >= 256 else cfg.ntl) // 64
    dr = {}
    for nm, shp in (("xT", [D, NT]), ("cond", [128, 8, 2]), ("ropec", [64, cfg.ntl]), ("ropes", [64, cfg.ntl]),
                    ("selI", [128, 4, 128]), ("c_ident", [128, 128]), ("c_rotm", [128, 128]), ("c_blk64", [128, 128]),
                    ("c_jt", [128, 4, 9]), ("c_masks", [64, 4, 64]), ("c_reset", [128, 512]),
                    ("bada", [128, depth, 72])):
        dr[nm] = P.dram(nm, shp)
    for nm, shp in SMALL.items():
        dr[nm] = P.dram(nm, [depth] + shp)
    dr["rw_gn"] = P.dram("rw_gn", [depth, 64, 2, NCT, 64])
    out = P.dram("out", [D, NT], kind="ExternalOutput")
    dbg = {}

    def dbg_out(name, buf):
        if name in debug:
            shp = list(buf.t.shape)
            d_ = P.dram("dbg_" + name, shp, kind="ExternalOutput")
            P.dma(d_[:], buf[:])
            dbg[name] = d_

    def internal(name, shape):
        t = nc.dram_tensor(name, list(shape), F32).ap()
        return Buf(name, t, is_dram=True)

    def allgather(out_b, in_b, groups):
        P.allgather(out_b[:], in_b[:], groups)

    wfull = []
    for l in range(depth):
        wl = {}
        import os
        only = os.environ.get("WONLY")
        for nm, (R, C) in WSHAPES.items():
            if only and nm not in only.split(","):
                wl[nm] = None
                continue
            full = P.dram(f"{nm}_f{l}", [R, C])
            wl[nm] = full
        wl["w_ada"] = [wl["w_ada0"], wl["w_ada1"]]
        wfull.append(wl)

    K = Consts(P, dr)
    if "noada" in stages:
        mods = [P.sb(f"mod{l}", [128, 2, 9, 8]) for l in range(depth)]
        for l in range(depth):
            P.memset(mods[l][:], 0.0)
    else:
        mods = stage_ada(P, cfg, K, dr, wfull, depth)
    if "mods" in debug:
        d_ = P.dram("dbg_mods", [128, 2, 9, 8], kind="ExternalOutput")
        P.dma(d_[:], mods[0][:])
    nws = P.sb("nws", [128, depth, 3, 8]); P.dma(nws[:], dr["normw"].re("l p i k -> p l i k"))

    bufs = {"xa": internal("xa", [D, NT])}
    P.dma(bufs["xa"][:], dr["xT"][:])
    ntl, ncx = cfg.ntl, cfg.ncx
    bufs["gu"] = Pieced(nc, "gu", 256, 256, col_bounds(cfg, 1024, ntl, ncx))
    bufs["gk"] = Pieced(nc, "gk", 128, 128, col_bounds(cfg, 2048, ntl, ncx))
    bufs["gf"] = Pieced(nc, "gf", 1024, 256, col_bounds(cfg, 1024, ntl, ncx))
    bufs["gv"] = PiecedT(nc, "gv", 128, col_bounds(cfg, 2048, ntl, ncx))
    seqb = [(r * ntl, (r + 1) * ntl) for r in range(4)] + [(cfg.L, cfg.TSEQ)]
    bufs["s5o"] = Pieced(nc, "s5o", 64, 64, seqb)
    bufs["rwo"] = Pieced(nc, "rwo", 64, 64, seqb)
    for nm, shp in (("qb", [512, NT]), ("sgate", [3072, NT]), ("yatt", [512, NT]),
                    ("rw_scr", [NCH, 2, 64, 8, 64]), ("rw_pc", [NCH, 2, 64, 1]), ("rw_scr2", [NCH, 64, 2, 64]),
                    ("rw_y", [NCH, 2, 64, 64])):
        bufs[nm] = internal(nm, shp)

    for l in range(depth):
        lw = dict(wfull[l])
        for nm in SMALL:
            lw[nm] = View(dr[nm], dr[nm].t[l])
        lw["rw_gn"] = View(dr["rw_gn"], dr["rw_gn"].t[l])
        nw = nws[:, l, :, :]
        if "a" in stages:
            stage_a(P, cfg, K, mods[l], nw, dr, lw, bufs)
        dbg_out(f"xa_a{l}", bufs["xa"]); dbg_out(f"gu{l}", bufs["gu"].loc_t[(0, 0)]); dbg_out(f"gk{l}", bufs["gk"].loc_t[(0, 0)])
        dbg_out(f"gf{l}", bufs["gf"].loc_t[(1, 0)]); dbg_out(f"gv{l}", bufs["gv"].loc_t[0]); dbg_out(f"qb{l}", bufs["qb"])
        dbg_out(f"sgate{l}", bufs["sgate"])
        for nm in ("gu", "gk", "gf", "gv"):
            bufs[nm].gather(P)
        if "gqa" in stages:
            stage_gqa(P, cfg, K, bufs)
        dbg_out(f"yatt{l}", bufs["yatt"])
        if "s5" in stages:
            stage_s5(P, cfg, K, dr, lw, bufs, l)
        dbg_out(f"s5o{l}", bufs["s5o"].loc_t[(0, 0)])
        if "rwkv" in stages:
            stage_rwkv(P, cfg, K, dr, lw, bufs)
        dbg_out(f"rwo{l}", bufs["rwo"].loc_t[(0, 0)])
        bufs["s5o"].gather(P)
        bufs["rwo"].gather(P)
        dst = out if l == depth - 1 else bufs["xa"]
        if "b" in stages:
            stage_b(P, cfg, K, mods[l], nw, dr, lw, bufs, dst)
        else:
            P.dma(out[:], bufs["xa"][:])
    P.emit()
    return nc


S5_STATE = 64


def consts_np():
    ident = np.eye(128, dtype=np.float32)
    rotm = np.zeros((128, 128), np.float32)
    for blk in range(2):
        o = blk * 64
        for i in range(32):
            rotm[o + i + 32, o + i] = -1.0
            rotm[o + i, o + i + 32] = 1.0
    blk64 = np.zeros((128, 128), np.float32)
    blk64[:64, :64] = 1.0
    blk64[64:, 64:] = 1.0
    jt = np.broadcast_to(np.arange(9, dtype=np.float32), (128, 4, 9)).copy()
    r = np.arange(64)[:, None]
    c = np.arange(64)[None, :]
    masks = np.stack([(c < r), (c > r), (c <= r), (c >= r)], axis=1).astype(np.float32)
    reset = np.ones((128, 512), np.float32)
    reset[:, ::64] = 0.0
    return {"c_ident": ident, "c_rotm": rotm, "c_blk64": blk64, "c_jt": jt, "c_masks": masks, "c_reset": reset}


def pcol(v):
    return np.ascontiguousarray(v.reshape(-1, 128).T)


def host_prep(inp, cfg, depth):
    f32 = np.float32
    inp = {k: np.asarray(v, dtype=f32) for k, v in inp.items()}
    C = consts_np()
    ntl, ncx, NT = cfg.ntl, cfg.ncx, cfg.nt
    NCT = (256 if ntl >= 256 else ntl) // 64
    maps = []
    fulls = []
    for l in range(depth):
        fulls.append({
            "w1a": inp["ffn_w1"][l, 0], "w3a": inp["ffn_w3"][l, 0], "w2a": inp["ffn_w2"][l, 0],
            "w1b": inp["ffn_w1"][l, 1], "w3b": inp["ffn_w3"][l, 1], "w2b": inp["ffn_w2"][l, 1],
            "w_in": inp["w_in"][l], "w_out": inp["w_out"][l], "w_br_s5": inp["w_br_s5"][l],
            "w_br_att": inp["w_br_att"][l], "w_br_rw": inp["w_br_rwkv"][l], "w_glu": inp["s5_w_glu"][l],
            "w_ada0": np.ascontiguousarray(inp["w_ada"][l][:, :4608]), "w_ada1": np.ascontiguousarray(inp["w_ada"][l][:, 4608:]),
        })
    half = 32
    inv_freq = (10000.0 ** (-np.arange(0, half, 2, dtype=f32) / half)).astype(f32)
    for core in range(8):
        b, q = divmod(core, 4)
        m = dict(C)
        xs = inp["x"][b, q * ntl:(q + 1) * ntl, :]
        cs = inp["ctx"][b, q * ncx:(q + 1) * ncx, :]
        m["xT"] = np.ascontiguousarray(np.concatenate([xs, cs], axis=0).T)
        m["cond"] = np.ascontiguousarray(np.stack([pcol(inp["c"][b]), pcol(inp["c_ctx"])], axis=-1))
        t = np.arange(q * ntl, (q + 1) * ntl)
        row = (t // 64).astype(f32)
        col = (t % 64).astype(f32)
        ang = np.concatenate([row[:, None] * inv_freq, col[:, None] * inv_freq], axis=-1)
        ang = np.concatenate([ang, ang], axis=-1)
        m["ropec"] = np.ascontiguousarray(np.cos(ang).T.astype(f32))
        m["ropes"] = np.ascontiguousarray(np.sin(ang).T.astype(f32))
        sel = np.zeros((128, 4, 128), f32)
        sel[:, q, :] = np.eye(128, dtype=f32)
        m["selI"] = sel
        m["bada"] = np.ascontiguousarray(np.stack([pcol(inp["b_ada"][l]) for l in range(depth)], axis=1))
        sm = {k: np.zeros([depth] + v, f32) for k, v in SMALL.items()}
        gn = np.zeros((depth, 64, 2, NCT, 64), f32)
        for l in range(depth):
            sm["normw"][l] = np.stack([pcol(inp["norm_w"][l, i]) for i in range(3)], axis=1)
            sm["qkw"][l, :, 0] = np.tile(inp["q_norm_w"][l], 2)
            sm["qkw"][l, :, 1] = np.tile(inp["k_norm_w"][l], 2)
            sm["b_glu"][l] = pcol(inp["s5_b_glu"][l])
            for d in range(2):
                for pr in range(2):
                    c = d * 2 + pr
                    for gi in range(2):
                        g = 4 * q + 2 * pr + gi
                        rows = slice(gi * 64, (gi + 1) * 64)
                        sm["s5_lam"][l, rows, 0, c] = inp["s5_lam_re"][l, d, g]
                        sm["s5_lam"][l, rows, 1, c] = inp["s5_lam_im"][l, d, g]
                        sm["s5_lam"][l, rows, 2, c] = inp["s5_log_dt"][l, d, g]
                        sm["s5_bL"][l, rows, 0, c, g * 16:(g + 1) * 16] = inp["s5_b_re"][l, d, g]
                        sm["s5_bL"][l, rows, 1, c, g * 16:(g + 1) * 16] = inp["s5_b_im"][l, d, g]
                        gl = 2 * pr + gi
                        sm["s5_cR"][l, rows, 0, c, gl * 16:(gl + 1) * 16] = inp["s5_c_re"][l, d, g].T
                        sm["s5_cR"][l, rows, 1, c, gl * 16:(gl + 1) * 16] = inp["s5_c_im"][l, d, g].T
            Dm = np.zeros((256, 64), f32)
            for gl in range(4):
                for h in range(16):
                    f = (4 * q + gl) * 16 + h
                    Dm[f, gl * 16 + h] = inp["s5_d"][l, f]
            sm["s5_D"][l] = Dm.reshape(2, 128, 64).transpose(1, 0, 2)
            hc = slice(q * 64, (q + 1) * 64)
            selr = np.zeros((256, 128), f32)
            for mm_ in range(128):
                selr[q * 64 + (mm_ % 64), mm_] = 1.0
            sm["rw_sel"][l] = selr.reshape(2, 128, 128).transpose(1, 0, 2)
            mu = inp["rwkv_mu"][l]
            cols = np.zeros((128, 10), f32)
            for u in range(2):
                rs = slice(u * 64, (u + 1) * 64)
                cols[rs, 0] = mu[0:256][hc]
                cols[rs, 1] = mu[256:512][hc]
                cols[rs, 2] = mu[512:768][hc]
                cols[rs, 3] = inp["rwkv_w0"][l, u, hc]
                cols[rs, 4] = inp["rwkv_a0"][l, u, hc]
                cols[rs, 5] = inp["rwkv_k_k"][l, hc]
                cols[rs, 6] = inp["rwkv_k_a"][l, hc]
                cols[rs, 7] = inp["rwkv_r_k"][l, q]
                sm["rw_lup"][l, 0:64, rs] = inp["rwkv_w_up"][l, u][:, hc]
                sm["rw_lup"][l, 64:128, rs] = inp["rwkv_a_up"][l, u][:, hc]
            cols[0:64, 8] = mu[768:832]
            cols[64:128, 8] = mu[832:896]
            cols[:, 9] = mu[896:1024]
            sm["rw_cols"][l] = cols
            sm["rw_gup"][l] = inp["rwkv_g_up"][l][:, hc]
            gn[l, :, 0, :, :] = inp["rwkv_gn_w"][l, hc][None, None, :]
            gn[l, :, 1, :, :] = inp["rwkv_gn_b"][l, hc][None, None, :]
            for nm, w in fulls[l].items():
                m[f"{nm}_f{l}"] = np.ascontiguousarray(w)
        m.update(sm)
        m["rw_gn"] = gn
        maps.append(m)
    return maps


def assemble(results, cfg, B=2):
    ntl = cfg.ntl
    out = np.zeros((B, 4 * ntl, D), np.float32)
    for core in range(8):
        b, q = divmod(core, 4)
        out[b, q * ntl:(q + 1) * ntl, :] = results[core]["out"][:, :ntl].T
    return out


def kernel(**inputs):
    cfg = Cfg(4096, 64)
    depth = 4
    nc = build(cfg, depth)
    maps = host_prep(inputs, cfg, depth)
    res = run_spmd(nc, maps)
    return assemble(res, cfg)
```

```python
import contextlib
import numpy as np
import concourse.bass as bass
import concourse.mybir as mybir
from concourse.bass_utils import run_bass_kernel_spmd

F32 = mybir.dt.float32
AF = mybir.ActivationFunctionType
ALU = mybir.AluOpType
AX = mybir.AxisListType

EPOCH = 10**9
ENGS = ("pe", "act", "dve", "pool", "sp")


class View:
    __slots__ = ("buf", "ap")

    def __init__(self, buf, ap):
        self.buf = buf
        self.ap = ap

    def __getitem__(self, k):
        return View(self.buf, self.ap[k])

    def re(self, pat_, **kw):
        return View(self.buf, self.ap.rearrange(pat_, **kw))

    def bc(self, shape):
        return View(self.buf, self.ap.to_broadcast(list(shape)))


class Buf:
    __slots__ = ("name", "t", "last_w", "readers", "dsem", "dsem_sw", "is_dram", "is_ap")

    def __init__(self, name, t, is_dram=False, is_ap=False):
        self.name = name
        self.t = t
        self.is_ap = is_ap or is_dram
        self.last_w = None
        self.readers = []
        self.dsem = None
        self.dsem_sw = None
        self.is_dram = is_dram

    def __getitem__(self, k):
        return View(self, self.t[k])

    def re(self, pat_, **kw):
        return View(self, self.t.rearrange(pat_, **kw) if self.is_ap else self.t[:].rearrange(pat_, **kw))


class Op:
    __slots__ = ("eng", "fn", "waits", "sem", "val", "inc", "idx", "dref")


def _ap(x):
    return x.ap if isinstance(x, View) else x


def _bufs(*xs):
    return [x.buf for x in xs if isinstance(x, View)]


class Prog:
    def __init__(self, nc):
        self.nc = nc
        self.ops = []
        self.stack = contextlib.ExitStack()
        self.eng_count = {e: 0 for e in ENGS}
        self.eng_sems = {e: [] for e in ENGS}
        self.waited = {}
        self.nsem = 0
        self.dma_rr = 0
        self.scopes = []
        self.pending = {e: {} for e in ENGS}
        self.all_dsems = []
        self.free_dsems = {False: [], True: []}
        self.scope_dsems = [[]]

    def _stk(self):
        return self.scopes[-1] if self.scopes else self.stack

    def sb(self, name, shape, dt=F32):
        self.nsem += 1
        t = self._stk().enter_context(self.nc.sbuf_tensor(f"sb_{name}_{self.nsem}", list(shape), dt))
        return Buf(name, t)

    def ps(self, name, shape, dt=F32):
        self.nsem += 1
        t = self._stk().enter_context(self.nc.psum_tensor(f"ps_{name}_{self.nsem}", list(shape), dt))
        return Buf(name, t)

    @contextlib.contextmanager
    def scope(self):
        st = contextlib.ExitStack()
        self.scopes.append(st)
        self.scope_dsems.append([])
        try:
            yield
        finally:
            self.scopes.pop()
            self.barrier()
            st.close()
            for d, sw in self.scope_dsems.pop():
                self.free_dsems[sw].append(d)

    def barrier(self):
        tot = {}
        for e in ENGS:
            c = self.eng_count[e]
            if c:
                sem = self.eng_sems[e][(c - 1) // EPOCH]
                tot[sem.name] = (sem, ((c - 1) % EPOCH) + 1)
        for d in self.all_dsems:
            if d[1]:
                tot[d[0].name] = (d[0], d[1])
        for e in ENGS:
            self.pending[e].update(tot)

    def dram(self, name, shape, dt=F32, kind="ExternalInput"):
        t = self.nc.dram_tensor(name, list(shape), dt, kind=kind).ap()
        return Buf(name, t, is_dram=True)

    def sub(self, view, name="sub"):
        self.nsem += 1
        return Buf(f"{name}_{self.nsem}", view.ap, is_ap=True)

    def _get_dsem(self, sb, sw):
        if self.free_dsems[sw]:
            d = self.free_dsems[sw].pop()
        else:
            d = [self._newsem(f"d{'s' if sw else 'h'}_{self.nsem}"), 0]
            self.all_dsems.append(d)
        if sb.is_dram or not self.scopes:
            pass
        else:
            self.scope_dsems[-1].append((d, sw))
        return d

    def _newsem(self, name):
        self.nsem += 1
        return self.stack.enter_context(self.nc.semaphore(name))

    def _deps(self, reads, writes):
        deps = set()
        for r in reads:
            if r.last_w is not None:
                deps.add(r.last_w)
        for w in writes:
            if w.last_w is not None:
                deps.add(w.last_w)
            deps.update(w.readers)
        return deps

    def _commit(self, idx, reads, writes):
        for r in reads:
            if r not in writes:
                r.readers.append(idx)
        for w in writes:
            w.last_w = idx
            w.readers = []

    def op(self, eng, fn, reads=(), writes=(), accum=False):
        o = Op()
        o.idx = len(self.ops)
        o.eng = eng
        o.fn = fn
        deps = self._deps(reads, writes)
        o.waits = self._mk_waits(eng, deps, accum)
        c = self.eng_count[eng]
        ep = c // EPOCH
        sems = self.eng_sems[eng]
        while len(sems) <= ep:
            sems.append(self._newsem(f"s_{eng}_{len(sems)}"))
        o.sem = sems[ep]
        o.val = (c % EPOCH) + 1
        o.inc = 1
        o.dref = None
        self.eng_count[eng] = c + 1
        self.ops.append(o)
        self._commit(o.idx, reads, writes)
        return o

    def dma(self, out, in_, eng=None, **kw):
        if eng is None:
            eng = "sp"
        o = Op()
        o.idx = len(self.ops)
        o.eng = eng
        oa, ia = out.ap, in_.ap
        o.fn = lambda e: e.dma_start(out=oa, in_=ia, **kw)
        reads = [in_.buf]
        writes = [out.buf]
        deps = self._deps(reads, writes)
        o.waits = self._mk_waits(eng, deps, False)
        sb = out.buf if not out.buf.is_dram else in_.buf
        sw = (eng == "pool")
        if sb.is_dram:
            if getattr(self, "dd_sem", None) is None:
                self.dd_sem = [self._newsem("dd_sem"), 0]
                self.all_dsems.append(self.dd_sem)
            dsem = self.dd_sem
        elif sw:
            if sb.dsem_sw is None:
                sb.dsem_sw = self._get_dsem(sb, True)
            dsem = sb.dsem_sw
        else:
            if sb.dsem is None:
                sb.dsem = self._get_dsem(sb, False)
            dsem = sb.dsem
        dsem[1] += 16
        o.sem = dsem[0]
        o.val = dsem[1]
        o.dref = dsem
        o.inc = 16
        self.ops.append(o)
        self._commit(o.idx, reads, writes)
        return o

    def allgather(self, out, in_, groups=None):
        groups = groups or [list(range(8))]
        oa, ia = out.ap, in_.ap
        if getattr(self, "cc_sem", None) is None:
            self.cc_sem = self._newsem("cc_sem")
        ccs = self.cc_sem

        def fn(e):
            e.collective_compute("AllGather", ALU.bypass, replica_groups=groups,
                                 ins=[ia.opt()], outs=[oa.opt()]).then_inc(ccs)
            e.wait_ge(ccs, 1)
            e.sem_clear(ccs)
            return e.nop()

        return self.op("pool", fn, reads=[in_.buf], writes=[out.buf])

    def _mk_waits(self, eng, deps, accum):
        waits = {}
        for d in deps:
            p = self.ops[d]
            if accum and p.eng == "pe" and eng == "pe" and p.dref is None:
                continue
            sem, val = p.sem, p.val
            if p.dref is not None:
                val = p.dref[1]
            key = sem.name
            if waits.get(key, (None, 0))[1] < val:
                waits[key] = (sem, val)
        if self.pending[eng]:
            for key, (sem, val) in self.pending[eng].items():
                if waits.get(key, (None, 0))[1] < val:
                    waits[key] = (sem, val)
            self.pending[eng] = {}
        out = []
        for key, (sem, val) in waits.items():
            k = (eng, key)
            if self.waited.get(k, 0) >= val:
                continue
            self.waited[k] = val
            out.append((sem, val))
        return out

    def mm(self, out, lhsT, rhs, start=True, stop=True, nowaw=False):
        oa, la, ra = out.ap, lhsT.ap, rhs.ap
        return self.op("pe", lambda e: e.matmul(oa, la, ra, start=start, stop=stop),
                       reads=_bufs(lhsT, rhs), writes=[out.buf], accum=(not start) or nowaw)

    def act(self, out, in_, func, bias=None, scale=None, accum_out=None, eng="act"):
        kw = {}
        if bias is not None:
            kw["bias"] = _ap(bias)
        if scale is not None:
            kw["scale"] = _ap(scale)
        if accum_out is not None:
            kw["accum_out"] = _ap(accum_out)
        oa, ia = out.ap, in_.ap
        return self.op("act", lambda e: e.activation(out=oa, in_=ia, func=func, **kw),
                       reads=_bufs(in_, bias, scale), writes=_bufs(out, accum_out))

    def tt(self, out, in0, in1, op, eng="dve"):
        oa, a, b = out.ap, in0.ap, in1.ap
        return self.op(eng, lambda e: e.tensor_tensor(out=oa, in0=a, in1=b, op=op),
                       reads=_bufs(in0, in1), writes=[out.buf])

    def ts(self, out, in0, s1, op0, s2=None, op1=None, eng="dve"):
        oa, a = out.ap, in0.ap
        s1a, s2a = _ap(s1), _ap(s2)
        if op1 is None:
            f = lambda e: e.tensor_scalar(out=oa, in0=a, scalar1=s1a, scalar2=None, op0=op0)
        else:
            f = lambda e: e.tensor_scalar(out=oa, in0=a, scalar1=s1a, scalar2=s2a, op0=op0, op1=op1)
        return self.op(eng, f, reads=_bufs(in0, s1, s2), writes=[out.buf])

    def stt(self, out, in0, scalar, in1, op0, op1, eng="dve"):
        oa, a, b, s = out.ap, in0.ap, in1.ap, _ap(scalar)
        return self.op(eng, lambda e: e.scalar_tensor_tensor(out=oa, in0=a, scalar=s, in1=b, op0=op0, op1=op1),
                       reads=_bufs(in0, in1, scalar), writes=[out.buf])

    def copy(self, out, in_, eng="dve"):
        oa, a = out.ap, in_.ap
        if eng == "act":
            return self.op("act", lambda e: e.copy(out=oa, in_=a), reads=[in_.buf], writes=[out.buf])
        return self.op(eng, lambda e: e.tensor_copy(out=oa, in_=a), reads=[in_.buf], writes=[out.buf])

    def recip(self, out, in_):
        oa, a = out.ap, in_.ap
        return self.op("dve", lambda e: e.reciprocal(out=oa, in_=a), reads=[in_.buf], writes=[out.buf])

    def reduce(self, out, in_, op=None, axis=None, eng="dve"):
        oa, a = out.ap, in_.ap
        op = ALU.add if op is None else op
        axis = AX.X if axis is None else axis
        return self.op(eng, lambda e: e.tensor_reduce(out=oa, in_=a, axis=axis, op=op), reads=[in_.buf], writes=[out.buf])

    def memset(self, out, val, eng="dve"):
        oa = out.ap
        return self.op(eng, lambda e: e.memset(oa, val), reads=[], writes=[out.buf])

    def scan(self, out, d0, d1, initial, op0, op1, eng="dve"):
        oa, a, b, i = out.ap, d0.ap, d1.ap, _ap(initial)
        return self.op(eng, lambda e: e.tensor_tensor_scan(out=oa, data0=a, data1=b, initial=i, op0=op0, op1=op1),
                       reads=_bufs(d0, d1, initial), writes=[out.buf])

    def emit(self, final_waits=()):
        nc = self.nc
        per = {e: [] for e in ENGS}
        for o in self.ops:
            per[o.eng].append(o)
        self.barrier()
        fin = list(self.pending["sp"].values())

        def run(e, lst, last=False):
            for o in lst:
                for sem, val in o.waits:
                    e.wait_ge(sem, val)
                ins = o.fn(e)
                if o.inc == 1 and o.dref is not None:
                    ins.then_inc(o.sem)
                else:
                    ins.then_inc(o.sem, o.inc)
            if last:
                for sem, val in fin:
                    e.wait_ge(sem, val)

        with nc.Block() as block:
            @block.tensor
            def _(e):
                run(e, per["pe"])

            @block.scalar
            def _(e):
                run(e, per["act"])

            @block.vector
            def _(e):
                run(e, per["dve"])

            @block.gpsimd
            def _(e):
                run(e, per["pool"])

            @block.sync
            def _(e):
                run(e, per["sp"], last=True)
        self.stack.close()


def run_spmd(nc, in_maps):
    res = run_bass_kernel_spmd(nc, in_maps, core_ids=list(range(len(in_maps))))
    return res.results


D = 1024
DFF = 2816
NK = 8
NF = 22
EPS = 1e-6
INCOLS = 5120
C0 = 0.6065306597126334
GN_EPS = 64e-5


class Cfg:
    def __init__(self, ntl=4096, ncx=64):
        self.ntl = ntl
        self.ncx = ncx
        self.nt = ntl + ncx
        self.L = 4 * ntl
        self.CTX = 4 * ncx
        self.TSEQ = self.L + self.CTX
        self.TN = min(512, ntl)

    def tiles(self):
        out = []
        c = 0
        while c < self.ntl:
            out.append((c, self.TN, 0))
            c += self.TN
        out.append((self.ntl, self.ncx, 1))
        return out


class Pieced:
    def __init__(self, nc, name, R, Rp, bounds):
        self.Rp = Rp
        self.bounds = bounds
        self.loc_t = {}
        self.gat_t = {}
        for rb in range(R // Rp):
            for ci, (a, b) in enumerate(bounds):
                self.loc_t[(rb, ci)] = Buf(f"{name}_l{rb}_{ci}", nc.dram_tensor(f"{name}_l{rb}_{ci}", [Rp, b - a], F32).ap(), is_dram=True)
                self.gat_t[(rb, ci)] = Buf(f"{name}_g{rb}_{ci}", nc.dram_tensor(f"{name}_g{rb}_{ci}", [4 * Rp, b - a], F32).ap(), is_dram=True)

    def _ci(self, c0, n):
        for ci, (a, b) in enumerate(self.bounds):
            if a <= c0 and c0 + n <= b:
                return ci, a
        raise ValueError((c0, n, self.bounds))

    def loc(self, r0, r1, c0, n):
        rb = r0 // self.Rp
        assert (r1 - 1) // self.Rp == rb
        ci, a = self._ci(c0, n)
        return self.loc_t[(rb, ci)][r0 - rb * self.Rp:r1 - rb * self.Rp, c0 - a:c0 - a + n]

    def gat(self, r0, r1, c0, n):
        rb = r0 // self.Rp
        assert (r1 - 1) // self.Rp == rb
        ci, a = self._ci(c0, n)
        v = self.gat_t[(rb, ci)].re("(r p) n -> p r n", r=4)
        return v[r0 - rb * self.Rp:r1 - rb * self.Rp, :, c0 - a:c0 - a + n]

    def gather(self, P):
        for k in self.loc_t:
            P.allgather(self.gat_t[k][:], self.loc_t[k][:], GRP4)


class PiecedT:
    def __init__(self, nc, name, F_, bounds):
        self.bounds = bounds
        self.loc_t = []
        self.gat_t = []
        for ci, (a, b) in enumerate(bounds):
            self.loc_t.append(Buf(f"{name}_l{ci}", nc.dram_tensor(f"{name}_l{ci}", [b - a, F_], F32).ap(), is_dram=True))
            self.gat_t.append(Buf(f"{name}_g{ci}", nc.dram_tensor(f"{name}_g{ci}", [4 * (b - a), F_], F32).ap(), is_dram=True))

    def loc(self, c0, n):
        for ci, (a, b) in enumerate(self.bounds):
            if a <= c0 and c0 + n <= b:
                return self.loc_t[ci][c0 - a:c0 - a + n, :]
        raise ValueError((c0, n))

    def gather(self, P):
        for l_, g_ in zip(self.loc_t, self.gat_t):
            P.allgather(g_[:], l_[:], GRP4)


def col_bounds(cfg, pc, total_lat, ctxw):
    pc = min(pc, total_lat)
    out = [(a, a + pc) for a in range(0, total_lat, pc)]
    out.append((total_lat, total_lat + ctxw))
    return out


class Consts:
    def __init__(self, P, dr):
        self.ident = P.sb("ident", [128, 128])
        self.rotm = P.sb("rotm", [128, 128])
        self.blk64 = P.sb("blk64", [128, 128])
        self.ones = P.sb("ones", [128, 128])
        P.dma(self.ident[:], dr["c_ident"][:])
        P.dma(self.rotm[:], dr["c_rotm"][:])
        P.dma(self.blk64[:], dr["c_blk64"][:])
        P.memset(self.ones[:], 1.0)


class TokCtx:
    def __init__(self, P, cfg, K):
        self.P = P
        self.K = K
        TN = cfg.TN
        self.TN = TN
        self.pa = [P.ps(f"pa{i}", [128, 512]) for i in range(2)]
        self.pb = [P.ps(f"pb{i}", [128, 512]) for i in range(2)]
        self.po = [P.ps(f"po{i}", [128, 512]) for i in range(4)]
        self.wA = [P.sb(f"wA{i}", [128, NK, 256]) for i in range(2)]
        self.wB = [P.sb(f"wB{i}", [128, NK, 256]) for i in range(2)]
        self.w2 = [P.sb(f"w2_{i}", [128, 512]) for i in range(3)]
        self.sq = [P.sb(f"sq{i}", [128, TN]) for i in range(2)]
        self.rstd = P.sb("rstd", [128, TN])
        self.tmp = [P.sb(f"tmp{i}", [128, TN]) for i in range(2)]
        self.g = P.sb("g", [128, NF, TN])
        self.cnt = 0
        self.A = P.sb("Asc", [128, 2, 3, 8])
        self.G = P.sb("Gsc", [128, 2, 2, 8])

    def set_mods(self, mod, nw):
        P = self.P
        self.mod = mod
        for s in range(2):
            for i in range(3):
                P.stt(self.A[:, s, i, :], mod[:, s, 3 * i + 1, :], 1.0, nw[:, i, :], ALU.add, ALU.mult)
            for i, j in ((0, 2), (1, 8)):
                P.ts(self.G[:, s, i, :], mod[:, s, j, :], 0.5, ALU.mult)

    def normmod(self, x, h, n, s, i):
        P = self.P
        ss = self.pa[0]
        for k in range(NK):
            sq = self.sq[k % 2]
            P.act(sq[:, :n], x[:, k, :n], AF.Square)
            P.mm(ss[:, :n], self.K.ones[:], sq[:, :n], start=(k == 0), stop=(k == NK - 1))
        P.ts(self.rstd[:, :n], ss[:, :n], 1.0 / D, ALU.mult, EPS, ALU.add)
        P.act(self.rstd[:, :n], self.rstd[:, :n], AF.Sqrt)
        P.recip(self.rstd[:, :n], self.rstd[:, :n])
        for k in range(NK):
            t = self.tmp[k % 2]
            P.tt(t[:, :n], x[:, k, :n], self.rstd[:, :n], ALU.mult)
            P.act(h[:, k, :n], t[:, :n], AF.Identity, bias=self.mod[:, s, 3 * i, k:k + 1],
                  scale=self.A[:, s, i, k:k + 1])

    def ffn(self, x, h, n, s, gi, w1_d, w3_d, w2_d):
        P = self.P
        for blk in range(NF // 2):
            wa = self.wA[blk % 2]
            wb = self.wB[blk % 2]
            P.dma(wa[:], w1_d[:, blk * 256:(blk + 1) * 256].re("(k p) f -> p k f", p=128))
            P.dma(wb[:], w3_d[:, blk * 256:(blk + 1) * 256].re("(k p) f -> p k f", p=128))
            for c in range(2):
                j = blk * 2 + c
                pa = self.pa[j % 2]
                pb = self.pb[j % 2]
                for k in range(NK):
                    P.mm(pa[:, :n], wa[:, k, c * 128:(c + 1) * 128], h[:, k, :n], start=(k == 0), stop=(k == NK - 1))
                for k in range(NK):
                    P.mm(pb[:, :n], wb[:, k, c * 128:(c + 1) * 128], h[:, k, :n], start=(k == 0), stop=(k == NK - 1))
                t = self.tmp[j % 2]
                P.act(t[:, :n], pa[:, :n], AF.Silu)
                P.tt(self.g[:, j, :n], t[:, :n], pb[:, :n], ALU.mult)
        for mh in range(2):
            for j in range(NF):
                w2 = self.w2[self.cnt % 3]
                self.cnt += 1
                P.dma(w2[:], w2_d[j * 128:(j + 1) * 128, mh * 512:(mh + 1) * 512])
                for m in range(4):
                    P.mm(self.po[m][:, :n], w2[:, m * 128:(m + 1) * 128], self.g[:, j, :n],
                         start=(j == 0), stop=(j == NF - 1))
            for m in range(4):
                k = mh * 4 + m
                P.stt(x[:, k, :n], self.po[m][:, :n], self.G[:, s, gi, k:k + 1], x[:, k, :n], ALU.mult, ALU.add)


def stage_a(P, cfg, K, mod, nw, dr, lw, bufs):
    with P.scope():
        T = TokCtx(P, cfg, K)
        T.set_mods(mod, nw)
        TN = cfg.TN
        x = P.sb("x", [128, NK, TN])
        h = P.sb("h", [128, NK, TN])
        stage = [P.sb(f"st{i}", [128, TN]) for i in range(4)]
        cs = [P.sb(f"ropec{i}", [128, TN]) for i in range(2)]
        sn = [P.sb(f"ropes{i}", [128, TN]) for i in range(2)]
        qkw = P.sb("qkw", [128, 2])
        P.dma(qkw[:], lw["qkw"][:])
        xa = bufs["xa"]
        xv = xa.re("(k p) n -> p k n", p=128)
        win = lw["w_in"]
        sti = 0
        for ti, (c0, n, s) in enumerate(cfg.tiles()):
            P.dma(x[:, :, :n], xv[:, :, c0:c0 + n])
            if s == 0:
                rc = cs[ti % 2]
                rs = sn[ti % 2]
                for hh in range(2):
                    P.dma(rc[hh * 64:(hh + 1) * 64, :n], dr["ropec"][:, c0:c0 + n])
                    P.dma(rs[hh * 64:(hh + 1) * 64, :n], dr["ropes"][:, c0:c0 + n])
            T.normmod(x, h, n, s, 0)
            T.ffn(x, h, n, s, 0, lw["w1a"], lw["w3a"], lw["w2a"])
            P.dma(xv[:, :, c0:c0 + n], x[:, :, :n], eng="pool")
            T.normmod(x, h, n, s, 1)
            for blk in range(INCOLS // 256):
                wa = T.wA[blk % 2]
                P.dma(wa[:], win[:, blk * 256:(blk + 1) * 256].re("(k p) f -> p k f", p=128))
                for c in range(2):
                    j = blk * 2 + c
                    if j == 3:
                        pv = T.po[2]
                        nsub = (n + 127) // 128
                        for sub in range(nsub):
                            w = min(128, n - sub * 128)
                            for k in range(NK):
                                P.mm(pv[:w, sub * 128:(sub + 1) * 128], h[:, k, sub * 128:sub * 128 + w],
                                     wa[:, k, 128:256], start=(k == 0), stop=(k == NK - 1))
                        st = stage[sti % 4]; sti += 1
                        for sub in range(nsub):
                            w = min(128, n - sub * 128)
                            P.copy(st[:w, sub * 128:(sub + 1) * 128], pv[:w, sub * 128:(sub + 1) * 128], eng="act")
                            P.dma(bufs["gv"].loc(c0 + sub * 128, w), st[:w, sub * 128:(sub + 1) * 128], eng="pool")
                        continue
                    pa = (T.pa + T.pb)[j % 4]
                    for k in range(NK):
                        P.mm(pa[:, :n], wa[:, k, c * 128:(c + 1) * 128], h[:, k, :n], start=(k == 0), stop=(k == NK - 1))
                    st = stage[sti % 4]; sti += 1
                    if j == 2 or 12 <= j < 16:
                        wcol = qkw[:, 1:2] if j == 2 else qkw[:, 0:1]
                        sq = T.sq[0]
                        P.act(sq[:, :n], pa[:, :n], AF.Square)
                        P.mm(T.po[0][:, :n], K.blk64[:], sq[:, :n])
                        P.ts(T.rstd[:, :n], T.po[0][:, :n], 1.0 / 64, ALU.mult, EPS, ALU.add)
                        P.act(T.rstd[:, :n], T.rstd[:, :n], AF.Sqrt)
                        P.recip(T.rstd[:, :n], T.rstd[:, :n])
                        if s == 1:
                            P.stt(st[:, :n], pa[:, :n], wcol, T.rstd[:, :n], ALU.mult, ALU.mult)
                        else:
                            xn = T.sq[1]
                            P.stt(xn[:, :n], pa[:, :n], wcol, T.rstd[:, :n], ALU.mult, ALU.mult)
                            P.mm(T.po[1][:, :n], K.rotm[:], xn[:, :n])
                            P.tt(T.tmp[0][:, :n], xn[:, :n], rc[:, :n], ALU.mult)
                            P.tt(T.tmp[1][:, :n], T.po[1][:, :n], rs[:, :n], ALU.mult)
                            P.tt(st[:, :n], T.tmp[0][:, :n], T.tmp[1][:, :n], ALU.add, eng="pool")
                        if j == 2:
                            dst = bufs["gk"].loc(0, 128, c0, n)
                        else:
                            dst = bufs["qb"][(j - 12) * 128:(j - 11) * 128, c0:c0 + n]
                    elif j >= 16:
                        P.act(st[:, :n], pa[:, :n], AF.Sigmoid)
                        dst = bufs["sgate"][(j - 16) * 128:(j - 15) * 128, c0:c0 + n]
                    else:
                        if j % 2 == 0:
                            P.copy(st[:, :n], pa[:, :n], eng="act")
                        else:
                            P.copy(st[:, :n], pa[:, :n], eng="dve")
                        if j < 2:
                            dst = bufs["gu"].loc(j * 128, (j + 1) * 128, c0, n)
                        else:
                            dst = bufs["gf"].loc((j - 4) * 128, (j - 3) * 128, c0, n)
                    P.dma(dst, st[:, :n], eng="pool")


def stage_gqa(P, cfg, K, bufs):
    with P.scope():
        NT, TS = cfg.nt, cfg.TSEQ
        nkt = TS // 128
        nct = cfg.CTX // 128
        gk, gv = bufs["gk"], bufs["gv"]
        kt = P.sb("kt", [128, TS])
        vt = P.sb("vt", [128, nkt, 2, 65])
        P.memset(vt[:, :, :, 64:65], 1.0)
        off = 0
        for ci, (a, b) in enumerate(gk.bounds):
            w = b - a
            P.dma(kt[:, off:off + 4 * w].re("p (r n) -> p r n", r=4), gk.gat(0, 128, a, w))
            off += 4 * w
        assert gv.bounds == gk.bounds
        off = 0
        for ci, (a, b) in enumerate(gv.bounds):
            w4 = 4 * (b - a)
            assert off % 128 == 0 and w4 % 128 == 0
            vview = gv.gat_t[ci].re("(t p) (h d) -> p t h d", p=128, h=2)
            nt_ = w4 // 128
            for t0 in range(0, nt_, 13):
                t1 = min(nt_, t0 + 13)
                for hh in range(2):
                    P.dma(vt[:, off // 128 + t0:off // 128 + t1, hh, 0:64], vview[:, t0:t1, hh, :])
            off += w4
        ktc = kt[:, cfg.L:TS]
        vtc = vt[:, cfg.L // 128:nkt, :, :]
        qt = [P.sb(f"qt{i}", [128, 4, cfg.TN]) for i in range(2)]
        pt = [P.sb(f"pt{i}", [128, cfg.TN]) for i in range(3)]
        ytm = P.sb("ytm", [128, 4, 512])
        yfm = [P.sb(f"yfm{i}", [128, cfg.TN]) for i in range(2)]
        rec = P.sb("rec", [128, 4])
        ps = [P.ps(f"ps{i}", [128, 512]) for i in range(3)]
        po = [P.ps(f"po{i}", [128, 128]) for i in range(4)]
        p2 = [P.ps(f"p2{i}", [128, 512]) for i in range(1)]
        qb = bufs["qb"]
        qv = qb.re("(hk g d) n -> hk d g n", hk=2, g=4)
        it = 0
        for ti, (c0, n, s) in enumerate(cfg.tiles()):
            q = qt[ti % 2]
            for hk in range(2):
                P.dma(q[hk * 64:(hk + 1) * 64, :, :n], qv[hk, :, :, c0:c0 + n])
            nsub = (n + 127) // 128
            keys, vals, ntile = (kt, vt, nkt) if s == 0 else (ktc, vtc, nct)
            for hk in range(2):
                for g in range(4):
                    hd = hk * 4 + g
                    for t in range(ntile):
                        sc = ps[it % 3]
                        pr = pt[it % 3]
                        it += 1
                        P.mm(sc[:, :n], keys[hk * 64:(hk + 1) * 64, t * 128:(t + 1) * 128],
                             q[hk * 64:(hk + 1) * 64, g, :n])
                        P.act(pr[:, :n], sc[:, :n], AF.Exp, scale=0.125)
                        for sub in range(nsub):
                            w = min(128, n - sub * 128)
                            P.mm(po[sub][:w, 0:65], pr[:, sub * 128:sub * 128 + w], vals[:, t, hk, :],
                                 start=(t == 0), stop=(t == ntile - 1))
                    for sub in range(nsub):
                        w = min(128, n - sub * 128)
                        P.recip(rec[:w, sub:sub + 1], po[sub][:w, 64:65])
                        P.ts(ytm[:w, sub, hd * 64:(hd + 1) * 64], po[sub][:w, 0:64], rec[:w, sub:sub + 1], ALU.mult)
            for fc in range(4):
                pp = p2[0]
                yf = yfm[fc % 2]
                for sub in range(nsub):
                    w = min(128, n - sub * 128)
                    P.mm(pp[:, sub * 128:sub * 128 + w], ytm[:w, sub, fc * 128:(fc + 1) * 128], K.ident[:w, :w])
                P.copy(yf[:, :n], pp[:, :n], eng="act")
                P.dma(bufs["yatt"][fc * 128:(fc + 1) * 128, c0:c0 + n], yf[:, :n], eng="pool")


def stage_b(P, cfg, K, mod, nw, dr, lw, bufs, out_dst):
    with P.scope():
        T = TokCtx(P, cfg, K)
        T.set_mods(mod, nw)
        TN = cfg.TN
        x = P.sb("x", [128, NK, TN])
        h = P.sb("h", [128, NK, TN])
        sel = P.sb("sel", [128, 4, 128])
        P.dma(sel[:], dr["selI"][:])
        wglu = P.sb("wglu", [128, 2, 256])
        bglu = P.sb("bglu", [128, 2])
        wbs = P.sb("wbs", [128, 2, D])
        wba = P.sb("wba", [128, 4, D])
        wbr = P.sb("wbr", [128, 2, D])
        P.dma(wglu[:], lw["w_glu"].re("(k p) f -> p k f", p=128))
        P.dma(bglu[:], lw["b_glu"][:])
        P.dma(wbs[:], lw["w_br_s5"].re("(k p) f -> p k f", p=128))
        P.dma(wba[:], lw["w_br_att"].re("(k p) f -> p k f", p=128))
        P.dma(wbr[:], lw["w_br_rw"].re("(k p) f -> p k f", p=128))
        cand = [P.sb(f"cand{i}", [128, 4, TN]) for i in range(1)]
        ys = P.sb("ys", [128, 2, TN])
        yr = h[:, 6:8, :]
        ya = h[:, 2:6, :]
        ge = h[:, 0:2, :]
        sg = [P.sb(f"sg{i}", [128, 3, TN]) for i in range(1)]
        xv = bufs["xa"].re("(k p) n -> p k n", p=128)
        ov = out_dst.re("(k p) n -> p k n", p=128)
        gs5 = bufs["s5o"]
        grw = bufs["rwo"]
        sgv = bufs["sgate"].re("(b k p) n -> p b k n", b=3, p=128)
        mixed = T.g
        ci = 0
        for ti, (c0, n, s) in enumerate(cfg.tiles()):
            P.dma(x[:, :, :n], xv[:, :, c0:c0 + n])
            P.dma(ya[:, :, :n], bufs["yatt"].re("(k p) n -> p k n", p=128)[:, :, c0:c0 + n])
            for src, dstt in ((gs5, ys), (grw, yr)):
                for kk in range(2):
                    cd = cand[0]
                    for r_ in range(4):
                        if s == 0:
                            gt_ = src.gat_t[(0, r_)][kk * 128:(kk + 1) * 128, c0:c0 + n]
                        else:
                            gt_ = src.gat_t[(0, 4)][kk * 128:(kk + 1) * 128, r_ * cfg.ncx:(r_ + 1) * cfg.ncx]
                        P.dma(cd[:, r_, :n], gt_)
                    pp = T.pa[kk]
                    for r in range(4):
                        P.mm(pp[:, :n], sel[:, r, :], cd[:, r, :n], start=(r == 0), stop=(r == 3))
                    P.copy(dstt[:, kk, :n], pp[:, :n], eng="act")
            for kk in range(2):
                y = ys[:, kk, :n]
                t0 = T.tmp[0][:, :n]
                t1 = T.tmp[1][:, :n]
                P.tt(t0, y, y, ALU.mult)
                P.ts(t0, t0, 0.044715, ALU.mult, 1.0, ALU.add)
                P.tt(t0, t0, y, ALU.mult)
                P.act(t1, t0, AF.Sigmoid, scale=1.5957691216057308)
                P.tt(ge[:, kk, :n], y, t1, ALU.mult)
            for mm_ in range(2):
                pp = T.pb[mm_]
                for kk in range(2):
                    P.mm(pp[:, :n], wglu[:, kk, mm_ * 128:(mm_ + 1) * 128], ge[:, kk, :n], start=(kk == 0), stop=(kk == 1))
                t1 = T.tmp[mm_][:, :n]
                P.act(t1, pp[:, :n], AF.Sigmoid, bias=bglu[:, mm_:mm_ + 1])
                P.tt(ys[:, mm_, :n], ge[:, mm_, :n], t1, ALU.mult)
            for m in range(NK):
                sgt = sg[0]
                P.dma(sgt[:, :, :n], sgv[:, :, m, c0:c0 + n])
                p0, p1, p2 = T.po[0], T.po[1], T.po[2]
                for kk in range(2):
                    P.mm(p0[:, :n], wbs[:, kk, m * 128:(m + 1) * 128], ys[:, kk, :n], start=(kk == 0), stop=(kk == 1))
                for kk in range(4):
                    P.mm(p1[:, :n], wba[:, kk, m * 128:(m + 1) * 128], ya[:, kk, :n], start=(kk == 0), stop=(kk == 3))
                for kk in range(2):
                    P.mm(p2[:, :n], wbr[:, kk, m * 128:(m + 1) * 128], yr[:, kk, :n], start=(kk == 0), stop=(kk == 1))
                t0 = T.tmp[0][:, :n]
                t1 = T.tmp[1][:, :n]
                P.tt(t0, p0[:, :n], sgt[:, 0, :n], ALU.mult)
                P.tt(t1, p1[:, :n], sgt[:, 1, :n], ALU.mult)
                P.tt(t0, t0, t1, ALU.add, eng="pool")
                P.tt(t1, p2[:, :n], sgt[:, 2, :n], ALU.mult)
                P.tt(mixed[:, m, :n], t0, t1, ALU.add, eng="pool")
            wout = lw["w_out"]
            for blk in range(4):
                wa = T.wA[blk % 2]
                P.dma(wa[:], wout[:, blk * 256:(blk + 1) * 256].re("(k p) f -> p k f", p=128))
                for c in range(2):
                    m = blk * 2 + c
                    pp = T.pa[m % 2]
                    for kk in range(NK):
                        P.mm(pp[:, :n], wa[:, kk, c * 128:(c + 1) * 128], mixed[:, kk, :n], start=(kk == 0), stop=(kk == NK - 1))
                    P.stt(x[:, m, :n], pp[:, :n], T.mod[:, s, 5, m:m + 1], x[:, m, :n], ALU.mult, ALU.add)
            T.normmod(x, h, n, s, 2)
            T.ffn(x, h, n, s, 1, lw["w1b"], lw["w3b"], lw["w2b"])
            P.dma(ov[:, :, c0:c0 + n], x[:, :, :n], eng="pool")


I32 = mybir.dt.int32
TWO_PI = 6.283185307179586


def sincos(P, ang, sin_out, cos_out, tmp, tmp2, tmpi):
    for out, off in ((sin_out, 32.5), (cos_out, 32.75)):
        P.ts(tmp, ang, 1.0 / TWO_PI, ALU.mult, off, ALU.add)
        P.copy(tmpi, tmp)
        P.copy(tmp2, tmpi)
        P.tt(tmp, tmp, tmp2, ALU.subtract)
        P.ts(tmp2, tmp, 0.0, ALU.is_lt)
        P.tt(tmp, tmp, tmp2, ALU.add)
        P.act(out, tmp, AF.Sin, scale=TWO_PI * (1 - 1e-6), bias=-3.141592653589793 * (1 - 1e-6))


def stage_s5(P, cfg, K, dr, lw, bufs, layer):
    NS = cfg.TSEQ // 8
    NCS = cfg.CTX // 8
    NLS = cfg.ntl // 8
    gu = bufs["gu"]
    with P.scope():
        lam = P.sb("lam", [128, 3, 4])
        bL = P.sb("bL", [128, 2, 4, 256])
        cR = P.sb("cR", [128, 2, 4, 64])
        cIn = P.sb("cIn", [128, 4, 64])
        Dm = P.sb("Dm", [128, 2, 64])
        jt = P.sb("jt", [128, 4, 9])
        P.dma(lam[:], lw["s5_lam"][:])
        P.dma(bL[:], lw["s5_bL"][:])
        P.dma(cR[:], lw["s5_cR"][:])
        P.dma(Dm[:], lw["s5_D"][:])
        P.dma(jt[:], dr["c_jt"][:])
        P.ts(cIn[:], cR[:, 1, :, :], -1.0, ALU.mult)
        dt = P.sb("dt", [128, 4]); aa = P.sb("aa", [128, 4]); th = P.sb("th", [128, 4])
        P.act(dt[:], lam[:, 2, :], AF.Exp)
        P.tt(aa[:], lam[:, 0, :], dt[:], ALU.mult)
        P.tt(th[:], lam[:, 1, :], dt[:], ALU.mult)
        ea = P.sb("ea", [128, 4, 9]); an = P.sb("an", [128, 4, 9])
        sn = P.sb("sn", [128, 4, 9]); cn = P.sb("cn", [128, 4, 9])
        t1 = P.sb("t1", [128, 4, 9]); t2 = P.sb("t2", [128, 4, 9]); ti = P.sb("ti", [128, 4, 9], I32)
        for c in range(4):
            P.ts(ea[:, c, :], jt[:, c, :], aa[:, c:c + 1], ALU.mult)
            P.ts(an[:, c, :], jt[:, c, :], th[:, c:c + 1], ALU.mult)
        P.act(ea[:], ea[:], AF.Exp)
        sincos(P, an[:], sn[:], cn[:], t1[:], t2[:], ti[:])
        pr = P.sb("pr", [128, 4, 9]); pi = P.sb("pi", [128, 4, 9])
        P.tt(pr[:], ea[:], cn[:], ALU.mult)
        P.tt(pi[:], ea[:], sn[:], ALU.mult)
        den = P.sb("den", [128, 4]); nr = P.sb("nr", [128, 4]); fr = P.sb("fr", [128, 4]); fi = P.sb("fi", [128, 4])
        u1 = P.sb("u1", [128, 4]); u2 = P.sb("u2", [128, 4])
        P.tt(den[:], lam[:, 0, :], lam[:, 0, :], ALU.mult)
        P.tt(u1[:], lam[:, 1, :], lam[:, 1, :], ALU.mult)
        P.tt(den[:], den[:], u1[:], ALU.add)
        P.recip(den[:], den[:])
        P.ts(nr[:], pr[:, :, 1], -1.0, ALU.add)
        P.tt(u1[:], nr[:], lam[:, 0, :], ALU.mult)
        P.tt(u2[:], pi[:, :, 1], lam[:, 1, :], ALU.mult)
        P.tt(fr[:], u1[:], u2[:], ALU.add)
        P.tt(fr[:], fr[:], den[:], ALU.mult)
        P.tt(u1[:], pi[:, :, 1], lam[:, 0, :], ALU.mult)
        P.tt(u2[:], nr[:], lam[:, 1, :], ALU.mult)
        P.tt(fi[:], u1[:], u2[:], ALU.subtract)
        P.tt(fi[:], fi[:], den[:], ALU.mult)
        zr = P.sb("zr", [128, 4, 9]); zi = P.sb("zi", [128, 4, 9]); zin = P.sb("zin", [128, 4, 9])
        for c in range(4):
            P.ts(t1[:, c, :], pi[:, c, :], fi[:, c:c + 1], ALU.mult)
            P.stt(zr[:, c, :], pr[:, c, :], fr[:, c:c + 1], t1[:, c, :], ALU.mult, ALU.subtract)
            P.ts(t1[:, c, :], pi[:, c, :], fr[:, c:c + 1], ALU.mult)
            P.stt(zi[:, c, :], pr[:, c, :], fi[:, c:c + 1], t1[:, c, :], ALU.mult, ALU.add)
        P.ts(zin[:], zi[:], -1.0, ALU.mult)
        pin = P.sb("pin", [128, 4, 9])
        P.ts(pin[:], pi[:], -1.0, ALU.mult)
        NLEV = 1
        while (1 << NLEV) < NS:
            NLEV += 1
        qr = P.sb("qr", [128, 4, 16]); qi = P.sb("qi", [128, 4, 16]); qin = P.sb("qin", [128, 4, 16])
        P.copy(qr[:, :, 0], pr[:, :, 8]); P.copy(qi[:, :, 0], pi[:, :, 8])
        for k in range(1, NLEV):
            P.tt(u1[:], qr[:, :, k - 1], qr[:, :, k - 1], ALU.mult)
            P.tt(u2[:], qi[:, :, k - 1], qi[:, :, k - 1], ALU.mult)
            P.tt(qr[:, :, k], u1[:], u2[:], ALU.subtract)
            P.tt(u1[:], qr[:, :, k - 1], qi[:, :, k - 1], ALU.mult)
            P.ts(qi[:, :, k], u1[:], 2.0, ALU.mult)
        P.ts(qin[:], qi[:], -1.0, ALU.mult)

        X = [[[P.sb(f"X{d}{p}{r}", [128, NS + 2]) for r in range(2)] for p in range(2)] for d in range(2)]
        Kmat = [[P.sb(f"Km{d}{j}", [128, 2, 64]) for j in range(8)] for d in range(2)]
        for d in range(2):
            for p in range(2):
                for r in range(2):
                    P.memset(X[d][p][r][:], 0.0)
        pw = [P.ps(f"pw{i}", [128, 512]) for i in range(4)]
        pk = [P.ps(f"pk{i}", [128, 2, 64]) for i in range(2)]

        def blocks():
            return [(True, 0, cfg.CTX)] + [(False, r, cfg.ntl) for r in range(4)]

        def pos_f(is_ctx, r):
            return 0 if is_ctx else NCS + r * NLS

        def pos_b(is_ctx, r):
            return 4 * NLS if is_ctx else r * NLS

        def load_u(ut, is_ctx, r):
            if is_ctx:
                for k_ in range(2):
                    P.dma(ut[:, k_, :cfg.CTX].re("p (r n) -> p r n", r=4), gu.gat(k_ * 128, (k_ + 1) * 128, cfg.ntl, cfg.ncx))
            else:
                for (a, b) in gu.bounds[:-1]:
                    for k_ in range(2):
                        P.dma(ut[:, k_, a:b], gu.gat(k_ * 128, (k_ + 1) * 128, a, b - a)[:, r, :])

        with P.scope():
            BcT = [[[P.sb(f"Bc{c}{r}{j}", [128, 2, 128]) for j in range(8)] for r in range(2)] for c in range(4)]
            Wre = [P.sb(f"Wre{i}", [128, 256]) for i in range(2)]
            Wim = [P.sb(f"Wim{i}", [128, 256]) for i in range(2)]
            tw = P.sb("tw", [128, 256])
            for d in range(2):
                for j in range(8):
                    for p in range(2):
                        c = d * 2 + p
                        wr, wi = Wre[p], Wim[p]
                        P.ts(tw[:], bL[:, 1, c, :], zi[:, c, j:j + 1], ALU.mult)
                        P.stt(wr[:], bL[:, 0, c, :], zr[:, c, j:j + 1], tw[:], ALU.mult, ALU.subtract)
                        P.ts(tw[:], bL[:, 1, c, :], zr[:, c, j:j + 1], ALU.mult)
                        P.stt(wi[:], bL[:, 0, c, :], zi[:, c, j:j + 1], tw[:], ALU.mult, ALU.add)
                        for r, w in ((0, wr), (1, wi)):
                            pp = pw[(2 * p + r) % 4]
                            for kt in range(2):
                                P.mm(pp[:, kt * 128:(kt + 1) * 128], w[:, kt * 128:(kt + 1) * 128], K.ident[:])
                            P.copy(BcT[c][r][j][:], pp[:, 0:256].re("p (k m) -> p k m", k=2), eng="act")
                    pq = pk[j % 2]
                    for kt in range(2):
                        lst = []
                        for p in range(2):
                            c = d * 2 + p
                            lst.append((Wre[p][:, kt * 128:(kt + 1) * 128], cR[:, 0, c, :]))
                            lst.append((Wim[p][:, kt * 128:(kt + 1) * 128], cIn[:, c, :]))
                        for i_, (l_, r_) in enumerate(lst):
                            P.mm(pq[:, kt, :], l_, r_, start=(i_ == 0), stop=(i_ == 3))
                    if j == 0 and d == 0:
                        P.tt(Kmat[d][j][:], pq[:], Dm[:], ALU.add)
                    else:
                        P.copy(Kmat[d][j][:], pq[:])
            ut = P.sb("ut", [128, 2, max(cfg.ntl, cfg.CTX)])
            cnt = 0
            for (is_ctx, r, ntok) in blocks():
                load_u(ut, is_ctx, r)
                nsup = ntok // 8
                uv = ut[:, :, :ntok].re("p k (c s) -> p k s c", s=8)
                for d in range(2):
                    p0 = (pos_f if d == 0 else pos_b)(is_ctx, r) + (1 if d == 0 else 0)
                    for p in range(2):
                        c = d * 2 + p
                        for ri in range(2):
                            pp = pw[cnt % 4]; cnt += 1
                            n_ = 0
                            for s_ in range(8):
                                j = 7 - s_ if d == 0 else s_
                                for kt in range(2):
                                    P.mm(pp[:, :nsup], BcT[c][ri][j][:, kt, :], uv[:, kt, s_, :],
                                         start=(n_ == 0), stop=(n_ == 15))
                                    n_ += 1
                            if cnt % 2:
                                P.copy(X[d][p][ri][:, p0:p0 + nsup], pp[:, :nsup], eng="act")
                            else:
                                P.copy(X[d][p][ri][:, p0:p0 + nsup], pp[:, :nsup])
        with P.scope():
            Y = [P.sb(f"Y{r}", [128, NS + 2]) for r in range(2)]
            for d in range(2):
                for p in range(2):
                    c = d * 2 + p
                    P.memset(Y[0][:], 0.0); P.memset(Y[1][:], 0.0)
                    src = X[d][p]
                    dst = Y
                    for k in range(NLEV):
                        dd = 1 << k
                        a_, b_, bn_ = qr[:, c, k:k + 1], qi[:, c, k:k + 1], qin[:, c, k:k + 1]
                        if d == 0:
                            lo, hi = 1, NS + 1
                            o_t, o_s, o_k = slice(lo + dd, hi), slice(lo, hi - dd), slice(lo, lo + dd)
                        else:
                            lo, hi = 0, NS
                            o_t, o_s, o_k = slice(lo, hi - dd), slice(lo + dd, hi), slice(hi - dd, hi)
                        sr, si = src[0], src[1]
                        dr_, di = dst[0], dst[1]
                        P.stt(dr_[:, o_t], sr[:, o_s], a_, sr[:, o_t], ALU.mult, ALU.add)
                        P.stt(dr_[:, o_t], si[:, o_s], bn_, dr_[:, o_t], ALU.mult, ALU.add)
                        P.stt(di[:, o_t], si[:, o_s], a_, si[:, o_t], ALU.mult, ALU.add)
                        P.stt(di[:, o_t], sr[:, o_s], b_, di[:, o_t], ALU.mult, ALU.add)
                        P.copy(dr_[:, o_k], sr[:, o_k], eng="act")
                        P.copy(di[:, o_k], si[:, o_k], eng="act")
                        src, dst = dst, src
                    if NLEV % 2 == 1:
                        P.copy(X[d][p][0][:], Y[0][:], eng="act")
                        P.copy(X[d][p][1][:], Y[1][:], eng="act")
        with P.scope():
            Cc = [[[P.sb(f"Cc{c}{r}{j}", [128, 64]) for j in range(9)] for r in range(2)] for c in range(4)]
            tc_ = P.sb("tc", [128, 64])
            for c in range(4):
                for j in range(1, 9):
                    P.ts(tc_[:], cR[:, 1, c, :], pi[:, c, j:j + 1], ALU.mult)
                    P.stt(Cc[c][0][j][:], cR[:, 0, c, :], pr[:, c, j:j + 1], tc_[:], ALU.mult, ALU.subtract)
                    P.ts(tc_[:], cR[:, 1, c, :], pr[:, c, j:j + 1], ALU.mult, -1.0, ALU.mult)
                    P.stt(Cc[c][1][j][:], cR[:, 0, c, :], pin[:, c, j:j + 1], tc_[:], ALU.mult, ALU.add)
            ut = P.sb("ut3", [128, 2, max(cfg.ntl, cfg.CTX)])
            yt = [P.sb(f"yt{i}", [64, max(cfg.ntl, cfg.CTX)]) for i in range(2)]
            cnt = 0
            for bi, (is_ctx, r, ntok) in enumerate(blocks()):
                load_u(ut, is_ctx, r)
                nsup = ntok // 8
                uv = ut[:, :, :ntok].re("p k (c s) -> p k s c", s=8)
                y = yt[bi % 2]
                yv = y[:, :ntok].re("p (c s) -> p s c", s=8)
                for tau in range(8):
                    pp = pw[cnt % 4]; cnt += 1
                    ops = []
                    for d in range(2):
                        pos = (pos_f if d == 0 else pos_b)(is_ctx, r)
                        xo = pos if d == 0 else pos + 1
                        jc = tau + 1 if d == 0 else 8 - tau
                        for p in range(2):
                            c = d * 2 + p
                            for ri in range(2):
                                ops.append((Cc[c][ri][jc][:], X[d][p][ri][:, xo:xo + nsup]))
                        taps = range(0, tau + 1) if d == 0 else range(tau, 8)
                        for s_ in taps:
                            j = tau - s_ if d == 0 else s_ - tau
                            km = Kmat[d][j]
                            for kt in range(2):
                                ops.append((km[:, kt, :], uv[:, kt, s_, :]))
                    for i_, (l_, r_) in enumerate(ops):
                        P.mm(pp[0:64, :nsup], l_, r_, start=(i_ == 0), stop=(i_ == len(ops) - 1))
                    if tau % 2:
                        P.copy(yv[:, tau, :], pp[0:64, :nsup], eng="act")
                    else:
                        P.copy(yv[:, tau, :], pp[0:64, :nsup])
                if is_ctx:
                    P.dma(bufs["s5o"].loc(0, 64, cfg.L, cfg.CTX), y[:, :ntok], eng="pool")
                else:
                    P.dma(bufs["s5o"].loc(0, 64, r * cfg.ntl, cfg.ntl), y[:, :ntok], eng="pool")


def stage_rwkv(P, cfg, K, dr, lw, bufs):
    NCH = cfg.TSEQ // 64
    NCC = cfg.CTX // 64
    N = 256 if cfg.ntl >= 256 else cfg.ntl
    NCT = N // 64
    gf = bufs["gf"]
    scr, scrpc, scr2, yscr = bufs["rw_scr"], bufs["rw_pc"], bufs["rw_scr2"], bufs["rw_y"]

    def tiles():
        out = [("ctx", 0, None)]
        for r in range(4):
            for c0 in range(0, cfg.ntl, N):
                out.append(("lat", NCC + (r * cfg.ntl + c0) // 64, (r, c0)))
        return out

    with P.scope():
        sel = P.sb("sel", [128, 2, 128]); P.dma(sel[:], lw["rw_sel"][:])
        cols = P.sb("cols", [128, 10]); P.dma(cols[:], lw["rw_cols"][:])
        lup = P.sb("lup", [128, 128]); P.dma(lup[:], lw["rw_lup"][:])
        gup = P.sb("gup", [128, 64]); P.dma(gup[:], lw["rw_gup"][:])
        gn = P.sb("gn", [64, 2, NCT, 64]); P.dma(gn[:], lw["rw_gn"][:])
        masks = P.sb("masks", [64, 4, 64]); P.dma(masks[:], dr["c_masks"][:])
        reset = P.sb("reset", [128, N]); P.dma(reset[:], dr["c_reset"][:, :N])
        om = P.sb("om", [128, 10]); hm = P.sb("hm", [128, 10])
        P.ts(om[:], cols[:], -1.0, ALU.mult, 1.0, ALU.add)
        P.ts(hm[:], cols[:], 0.5, ALU.mult)
        SL, SU, IL, IU = (masks[:, i, :] for i in range(4))
        mL = (SL, SU); mN = (SU, SL); mNI = (IU, IL)
        id64 = (K.ident[0:64, 0:64], K.ident[64:128, 64:128])

        with P.scope():
            banks = [P.ps(f"bk{i}", [128, 512]) for i in range(8)]
            psel, pwp, pap, pss = banks[0], banks[1], banks[2], banks[3]
            sc = [0]

            def bank():
                sc[0] += 1
                return banks[4 + sc[0] % 4]

            rkv = [P.sb(f"rkv{i}", [128, 6, N + 2]) for i in range(2)]
            lo = [P.sb(f"lo{i}", [128, N + 2]) for i in range(2)]
            glo = [P.sb(f"glo{i}", [128, N + 2]) for i in range(2)]
            W = {nm: P.sb("w_" + nm, [128, N + 2]) for nm in
                 ("ts", "s", "r", "k", "v", "lol", "gll", "tl", "sg", "ag", "kks", "sq", "rn", "kkn", "bv",
                  "kd", "cs", "csd", "tmp", "e1", "e2", "e3", "At", "Bt", "Kt", "Rt", "Bh", "Kh", "rkk", "sgl")}
            tot = P.sb("tot", [128, NCT]); e3t = P.sb("e3t", [128, NCT])
            PAD = {nm: P.sb("pad_" + nm, [128, N]) for nm in ("At", "Bt", "Kt", "v", "Bh", "Kh")}
            for nm in PAD:
                P.memset(PAD[nm][0:64, :], 0.0)
            NSET = 2 * NCT
            CT = {nm: P.sb(f"cu_{nm}", [64, NSET, 64]) for nm in ("L0", "L1", "N0", "N1", "Q", "Atm", "Wsb", "LakT")}
            pack = P.sb("cu_pack", [64, NSET, 8, 64])
            pack2 = P.sb("cu_pack2", [64, NCT, 2, 64])
            MK = {nm: P.sb(f"mk_{nm}", [64, NSET, 64]) for nm in ("L", "N", "NI", "I")}
            for i in range(NSET):
                u = i % 2
                P.copy(MK["L"][:, i, :], mL[u]); P.copy(MK["N"][:, i, :], mN[u]); P.copy(MK["NI"][:, i, :], mNI[u])
                P.copy(MK["I"][:, i, :], K.ident[0:64, 0:64])

            for ti, (kind, ch0, info) in enumerate(tiles()):
                a_rkv, a_lo, a_glo = rkv[ti % 2], lo[ti % 2], glo[ti % 2]
                for t_ in (a_rkv, a_lo, a_glo):
                    if t_ is a_rkv:
                        P.memset(t_[:, :, 0:1], 0.0); P.memset(t_[:, :, N + 1:N + 2], 0.0)
                    else:
                        P.memset(t_[:, 0:1], 0.0); P.memset(t_[:, N + 1:N + 2], 0.0)

                def ld(rank, src0, n_, dst0):
                    kw = {"allow_slow_non_contiguous": True} if n_ == 1 else {}
                    for a_ in range(3):
                        for k_ in range(2):
                            rr0 = a_ * 256 + k_ * 128
                            P.dma(a_rkv[:, 2 * a_ + k_, dst0:dst0 + n_], gf.gat(rr0, rr0 + 128, src0, n_)[:, rank, :], **kw)
                    P.dma(a_lo[:, dst0:dst0 + n_], gf.gat(768, 896, src0, n_)[:, rank, :], **kw)
                    P.dma(a_glo[:, dst0:dst0 + n_], gf.gat(896, 1024, src0, n_)[:, rank, :], **kw)

                if kind == "ctx":
                    for r in range(4):
                        ld(r, cfg.ntl, cfg.ncx, 1 + r * cfg.ncx)
                else:
                    r, c0 = info
                    ld(r, c0, N, 1)
                    if c0 > 0:
                        ld(r, c0 - 1, 1, 0)
                    elif r > 0:
                        ld(r - 1, cfg.ntl - 1, 1, 0)
                    if c0 + N < cfg.ntl:
                        ld(r, c0 + N, 1, N + 1)
                    elif r < 3:
                        ld(r + 1, 0, 1, N + 1)

                def lerp(dst, src, ci):
                    P.tt(W["s"][:, :N], src[:, 0:N], src[:, 2:N + 2], ALU.add, eng="pool")
                    P.ts(W["s"][:, :N], W["s"][:, :N], hm[:, ci:ci + 1], ALU.mult, eng="pool")
                    P.stt(dst[:, :N], src[:, 1:N + 1], om[:, ci:ci + 1], W["s"][:, :N], ALU.mult, ALU.add)

                for ai, nm in enumerate(("r", "k", "v")):
                    for kt in range(2):
                        P.mm(psel[:, :N + 2], sel[:, kt, :], a_rkv[:, 2 * ai + kt, :], start=(kt == 0), stop=(kt == 1))
                    P.copy(W["ts"][:], psel[:, :N + 2], eng="act")
                    lerp(W[nm], W["ts"], ai)
                lerp(W["lol"], a_lo, 8)
                lerp(W["gll"], a_glo, 9)
                r_, k_, v_ = W["r"], W["k"], W["v"]
                P.act(W["tl"][0:64, :N], W["lol"][0:64, :N], AF.Tanh)
                P.mm(pwp[:, :N], lup[0:64, :], W["tl"][0:64, :N])
                P.act(W["sg"][:, :N], pwp[:, :N], AF.Sigmoid, bias=cols[:, 3:4])
                P.mm(pap[:, :N], lup[64:128, :], W["lol"][64:128, :N])
                P.act(W["ag"][:, :N], pap[:, :N], AF.Sigmoid, bias=cols[:, 4:5])
                P.act(W["sgl"][:, :N], W["gll"][:, :N], AF.Sigmoid)
                P.ts(W["kks"][:, :N], k_[:, :N], cols[:, 5:6], ALU.mult)
                P.tt(W["sq"][:, :N], W["kks"][:, :N], W["kks"][:, :N], ALU.mult)
                P.mm(pss[:, :N], K.blk64[:], W["sq"][:, :N])
                P.ts(W["rn"][:, :N], pss[:, :N], 1e-24, ALU.max)
                P.act(W["rn"][:, :N], W["rn"][:, :N], AF.Sqrt)
                P.recip(W["rn"][:, :N], W["rn"][:, :N])
                P.tt(W["kkn"][:, :N], W["kks"][:, :N], W["rn"][:, :N], ALU.mult)
                P.tt(W["bv"][:, :N], W["kkn"][:, :N], W["ag"][:, :N], ALU.mult)
                P.ts(W["kd"][:, :N], W["ag"][:, :N], -1.0, ALU.add, cols[:, 6:7], ALU.mult)
                P.stt(W["kd"][:, :N], W["kd"][:, :N], 1.0, k_[:, :N], ALU.add, ALU.mult)
                sg = W["sg"]
                P.scan(W["cs"][:, :N], reset[:, :N], sg[:, :N], 0.0, ALU.mult, ALU.add)
                P.copy(tot[:], W["cs"][:, :N].re("p (c t) -> p c t", t=64)[:, :, 63])
                P.copy(W["csd"][0:64, :N], W["cs"][0:64, :N])
                P.tt(W["tmp"][64:128, :N], sg[64:128, :N], W["cs"][64:128, :N], ALU.subtract)
                for c in range(NCT):
                    P.ts(W["csd"][64:128, c * 64:(c + 1) * 64], W["tmp"][64:128, c * 64:(c + 1) * 64],
                         tot[64:128, c:c + 1], ALU.add)
                csd = W["csd"]
                P.tt(W["tmp"][:, :N], csd[:, :N], sg[:, :N], ALU.subtract)
                P.act(W["e1"][:, :N], W["tmp"][:, :N], AF.Exp, scale=-C0)
                P.act(W["e2"][:, :N], csd[:, :N], AF.Exp, scale=C0)
                P.act(W["e3"][:, :N], csd[:, :N], AF.Exp, scale=-C0)
                P.act(e3t[:], tot[:], AF.Exp, scale=-C0)
                P.stt(W["At"][:, :N], W["kkn"][:, :N], -1.0, W["e1"][:, :N], ALU.mult, ALU.mult)
                P.tt(W["Bt"][:, :N], W["bv"][:, :N], W["e2"][:, :N], ALU.mult)
                P.tt(W["Kt"][:, :N], W["kd"][:, :N], W["e2"][:, :N], ALU.mult)
                P.tt(W["Rt"][:, :N], r_[:, :N], W["e3"][:, :N], ALU.mult, eng="pool")
                for c in range(NCT):
                    cs_ = slice(c * 64, (c + 1) * 64)
                    P.ts(W["Bh"][:, cs_], W["Bt"][:, cs_], e3t[:, c:c + 1], ALU.mult, eng="pool")
                    P.ts(W["Kh"][:, cs_], W["Kt"][:, cs_], e3t[:, c:c + 1], ALU.mult, eng="pool")
                P.stt(W["rkk"][:, :N], W["kd"][:, :N], cols[:, 7:8], r_[:, :N], ALU.mult, ALU.mult)

                import os
                RWS = int(os.environ.get("RW_SUB", "9"))
                if RWS < 1:
                    continue
                CU = [(c, u) for c in range(NCT) for u in range(2)]

                for pi_, nm in enumerate(PAD):
                    P.copy(PAD[nm][64:128, :N], W[nm][64:128, :N], eng=("act" if pi_ % 2 else "pool"))

                def fm(nm, c, u):
                    return W[nm][u * 64:(u + 1) * 64, c * 64:(c + 1) * 64]

                def fml(nm, c, u):
                    if u == 0:
                        return W[nm][0:64, c * 64:(c + 1) * 64]
                    return PAD[nm][:, c * 64:(c + 1) * 64]

                def fmr(nm, c, u):
                    if u == 0:
                        return W[nm][0:64, c * 64:(c + 1) * 64]
                    return W[nm][:, c * 64:(c + 1) * 64]

                idr = (K.ident[0:64, 0:64], K.ident[:, 64:128])

                def bview(b):
                    return b[0:64, :].re("p (i t) -> p i t", t=64)

                def mm_mask(dst, lname, rname, mk):
                    b = bank()
                    for i, (c, u) in enumerate(CU):
                        P.mm(b[0:64, i * 64:(i + 1) * 64], fml(lname, c, u), fmr(rname, c, u), nowaw=(i > 0))
                    P.tt(dst, bview(b)[:, :NSET, :], MK[mk][:], ALU.mult)

                mm_mask(CT["L0"][:], "At", "Bt", "L")
                mm_mask(CT["N0"][:], "Bt", "At", "N")
                mm_mask(CT["LakT"][:], "Kt", "At", "N")
                mm_mask(pack[:, :, 5, :], "Bt", "Rt", "NI")
                mm_mask(pack[:, :, 6, :], "Kt", "Rt", "NI")
                if RWS < 2:
                    continue
                for nm, dst in (("At", CT["Atm"][:]), ("v", pack[:, :, 4, :]), ("Bh", pack[:, :, 2, :]), ("Kh", pack[:, :, 3, :])):
                    b = bank()
                    for i, (c, u) in enumerate(CU):
                        P.mm(b[0:64, i * 64:(i + 1) * 64], fml(nm, c, u), idr[u], nowaw=(i > 0))
                    P.copy(dst, bview(b)[:, :NSET, :], eng="act")
                if RWS < 3:
                    continue
                P.tt(CT["Q"][:], CT["N0"][:], MK["I"][:], ALU.add, eng="pool")
                cur = 0
                for lvl in range(1, 6):
                    nxt = 1 - cur
                    b = bank()
                    for i in range(NSET):
                        P.mm(b[0:64, i * 64:(i + 1) * 64], CT[f"N{cur}"][:, i, :], CT[f"L{cur}"][:, i, :], nowaw=(i > 0))
                    P.copy(CT[f"L{nxt}"][:], bview(b)[:, :NSET, :], eng="act")
                    if lvl < 5:
                        b = bank()
                        for i in range(NSET):
                            P.mm(b[0:64, i * 64:(i + 1) * 64], CT[f"L{cur}"][:, i, :], CT[f"N{cur}"][:, i, :], nowaw=(i > 0))
                        P.copy(CT[f"N{nxt}"][:], bview(b)[:, :NSET, :], eng="act")
                    b = bank()
                    for i in range(NSET):
                        P.mm(b[0:64, i * 64:(i + 1) * 64], CT[f"L{nxt}"][:, i, :], CT["Q"][:, i, :], nowaw=(i > 0))
                    P.tt(CT["Q"][:], CT["Q"][:], bview(b)[:, :NSET, :], ALU.add)
                    cur = nxt
                if RWS < 4:
                    continue
                b = bank()
                for i in range(NSET):
                    P.mm(b[0:64, i * 64:(i + 1) * 64], CT["LakT"][:, i, :], pack[:, i, 4, :], nowaw=(i > 0))
                P.copy(CT["Wsb"][:], bview(b)[:, :NSET, :], eng="act")
                b = bank()
                for i in range(NSET):
                    P.mm(b[0:64, i * 64:(i + 1) * 64], CT["Q"][:, i, :], CT["Wsb"][:, i, :], nowaw=(i > 0))
                P.copy(pack[:, :, 1, :], bview(b)[:, :NSET, :], eng="act")
                b = bank()
                for i in range(NSET):
                    P.mm(b[0:64, i * 64:(i + 1) * 64], CT["Atm"][:, i, :], CT["Q"][:, i, :], nowaw=(i > 0))
                P.copy(pack[:, :, 0, :], bview(b)[:, :NSET, :])
                if RWS < 5:
                    continue
                b = bank()
                for c in range(NCT):
                    P.mm(b[0:64, c * 64:(c + 1) * 64], W["rkk"][:, c * 64:(c + 1) * 64], K.ones[:, 0:64])
                for c in range(NCT):
                    P.tt(pack2[:, c, 0, :], b[0:64, c * 64:(c + 1) * 64], pack[:, 2 * c, 4, :], ALU.mult)
                b = bank()
                for c in range(NCT):
                    P.mm(b[0:64, c * 64:(c + 1) * 64], W["sgl"][:, c * 64:(c + 1) * 64], gup[:])
                P.copy(pack2[:, :, 1, :], bview(b)[:, :NCT, :], eng="act")
                if RWS < 6:
                    continue
                for i, (c, u) in enumerate(CU):
                    n = ch0 + c
                    P.dma(scr[n, u, :, 0:7, :], pack[:, i, 0:7, :], eng="pool")
                    P.dma(scr[n, u, :, 7, :], fm("Rt", c, u), eng="pool")
                    P.dma(scrpc[n, u, :, :], e3t[u * 64:(u + 1) * 64, c:c + 1], eng="pool", allow_slow_non_contiguous=True)
                for c in range(NCT):
                    P.dma(scr2[ch0 + c, :, :, :], pack2[:, c, :, :], eng="pool")

        import os
        RWP = int(os.environ.get("RW_PHASE", "3"))
        if RWP < 2:
            return
        with P.scope():
            pUb = [P.ps(f"pU{u}", [128, 512])[0:64, 0:64] for u in range(2)]
            pYb = [P.ps(f"pY{u}", [128, 512])[0:64, 0:64] for u in range(2)]
            pSb = [P.ps(f"pS{u}", [128, 512])[0:64, 0:64] for u in range(2)]
            St = [P.sb(f"St{u}", [64, 64]) for u in range(2)]
            for u in range(2):
                P.memset(St[u][:], 0.0)
            NB = 4
            ops = [[P.sb(f"ops{u}_{i}", [64, 8, 64]) for i in range(NB)] for u in range(2)]
            pcs = [[P.sb(f"pc{u}_{i}", [64, 1]) for i in range(NB)] for u in range(2)]
            usb = [[P.sb(f"usb{u}_{i}", [64, 64]) for i in range(2)] for u in range(2)]
            ysb = [[P.sb(f"ysb{u}_{i}", [64, 64]) for i in range(NB)] for u in range(2)]
            order = [list(range(NCH)),
                     list(range(NCC - 1, -1, -1)) + list(range(NCH - 1, NCC - 1, -1))]
            for i in range(NCH):
                for u in range(2):
                    n = order[u][i]
                    o = ops[u][i % NB]; pc = pcs[u][i % NB]
                    P.dma(o[:], scr[n, u, :, :, :])
                    P.dma(pc[:], scrpc[n, u, :, :], allow_slow_non_contiguous=True)
                    U = usb[u][i % 2]
                    pU = pUb[u]
                    P.mm(pU[:], o[:, 0, :], St[u][:])
                    P.tt(U[:], pU[:], o[:, 1, :], ALU.add)
                    pY = pYb[u]
                    P.mm(pY[:], o[:, 7, :], St[u][:], start=True, stop=False)
                    P.mm(pY[:], o[:, 5, :], U[:], start=False, stop=False)
                    P.mm(pY[:], o[:, 6, :], o[:, 4, :], start=False, stop=True)
                    y = ysb[u][i % NB]
                    P.copy(y[:], pY[:], eng="act")
                    P.dma(yscr[n, u, :, :], y[:], eng="pool")
                    pS = pSb[u]
                    P.mm(pS[:], o[:, 2, :], U[:], start=True, stop=False)
                    P.mm(pS[:], o[:, 3, :], o[:, 4, :], start=False, stop=True)
                    P.stt(St[u][:], St[u][:], pc[:, 0:1], pS[:], ALU.mult, ALU.add)

        if RWP < 3:
            return
        with P.scope():
            bk = [P.ps(f"b3k{i}", [128, 512]) for i in range(2)]
            yf = [P.sb(f"yf{i}", [64, NCT, 64]) for i in range(2)]
            yb = [P.sb(f"yb{i}", [64, NCT, 64]) for i in range(2)]
            p2 = [P.sb(f"p2_{i}", [64, 2, NCT, 64]) for i in range(2)]
            y = P.sb("y3", [64, NCT, 64]); sq = P.sb("sq3", [64, NCT, 64])
            s1 = P.sb("s1", [64, NCT]); s2 = P.sb("s2", [64, NCT])
            ofm = [P.sb(f"ofm{i}", [64, N]) for i in range(2)]
            for ti, (kind, ch0, info) in enumerate(tiles()):
                a, b, q = yf[ti % 2], yb[ti % 2], p2[ti % 2]
                P.dma(a[:], yscr[ch0:ch0 + NCT, 0, :, :].re("c t v -> t c v"))
                P.dma(b[:], yscr[ch0:ch0 + NCT, 1, :, :].re("c t v -> t c v"))
                for it_ in range(2):
                    P.dma(q[:, it_, :, :], scr2[ch0:ch0 + NCT, :, it_, :].re("c t v -> t c v"))
                P.tt(y[:], a[:], b[:], ALU.add)
                P.reduce(s1[:], y[:])
                P.ts(s1[:], s1[:], -1.0 / 64, ALU.mult)
                for c in range(NCT):
                    P.ts(y[:, c, :], y[:, c, :], s1[:, c:c + 1], ALU.add)
                P.tt(sq[:], y[:], y[:], ALU.mult, eng="pool")
                P.reduce(s2[:], sq[:])
                P.ts(s2[:], s2[:], 1.0 / 64, ALU.mult, GN_EPS, ALU.add)
                P.act(s2[:], s2[:], AF.Sqrt)
                P.recip(s2[:], s2[:])
                for c in range(NCT):
                    P.ts(y[:, c, :], y[:, c, :], s2[:, c:c + 1], ALU.mult)
                P.tt(y[:], y[:], gn[:, 0, :, :], ALU.mult)
                P.tt(y[:], y[:], gn[:, 1, :, :], ALU.add)
                P.tt(y[:], y[:], q[:, 0, :, :], ALU.add)
                P.tt(y[:], y[:], q[:, 1, :, :], ALU.mult)
                pp = bk[ti % 2]
                for c in range(NCT):
                    P.mm(pp[0:64, c * 64:(c + 1) * 64], y[:, c, :], K.ident[0:64, 0:64])
                of = ofm[ti % 2]
                P.copy(of[:, :N], pp[0:64, :N], eng="act")
                if kind == "ctx":
                    P.dma(bufs["rwo"].loc(0, 64, cfg.L, cfg.CTX), of[:, :cfg.CTX], eng="pool")
                else:
                    r, c0 = info
                    P.dma(bufs["rwo"].loc(0, 64, r * cfg.ntl + c0, N), of[:, :N], eng="pool")


def stage_ada(P, cfg, K, dr, wfull, depth):
    mods = [P.sb(f"mod{l}", [128, 2, 9, 8]) for l in range(depth)]
    with P.scope():
        cond = P.sb("cond", [128, 8, 2]); P.dma(cond[:], dr["cond"][:])
        P.act(cond[:], cond[:], AF.Silu)
        bada = P.sb("bada", [128, depth, 72]); P.dma(bada[:], dr["bada"][:])
        wA = [P.sb(f"wada{i}", [128, NK, 256]) for i in range(2)]
        pp = [P.ps(f"pada{i}", [128, 2]) for i in range(4)]
        cnt = 0
        for l in range(depth):
            for blk in range(36):
                half, b2 = divmod(blk, 18)
                wa = wA[blk % 2]
                P.dma(wa[:], wfull[l]["w_ada"][half][:, b2 * 256:(b2 + 1) * 256].re("(k p) f -> p k f", p=128))
                for c in range(2):
                    cc = blk * 2 + c
                    j, k = divmod(cc, 8)
                    ps = pp[cnt % 4]; cnt += 1
                    for dk in range(NK):
                        P.mm(ps[:], wa[:, dk, c * 128:(c + 1) * 128], cond[:, dk, :], start=(dk == 0), stop=(dk == NK - 1))
                    P.ts(mods[l][:, :, j, k], ps[:], bada[:, l, cc:cc + 1], ALU.add)
    return mods


WSHAPES = {
    "w1a": (D, DFF), "w3a": (D, DFF), "w2a": (DFF, D), "w1b": (D, DFF), "w3b": (D, DFF), "w2b": (DFF, D),
    "w_in": (D, INCOLS), "w_out": (D, D), "w_br_s5": (256, D), "w_br_att": (512, D), "w_br_rw": (256, D),
    "w_glu": (256, 256), "w_ada0": (D, 4608), "w_ada1": (D, 4608),
}
SMALL = {
    "normw": [128, 3, 8], "qkw": [128, 2], "b_glu": [128, 2],
    "s5_lam": [128, 3, 4], "s5_bL": [128, 2, 4, 256], "s5_cR": [128, 2, 4, 64], "s5_D": [128, 2, 64],
    "rw_sel": [128, 2, 128], "rw_cols": [128, 10], "rw_lup": [128, 128], "rw_gup": [128, 64],
}
ALL8 = [list(range(8))]
GRP4 = [[0, 1, 2, 3], [4, 5, 6, 7]]


def build(cfg, depth=4, debug=(), stages=("a", "gqa", "s5", "rwkv", "b")):
    nc = bass.Bass("TRN2", target_bir_lowering=False)
    P = Prog(nc)
    NT, TS = cfg.nt, cfg.TSEQ
    NCH = TS // 64
    NCT = (256 if cfg.ntl >= 256 else cfg.ntl) // 64
    dr = {}
    for nm, shp in (("xT", [D, NT]), ("cond", [128, 8, 2]), ("ropec", [64, cfg.ntl]), ("ropes", [64, cfg.ntl]),
                    ("selI", [128, 4, 128]), ("c_ident", [128, 128]), ("c_rotm", [128, 128]), ("c_blk64", [128, 128]),
                    ("c_jt", [128, 4, 9]), ("c_masks", [64, 4, 64]), ("c_reset", [128, 512]),
                    ("bada", [128, depth, 72])):
        dr[nm] = P.dram(nm, shp)
    for nm, shp in SMALL.items():
        dr[nm] = P.dram(nm, [depth] + shp)
    dr["rw_gn"] = P.dram("rw_gn", [depth, 64, 2, NCT, 64])
    out = P.dram("out", [D, NT], kind="ExternalOutput")
    dbg = {}

    def dbg_out(name, buf):
        if name in debug:
            shp = list(buf.t.shape)
            d_ = P.dram("dbg_" + name, shp, kind="ExternalOutput")
            P.dma(d_[:], buf[:])
            dbg[name] = d_

    def internal(name, shape):
        t = nc.dram_tensor(name, list(shape), F32).ap()
        return Buf(name, t, is_dram=True)

    def allgather(out_b, in_b, groups):
        P.allgather(out_b[:], in_b[:], groups)

    wfull = []
    for l in range(depth):
        wl = {}
        import os
        only = os.environ.get("WONLY")
        for nm, (R, C) in WSHAPES.items():
            if only and nm not in only.split(","):
                wl[nm] = None
                continue
            full = P.dram(f"{nm}_f{l}", [R, C])
            wl[nm] = full
        wl["w_ada"] = [wl["w_ada0"], wl["w_ada1"]]
        wfull.append(wl)

    K = Consts(P, dr)
    if "noada" in stages:
        mods = [P.sb(f"mod{l}", [128, 2, 9, 8]) for l in range(depth)]
        for l in range(depth):
            P.memset(mods[l][:], 0.0)
    else:
        mods = stage_ada(P, cfg, K, dr, wfull, depth)
    if "mods" in debug:
        d_ = P.dram("dbg_mods", [128, 2, 9, 8], kind="ExternalOutput")
        P.dma(d_[:], mods[0][:])
    nws = P.sb("nws", [128, depth, 3, 8]); P.dma(nws[:], dr["normw"].re("l p i k -> p l i k"))

    bufs = {"xa": internal("xa", [D, NT])}
    P.dma(bufs["xa"][:], dr["xT"][:])
    ntl, ncx = cfg.ntl, cfg.ncx
    bufs["gu"] = Pieced(nc, "gu", 256, 256, col_bounds(cfg, 1024, ntl, ncx))
    bufs["gk"] = Pieced(nc, "gk", 128, 128, col_bounds(cfg, 2048, ntl, ncx))
    bufs["gf"] = Pieced(nc, "gf", 1024, 256, col_bounds(cfg, 1024, ntl, ncx))
    bufs["gv"] = PiecedT(nc, "gv", 128, col_bounds(cfg, 2048, ntl, ncx))
    seqb = [(r * ntl, (r + 1) * ntl) for r in range(4)] + [(cfg.L, cfg.TSEQ)]
    bufs["s5o"] = Pieced(nc, "s5o", 64, 64, seqb)
    bufs["rwo"] = Pieced(nc, "rwo", 64, 64, seqb)
    for nm, shp in (("qb", [512, NT]), ("sgate", [3072, NT]), ("yatt", [512, NT]),
                    ("rw_scr", [NCH, 2, 64, 8, 64]), ("rw_pc", [NCH, 2, 64, 1]), ("rw_scr2", [NCH, 64, 2, 64]),
                    ("rw_y", [NCH, 2, 64, 64])):
        bufs[nm] = internal(nm, shp)

    for l in range(depth):
        lw = dict(wfull[l])
        for nm in SMALL:
            lw[nm] = View(dr[nm], dr[nm].t[l])
        lw["rw_gn"] = View(dr["rw_gn"], dr["rw_gn"].t[l])
        nw = nws[:, l, :, :]
        if "a" in stages:
            stage_a(P, cfg, K, mods[l], nw, dr, lw, bufs)
        dbg_out(f"xa_a{l}", bufs["xa"]); dbg_out(f"gu{l}", bufs["gu"].loc_t[(0, 0)]); dbg_out(f"gk{l}", bufs["gk"].loc_t[(0, 0)])
        dbg_out(f"gf{l}", bufs["gf"].loc_t[(1, 0)]); dbg_out(f"gv{l}", bufs["gv"].loc_t[0]); dbg_out(f"qb{l}", bufs["qb"])
        dbg_out(f"sgate{l}", bufs["sgate"])
        for nm in ("gu", "gk", "gf", "gv"):
            bufs[nm].gather(P)
        if "gqa" in stages:
            stage_gqa(P, cfg, K, bufs)
        dbg_out(f"yatt{l}", bufs["yatt"])
        if "s5" in stages:
            stage_s5(P, cfg, K, dr, lw, bufs, l)
        dbg_out(f"s5o{l}", bufs["s5o"].loc_t[(0, 0)])
        if "rwkv" in stages:
            stage_rwkv(P, cfg, K, dr, lw, bufs)
        dbg_out(f"rwo{l}", bufs["rwo"].loc_t[(0, 0)])
        bufs["s5o"].gather(P)
        bufs["rwo"].gather(P)
        dst = out if l == depth - 1 else bufs["xa"]
        if "b" in stages:
            stage_b(P, cfg, K, mods[l], nw, dr, lw, bufs, dst)
        else:
            P.dma(out[:], bufs["xa"][:])
    P.emit()
    return nc


S5_STATE = 64


def consts_np():
    ident = np.eye(128, dtype=np.float32)
    rotm = np.zeros((128, 128), np.float32)
    for blk in range(2):
        o = blk * 64
        for i in range(32):
            rotm[o + i + 32, o + i] = -1.0
            rotm[o + i, o + i + 32] = 1.0
    blk64 = np.zeros((128, 128), np.float32)
    blk64[:64, :64] = 1.0
    blk64[64:, 64:] = 1.0
    jt = np.broadcast_to(np.arange(9, dtype=np.float32), (128, 4, 9)).copy()
    r = np.arange(64)[:, None]
    c = np.arange(64)[None, :]
    masks = np.stack([(c < r), (c > r), (c <= r), (c >= r)], axis=1).astype(np.float32)
    reset = np.ones((128, 512), np.float32)
    reset[:, ::64] = 0.0
    return {"c_ident": ident, "c_rotm": rotm, "c_blk64": blk64, "c_jt": jt, "c_masks": masks, "c_reset": reset}


def pcol(v):
    return np.ascontiguousarray(v.reshape(-1, 128).T)


def host_prep(inp, cfg, depth):
    f32 = np.float32
    inp = {k: np.asarray(v, dtype=f32) for k, v in inp.items()}
    C = consts_np()
    ntl, ncx, NT = cfg.ntl, cfg.ncx, cfg.nt
    NCT = (256 if ntl >= 256 else ntl) // 64
    maps = []
    fulls = []
    for l in range(depth):
        fulls.append({
            "w1a": inp["ffn_w1"][l, 0], "w3a": inp["ffn_w3"][l, 0], "w2a": inp["ffn_w2"][l, 0],
            "w1b": inp["ffn_w1"][l, 1], "w3b": inp["ffn_w3"][l, 1], "w2b": inp["ffn_w2"][l, 1],
            "w_in": inp["w_in"][l], "w_out": inp["w_out"][l], "w_br_s5": inp["w_br_s5"][l],
            "w_br_att": inp["w_br_att"][l], "w_br_rw": inp["w_br_rwkv"][l], "w_glu": inp["s5_w_glu"][l],
            "w_ada0": np.ascontiguousarray(inp["w_ada"][l][:, :4608]), "w_ada1": np.ascontiguousarray(inp["w_ada"][l][:, 4608:]),
        })
    half = 32
    inv_freq = (10000.0 ** (-np.arange(0, half, 2, dtype=f32) / half)).astype(f32)
    for core in range(8):
        b, q = divmod(core, 4)
        m = dict(C)
        xs = inp["x"][b, q * ntl:(q + 1) * ntl, :]
        cs = inp["ctx"][b, q * ncx:(q + 1) * ncx, :]
        m["xT"] = np.ascontiguousarray(np.concatenate([xs, cs], axis=0).T)
        m["cond"] = np.ascontiguousarray(np.stack([pcol(inp["c"][b]), pcol(inp["c_ctx"])], axis=-1))
        t = np.arange(q * ntl, (q + 1) * ntl)
        row = (t // 64).astype(f32)
        col = (t % 64).astype(f32)
        ang = np.concatenate([row[:, None] * inv_freq, col[:, None] * inv_freq], axis=-1)
        ang = np.concatenate([ang, ang], axis=-1)
        m["ropec"] = np.ascontiguousarray(np.cos(ang).T.astype(f32))
        m["ropes"] = np.ascontiguousarray(np.sin(ang).T.astype(f32))
        sel = np.zeros((128, 4, 128), f32)
        sel[:, q, :] = np.eye(128, dtype=f32)
        m["selI"] = sel
        m["bada"] = np.ascontiguousarray(np.stack([pcol(inp["b_ada"][l]) for l in range(depth)], axis=1))
        sm = {k: np.zeros([depth] + v, f32) for k, v in SMALL.items()}
        gn = np.zeros((depth, 64, 2, NCT, 64), f32)
        for l in range(depth):
            sm["normw"][l] = np.stack([pcol(inp["norm_w"][l, i]) for i in range(3)], axis=1)
            sm["qkw"][l, :, 0] = np.tile(inp["q_norm_w"][l], 2)
            sm["qkw"][l, :, 1] = np.tile(inp["k_norm_w"][l], 2)
            sm["b_glu"][l] = pcol(inp["s5_b_glu"][l])
            for d in range(2):
                for pr in range(2):
                    c = d * 2 + pr
                    for gi in range(2):
                        g = 4 * q + 2 * pr + gi
                        rows = slice(gi * 64, (gi + 1) * 64)
                        sm["s5_lam"][l, rows, 0, c] = inp["s5_lam_re"][l, d, g]
                        sm["s5_lam"][l, rows, 1, c] = inp["s5_lam_im"][l, d, g]
                        sm["s5_lam"][l, rows, 2, c] = inp["s5_log_dt"][l, d, g]
                        sm["s5_bL"][l, rows, 0, c, g * 16:(g + 1) * 16] = inp["s5_b_re"][l, d, g]
                        sm["s5_bL"][l, rows, 1, c, g * 16:(g + 1) * 16] = inp["s5_b_im"][l, d, g]
                        gl = 2 * pr + gi
                        sm["s5_cR"][l, rows, 0, c, gl * 16:(gl + 1) * 16] = inp["s5_c_re"][l, d, g].T
                        sm["s5_cR"][l, rows, 1, c, gl * 16:(gl + 1) * 16] = inp["s5_c_im"][l, d, g].T
            Dm = np.zeros((256, 64), f32)
            for gl in range(4):
                for h in range(16):
                    f = (4 * q + gl) * 16 + h
                    Dm[f, gl * 16 + h] = inp["s5_d"][l, f]
            sm["s5_D"][l] = Dm.reshape(2, 128, 64).transpose(1, 0, 2)
            hc = slice(q * 64, (q + 1) * 64)
            selr = np.zeros((256, 128), f32)
            for mm_ in range(128):
                selr[q * 64 + (mm_ % 64), mm_] = 1.0
            sm["rw_sel"][l] = selr.reshape(2, 128, 128).transpose(1, 0, 2)
            mu = inp["rwkv_mu"][l]
            cols = np.zeros((128, 10), f32)
            for u in range(2):
                rs = slice(u * 64, (u + 1) * 64)
                cols[rs, 0] = mu[0:256][hc]
                cols[rs, 1] = mu[256:512][hc]
                cols[rs, 2] = mu[512:768][hc]
                cols[rs, 3] = inp["rwkv_w0"][l, u, hc]
                cols[rs, 4] = inp["rwkv_a0"][l, u, hc]
                cols[rs, 5] = inp["rwkv_k_k"][l, hc]
                cols[rs, 6] = inp["rwkv_k_a"][l, hc]
                cols[rs, 7] = inp["rwkv_r_k"][l, q]
                sm["rw_lup"][l, 0:64, rs] = inp["rwkv_w_up"][l, u][:, hc]
                sm["rw_lup"][l, 64:128, rs] = inp["rwkv_a_up"][l, u][:, hc]
            cols[0:64, 8] = mu[768:832]
            cols[64:128, 8] = mu[832:896]
            cols[:, 9] = mu[896:1024]
            sm["rw_cols"][l] = cols
            sm["rw_gup"][l] = inp["rwkv_g_up"][l][:, hc]
            gn[l, :, 0, :, :] = inp["rwkv_gn_w"][l, hc][None, None, :]
            gn[l, :, 1, :, :] = inp["rwkv_gn_b"][l, hc][None, None, :]
            for nm, w in fulls[l].items():
                m[f"{nm}_f{l}"] = np.ascontiguousarray(w)
        m.update(sm)
        m["rw_gn"] = gn
        maps.append(m)
    return maps


def assemble(results, cfg, B=2):
    ntl = cfg.ntl
    out = np.zeros((B, 4 * ntl, D), np.float32)
    for core in range(8):
        b, q = divmod(core, 4)
        out[b, q * ntl:(q + 1) * ntl, :] = results[core]["out"][:, :ntl].T
    return out


def kernel(**inputs):
    cfg = Cfg(4096, 64)
    depth = 4
    nc = build(cfg, depth)
    maps = host_prep(inputs, cfg, depth)
    res = run_spmd(nc, maps)
    return assemble(res, cfg)
```
